# Optimizing a Trainium2 kernel written in Bass

```python
import math
import jax, jax.numpy as jnp
from jax import lax
import numpy as np

D_MODEL = 2048
BATCH = 4
SEQ = 4096
DEPTH = 4

N_MIXERS = 4
MEM_LEN = 256
RMS_EPS = 1e-6
NEG = -1e30
BIG = 1e30

SSD_EXPAND = 2
SSD_D_INNER = SSD_EXPAND * D_MODEL
SSD_HEAD_DIM = 64
SSD_N_HEADS = SSD_D_INNER // SSD_HEAD_DIM
SSD_N_GROUPS = 8
SSD_HPG = SSD_N_HEADS // SSD_N_GROUPS
SSD_D_STATE = 128
SSD_CONV = 4
SSD_CHUNK = 128
SSD_CONV_DIM = SSD_D_INNER + 2 * SSD_N_GROUPS * SSD_D_STATE
SSD_IN_DIM = SSD_D_INNER + SSD_CONV_DIM + SSD_N_HEADS

NSA_HEAD_DIM = 128
NSA_N_HEADS = D_MODEL // NSA_HEAD_DIM
NSA_KV_GROUPS = 4
NSA_HPG = NSA_N_HEADS // NSA_KV_GROUPS
NSA_KV_DIM = NSA_KV_GROUPS * NSA_HEAD_DIM
NSA_CMP_BLOCK = 32
NSA_CMP_STRIDE = 16
NSA_CMP_HIDDEN = 256
NSA_SLC_BLOCK = 64
NSA_TOPK = 16
NSA_N_LOCAL = 2
NSA_WINDOW = 512
NSA_Q_BLOCK = 64
NSA_WIN_Q_BLOCK = 128
NSA_IN_DIM = NSA_N_HEADS * NSA_HEAD_DIM + 6 * NSA_KV_DIM + 3 * NSA_N_HEADS

SGU_CHUNK = 128
SGU_WIDTH = 2 * D_MODEL
SGU_GROUPS = 8
SGU_GROUP_DIM = SGU_WIDTH // SGU_GROUPS

POOL_WINDOWS = (2, 4, 8, 16)
POOL_GROUPS = len(POOL_WINDOWS)
POOL_GROUP_DIM = D_MODEL // POOL_GROUPS

XA_HEADS = 4
XA_HEAD_DIM = 128
XA_DIM = XA_HEADS * XA_HEAD_DIM

FFN_HIDDEN = 5632
FFN_CONV = 3

kernel_name = "hybrid_interleaved_ssd_nsa_sgu_pool_decoder"


def _rmsnorm(x, g):
    xf = x.astype(jnp.float32)
    y = xf * lax.rsqrt(jnp.mean(xf * xf, axis=-1, keepdims=True) + RMS_EPS)
    return (y * g.astype(jnp.float32)).astype(x.dtype)


def _layernorm(x, g):
    xf = x.astype(jnp.float32)
    mu = jnp.mean(xf, axis=-1, keepdims=True)
    xc = xf - mu
    y = xc * lax.rsqrt(jnp.mean(xc * xc, axis=-1, keepdims=True) + RMS_EPS)
    return (y * g.astype(jnp.float32)).astype(x.dtype)


def _causal_dwconv(x, w, b):
    k_w, L = w.shape[0], x.shape[1]
    xp = jnp.pad(x, ((0, 0), (k_w - 1, 0), (0, 0)))
    y = b + xp[:, 0:L] * w[0]
    for k in range(1, k_w):
        y = y + xp[:, k:k + L] * w[k]
    return y


def _ssd_mixer(h, w_in, conv_w, conv_b, dt_bias, a_log, d_skip, norm_g, w_out):
    f32 = jnp.float32
    bsz, L, _ = h.shape
    G, J, P, N, Q = SSD_N_GROUPS, SSD_HPG, SSD_HEAD_DIM, SSD_D_STATE, SSD_CHUNK
    nc = L // Q
    proj = h @ w_in
    z, xbc, dt = jnp.split(proj, [SSD_D_INNER, SSD_D_INNER + SSD_CONV_DIM], axis=-1)
    xbc = jax.nn.silu(_causal_dwconv(xbc, conv_w, conv_b)).astype(f32)
    xs, bm, cm = jnp.split(xbc, [SSD_D_INNER, SSD_D_INNER + G * N], axis=-1)
    xs = xs.reshape(bsz, nc, Q, G, J, P)
    bm = bm.reshape(bsz, nc, Q, G, N)
    cm = cm.reshape(bsz, nc, Q, G, N)
    dt = jax.nn.softplus(dt.astype(f32) + dt_bias.astype(f32)).reshape(bsz, nc, Q, G, J)
    a = dt * (-jnp.exp(a_log.astype(f32))).reshape(G, J)
    a_cs = jnp.cumsum(a.transpose(0, 3, 4, 1, 2), axis=-1)
    xdt = xs * dt[..., None]
    causal = jnp.tril(jnp.ones((Q, Q), dtype=bool))
    decay = jnp.exp(jnp.where(causal, a_cs[..., :, None] - a_cs[..., None, :], -jnp.inf))
    cb = jnp.einsum('bclgn,bcsgn->bgcls', cm, bm)
    y_diag = jnp.einsum('bgjcls,bcsgjp->bclgjp', cb[:, :, None] * decay, xdt)
    decay_to_end = jnp.exp(a_cs[..., -1:] - a_cs)
    states = jnp.einsum('bcsgn,bgjcs,bcsgjp->bcgjpn', bm, decay_to_end, xdt)
    chunk_decay = jnp.exp(a_cs[..., -1])

    def step(carry, inp):
        st, dec = inp
        return carry * dec[..., None, None] + st, carry

    init = jnp.zeros((bsz, G, J, P, N), f32)
    _, prev = lax.scan(step, init, (jnp.moveaxis(states, 1, 0), jnp.moveaxis(chunk_decay, 3, 0)))
    y_off = jnp.einsum('bclgn,cbgjpn,bgjcl->bclgjp', cm, prev, jnp.exp(a_cs))
    y = y_diag + y_off + xs * d_skip.astype(f32).reshape(G, J)[:, :, None]
    y = y.reshape(bsz, L, SSD_D_INNER)
    y = _rmsnorm(y * jax.nn.silu(z.astype(f32)), norm_g)
    return y.astype(h.dtype) @ w_out


def _nsa_mixer(h, w_in, cmp_pos, cmp_w1, cmp_w2, w_out):
    f32 = jnp.float32
    bsz, L, _ = h.shape
    G, J, Dh = NSA_KV_GROUPS, NSA_HPG, NSA_HEAD_DIM
    scale = Dh ** -0.5
    proj = h @ w_in
    offs = np.cumsum([NSA_N_HEADS * Dh] + [NSA_KV_DIM] * 6).tolist()
    q, kc_in, vc_in, ks, vs, kw, vw, gate_logits = jnp.split(proj, offs, axis=-1)
    q = q.reshape(bsz, L, G, J, Dh)
    kc_in, vc_in, ks, vs, kw, vw = [t.reshape(bsz, L, G, Dh) for t in (kc_in, vc_in, ks, vs, kw, vw)]
    gates = jax.nn.sigmoid(gate_logits.astype(f32)).reshape(bsz, L, G, J, 3).astype(h.dtype)
    t_pos = jnp.arange(L)

    ncmp = (L - NSA_CMP_BLOCK) // NSA_CMP_STRIDE + 1
    cmp_idx = np.arange(ncmp)[:, None] * NSA_CMP_STRIDE + np.arange(NSA_CMP_BLOCK)[None, :]

    def compress(t, pos, w1, w2):
        blk = t[:, cmp_idx] + pos[None, None, :, None, :]
        flat = blk.transpose(0, 1, 3, 2, 4).reshape(bsz, ncmp, G, NSA_CMP_BLOCK * Dh)
        return jax.nn.gelu(flat @ w1) @ w2

    kc = compress(kc_in, cmp_pos[0], cmp_w1[0], cmp_w2[0])
    vc = compress(vc_in, cmp_pos[1], cmp_w1[1], cmp_w2[1])
    s_c = jnp.einsum('btgjd,bigd->bgjti', q, kc).astype(f32) * scale
    cmp_end = jnp.arange(ncmp) * NSA_CMP_STRIDE + NSA_CMP_BLOCK - 1
    valid_c = cmp_end[None, :] <= t_pos[:, None]
    p_c = jnp.where(valid_c, jax.nn.softmax(jnp.where(valid_c, s_c, NEG), axis=-1), 0.0)
    o_cmp = jnp.einsum('bgjti,bigd->btgjd', p_c.astype(vc.dtype), vc)

    nslc = L // NSA_SLC_BLOCK
    ci = np.arange(ncmp)[:, None] * NSA_CMP_STRIDE
    sj = np.arange(nslc)[None, :] * NSA_SLC_BLOCK
    cover = jnp.asarray(((ci <= sj + NSA_SLC_BLOCK - 1) & (ci + NSA_CMP_BLOCK - 1 >= sj)).astype(np.float32))
    imp = jnp.einsum('bgjti,ik->bgtk', p_c, cover)
    blk = jnp.arange(nslc)[None, :]
    cur = (t_pos // NSA_SLC_BLOCK)[:, None]
    forced = (blk == 0) | ((blk <= cur) & (blk > cur - 1 - NSA_N_LOCAL))
    future = blk * NSA_SLC_BLOCK > t_pos[:, None]
    sel_score = jnp.where(forced, BIG, jnp.where(future, NEG, imp))
    topk = min(NSA_TOPK, nslc)
    _, sel_idx = lax.top_k(sel_score, topk)

    ks_blk = ks.reshape(bsz, nslc, NSA_SLC_BLOCK, G, Dh).transpose(0, 3, 1, 2, 4)
    vs_blk = vs.reshape(bsz, nslc, NSA_SLC_BLOCK, G, Dh).transpose(0, 3, 1, 2, 4)
    nqb = L // NSA_Q_BLOCK
    q_blocks = q.reshape(bsz, nqb, NSA_Q_BLOCK, G, J, Dh).transpose(1, 0, 2, 3, 4, 5)
    idx_blocks = sel_idx.reshape(bsz, G, nqb, NSA_Q_BLOCK, topk).transpose(2, 0, 1, 3, 4)
    starts = jnp.arange(nqb, dtype=jnp.int32) * NSA_Q_BLOCK
    b_i = jnp.arange(bsz)[:, None, None, None]
    g_i = jnp.arange(G)[None, :, None, None]
    in_blk = jnp.arange(NSA_SLC_BLOCK)

    def sel_block(args):
        qb, ib, t0 = args
        kg = ks_blk[b_i, g_i, ib]
        vg = vs_blk[b_i, g_i, ib]
        s = jnp.einsum('bqgjd,bgqksd->bgjqks', qb, kg).astype(f32) * scale
        pos = ib[..., None] * NSA_SLC_BLOCK + in_blk
        tq = t0 + jnp.arange(NSA_Q_BLOCK)
        valid = (pos <= tq[None, None, :, None, None])[:, :, None]
        p = jax.nn.softmax(jnp.where(valid, s, NEG), axis=(-2, -1))
        return jnp.einsum('bgjqks,bgqksd->bqgjd', p.astype(vg.dtype), vg)

    o_slc = lax.map(sel_block, (q_blocks, idx_blocks, starts))
    o_slc = o_slc.transpose(1, 0, 2, 3, 4, 5).reshape(bsz, L, G, J, Dh)

    WB = NSA_WIN_Q_BLOCK
    nb = L // WB
    nslab = NSA_WINDOW // WB + 1
    slab_len = nslab * WB
    slab_idx = np.arange(nb)[:, None] + np.arange(nslab)[None, :]
    kp = jnp.pad(kw, ((0, 0), (NSA_WINDOW, 0), (0, 0), (0, 0))).reshape(bsz, nb + nslab - 1, WB, G, Dh)
    vp = jnp.pad(vw, ((0, 0), (NSA_WINDOW, 0), (0, 0), (0, 0))).reshape(bsz, nb + nslab - 1, WB, G, Dh)
    k_slab = kp[:, slab_idx].reshape(bsz, nb, slab_len, G, Dh)
    v_slab = vp[:, slab_idx].reshape(bsz, nb, slab_len, G, Dh)
    qw = q.reshape(bsz, nb, WB, G, J, Dh)
    s_w = jnp.einsum('bnqgjd,bnkgd->bgjnqk', qw, k_slab).astype(f32) * scale
    spos = jnp.arange(nb)[:, None] * WB - NSA_WINDOW + jnp.arange(slab_len)[None, :]
    tpos = jnp.arange(nb)[:, None] * WB + jnp.arange(WB)[None, :]
    diff = tpos[:, :, None] - spos[:, None, :]
    valid_w = (spos[:, None, :] >= 0) & (diff >= 0) & (diff < NSA_WINDOW)
    p_w = jax.nn.softmax(jnp.where(valid_w, s_w, NEG), axis=-1)
    o_win = jnp.einsum('bgjnqk,bnkgd->bnqgjd', p_w.astype(v_slab.dtype), v_slab).reshape(bsz, L, G, J, Dh)

    o = gates[..., 0:1] * o_cmp + gates[..., 1:2] * o_slc + gates[..., 2:3] * o_win
    return o.reshape(bsz, L, NSA_N_HEADS * Dh) @ w_out


def _sgu_mixer(h, w_in, b_in, ln_g, w_spatial, b_spatial, w_out):
    bsz, L, _ = h.shape
    nc = L // SGU_CHUNK
    proj = jax.nn.gelu(h @ w_in + b_in)
    u, v = jnp.split(proj, 2, axis=-1)
    v = _layernorm(v, ln_g).reshape(bsz, nc, SGU_CHUNK, SGU_GROUPS, SGU_GROUP_DIM)
    tri = jnp.tril(jnp.ones((SGU_CHUNK, SGU_CHUNK), dtype=bool))
    w_m = jnp.where(tri[None], w_spatial, 0.0).astype(v.dtype)
    sv = jnp.einsum('gts,bcsgd->bctgd', w_m, v) + b_spatial.T[:, :, None]
    return (u * sv.reshape(bsz, L, SGU_WIDTH)) @ w_out


def _pool_mixer(h, w_in, w_group, scale, w_out):
    bsz, L, _ = h.shape
    z = (h @ w_in).reshape(bsz, L, POOL_GROUPS, POOL_GROUP_DIM).astype(jnp.float32)
    cs0 = jnp.concatenate([jnp.zeros_like(z[:, :1]), jnp.cumsum(z, axis=1)], axis=1)
    t_pos = jnp.arange(L)
    outs = []
    for gi, win in enumerate(POOL_WINDOWS):
        c = cs0[:, :, gi]
        lower = jnp.concatenate([jnp.zeros_like(c[:, :win - 1]), c[:, :L + 1 - win]], axis=1)
        count = jnp.minimum(t_pos + 1, win).astype(jnp.float32)[None, :, None]
        outs.append((c[:, 1:] - lower) / count - z[:, :, gi])
    pooled = jnp.stack(outs, axis=2).astype(h.dtype)
    y = jnp.einsum('blgd,gde->blge', pooled, w_group) * scale
    return y.reshape(bsz, L, D_MODEL) @ w_out


def _memory_cross_attention(h, mem_n, w_q, w_kv, w_o):
    bsz, L, _ = h.shape
    q = (h @ w_q).reshape(bsz, L, XA_HEADS, XA_HEAD_DIM)
    k, v = jnp.split(mem_n @ w_kv, 2, axis=-1)
    k = k.reshape(bsz, -1, XA_HEADS, XA_HEAD_DIM)
    v = v.reshape(bsz, -1, XA_HEADS, XA_HEAD_DIM)
    s = jnp.einsum('bthd,bmhd->bhtm', q, k).astype(jnp.float32) * (XA_HEAD_DIM ** -0.5)
    p = jax.nn.softmax(s, axis=-1).astype(v.dtype)
    o = jnp.einsum('bhtm,bmhd->bthd', p, v).reshape(bsz, L, XA_DIM)
    return o @ w_o


def _conv_ffn(h, w_up, conv_w, conv_b, w_down):
    u = _causal_dwconv(h @ w_up, conv_w, conv_b)
    gate, val = jnp.split(u, 2, axis=-1)
    return (jax.nn.silu(gate) * val) @ w_down


def setup_inputs(seed: int = 0) -> dict:
    key = jax.random.key(seed)
    keys = iter(jax.random.split(key, 64))
    f32 = jnp.float32

    def nrm(shape, fan_in):
        return jax.random.normal(next(keys), shape, f32) * (fan_in ** -0.5)

    def gain(shape):
        return 1.0 + 0.05 * jax.random.normal(next(keys), shape, f32)

    def small(shape, s=0.02):
        return s * jax.random.normal(next(keys), shape, f32)

    n_a, n_b, n_c, n_d = [len(range(m, DEPTH, N_MIXERS)) for m in range(N_MIXERS)]
    x = jax.random.normal(next(keys), (BATCH, SEQ, D_MODEL), f32)
    mem = jax.random.normal(next(keys), (BATCH, MEM_LEN, D_MODEL), f32)
    norm_pre = gain((DEPTH, 3, D_MODEL))
    norm_post = gain((DEPTH, 3, D_MODEL))
    norm_mem = gain((DEPTH, D_MODEL))
    ssd_w_in = nrm((n_a, D_MODEL, SSD_IN_DIM), D_MODEL)
    ssd_conv_w = nrm((n_a, SSD_CONV, SSD_CONV_DIM), SSD_CONV)
    ssd_conv_b = small((n_a, SSD_CONV_DIM))
    dt0 = jnp.exp(jax.random.uniform(next(keys), (n_a, SSD_N_HEADS), f32, math.log(1e-3), math.log(1e-1)))
    ssd_dt_bias = dt0 + jnp.log(-jnp.expm1(-dt0))
    ssd_a_log = jnp.log(jax.random.uniform(next(keys), (n_a, SSD_N_HEADS), f32, 1.0, 16.0))
    ssd_d = gain((n_a, SSD_N_HEADS))
    ssd_norm_g = gain((n_a, SSD_D_INNER))
    ssd_w_out = nrm((n_a, SSD_D_INNER, D_MODEL), SSD_D_INNER)
    nsa_w_in = nrm((n_b, D_MODEL, NSA_IN_DIM), D_MODEL)
    nsa_cmp_pos = small((n_b, 2, NSA_CMP_BLOCK, NSA_HEAD_DIM), 0.1)
    nsa_cmp_w1 = nrm((n_b, 2, NSA_CMP_BLOCK * NSA_HEAD_DIM, NSA_CMP_HIDDEN), NSA_CMP_BLOCK * NSA_HEAD_DIM)
    nsa_cmp_w2 = nrm((n_b, 2, NSA_CMP_HIDDEN, NSA_HEAD_DIM), NSA_CMP_HIDDEN)
    nsa_w_out = nrm((n_b, NSA_N_HEADS * NSA_HEAD_DIM, D_MODEL), NSA_N_HEADS * NSA_HEAD_DIM)
    sgu_w_in = nrm((n_c, D_MODEL, 2 * SGU_WIDTH), D_MODEL)
    sgu_b_in = small((n_c, 2 * SGU_WIDTH))
    sgu_ln_g = gain((n_c, SGU_WIDTH))
    sgu_w_spatial = nrm((n_c, SGU_GROUPS, SGU_CHUNK, SGU_CHUNK), SGU_CHUNK)
    sgu_b_spatial = 1.0 + small((n_c, SGU_GROUPS, SGU_CHUNK), 0.1)
    sgu_w_out = nrm((n_c, SGU_WIDTH, D_MODEL), SGU_WIDTH)
    pool_w_in = nrm((n_d, D_MODEL, D_MODEL), D_MODEL)
    pool_w_group = nrm((n_d, POOL_GROUPS, POOL_GROUP_DIM, POOL_GROUP_DIM), POOL_GROUP_DIM)
    pool_scale = 1.0 + small((n_d, POOL_GROUPS, POOL_GROUP_DIM), 0.1)
    pool_w_out = nrm((n_d, D_MODEL, D_MODEL), D_MODEL)
    xa_w_q = nrm((DEPTH, D_MODEL, XA_DIM), D_MODEL)
    xa_w_kv = nrm((DEPTH, D_MODEL, 2 * XA_DIM), D_MODEL)
    xa_w_o = nrm((DEPTH, XA_DIM, D_MODEL), XA_DIM)
    ffn_w_up = nrm((DEPTH, D_MODEL, 2 * FFN_HIDDEN), D_MODEL)
    ffn_conv_w = nrm((DEPTH, FFN_CONV, 2 * FFN_HIDDEN), FFN_CONV)
    ffn_conv_b = small((DEPTH, 2 * FFN_HIDDEN))
    ffn_w_down = nrm((DEPTH, FFN_HIDDEN, D_MODEL), FFN_HIDDEN)
    return {
        "x": x, "mem": mem,
        "norm_pre": norm_pre, "norm_post": norm_post, "norm_mem": norm_mem,
        "ssd_w_in": ssd_w_in, "ssd_conv_w": ssd_conv_w, "ssd_conv_b": ssd_conv_b,
        "ssd_dt_bias": ssd_dt_bias, "ssd_a_log": ssd_a_log, "ssd_d": ssd_d,
        "ssd_norm_g": ssd_norm_g, "ssd_w_out": ssd_w_out,
        "nsa_w_in": nsa_w_in, "nsa_cmp_pos": nsa_cmp_pos, "nsa_cmp_w1": nsa_cmp_w1,
        "nsa_cmp_w2": nsa_cmp_w2, "nsa_w_out": nsa_w_out,
        "sgu_w_in": sgu_w_in, "sgu_b_in": sgu_b_in, "sgu_ln_g": sgu_ln_g,
        "sgu_w_spatial": sgu_w_spatial, "sgu_b_spatial": sgu_b_spatial, "sgu_w_out": sgu_w_out,
        "pool_w_in": pool_w_in, "pool_w_group": pool_w_group, "pool_scale": pool_scale,
        "pool_w_out": pool_w_out,
        "xa_w_q": xa_w_q, "xa_w_kv": xa_w_kv, "xa_w_o": xa_w_o,
        "ffn_w_up": ffn_w_up, "ffn_conv_w": ffn_conv_w, "ffn_conv_b": ffn_conv_b,
        "ffn_w_down": ffn_w_down,
    }


def reference(x, mem, norm_pre, norm_post, norm_mem,
              ssd_w_in, ssd_conv_w, ssd_conv_b, ssd_dt_bias, ssd_a_log, ssd_d, ssd_norm_g, ssd_w_out,
              nsa_w_in, nsa_cmp_pos, nsa_cmp_w1, nsa_cmp_w2, nsa_w_out,
              sgu_w_in, sgu_b_in, sgu_ln_g, sgu_w_spatial, sgu_b_spatial, sgu_w_out,
              pool_w_in, pool_w_group, pool_scale, pool_w_out,
              xa_w_q, xa_w_kv, xa_w_o,
              ffn_w_up, ffn_conv_w, ffn_conv_b, ffn_w_down):
    for i in range(DEPTH):
        kind, j = i % N_MIXERS, i // N_MIXERS
        h = _rmsnorm(x, norm_pre[i, 0])
        if kind == 0:
            m = _ssd_mixer(h, ssd_w_in[j], ssd_conv_w[j], ssd_conv_b[j], ssd_dt_bias[j],
                           ssd_a_log[j], ssd_d[j], ssd_norm_g[j], ssd_w_out[j])
        elif kind == 1:
            m = _nsa_mixer(h, nsa_w_in[j], nsa_cmp_pos[j], nsa_cmp_w1[j], nsa_cmp_w2[j], nsa_w_out[j])
        elif kind == 2:
            m = _sgu_mixer(h, sgu_w_in[j], sgu_b_in[j], sgu_ln_g[j], sgu_w_spatial[j],
                           sgu_b_spatial[j], sgu_w_out[j])
        else:
            m = _pool_mixer(h, pool_w_in[j], pool_w_group[j], pool_scale[j], pool_w_out[j])
        x = x + _rmsnorm(m, norm_post[i, 0])
        h = _rmsnorm(x, norm_pre[i, 1])
        a = _memory_cross_attention(h, _rmsnorm(mem, norm_mem[i]), xa_w_q[i], xa_w_kv[i], xa_w_o[i])
        x = x + _rmsnorm(a, norm_post[i, 1])
        h = _rmsnorm(x, norm_pre[i, 2])
        f = _conv_ffn(h, ffn_w_up[i], ffn_conv_w[i], ffn_conv_b[i], ffn_w_down[i])
        x = x + _rmsnorm(f, norm_post[i, 2])
    return x
```

```python
from contextlib import ExitStack
import numpy as np
import concourse.bass as bass
import concourse.mybir as mybir
from concourse.bass_utils import run_bass_kernel_spmd

F32 = mybir.dt.float32
BF16 = mybir.dt.bfloat16
AF = mybir.ActivationFunctionType
ALU = mybir.AluOpType

D = 2048
NCH = 16
DEPTH = 4
FFN_H = 5632
FFN_HC = 44
EPS = 1e-6
ENGS = ("tensor", "vector", "scalar", "gpsimd", "sync")
NDMA = 8


_FILL = {}


def fillreg(e, v):
    k = (id(e), float(v))
    if k not in _FILL:
        _FILL[k] = e.to_reg(float(v))
    return _FILL[k]


class Buf:
    __slots__ = ("name", "recs")

    def __init__(self, name):
        self.name = name
        self.recs = []


def _norm(x):
    if isinstance(x, Tile):
        return (x.buf, None)
    return x if isinstance(x, tuple) else (x, None)


class Tile:
    __slots__ = ("ap", "buf")

    def __init__(self, ap, buf):
        self.ap = ap
        self.buf = buf

    def __getitem__(self, k):
        return self.ap[k]

    def s(self, sub):
        return (self.buf, sub)


class Prog:
    def __init__(self, nc):
        self.nc = nc
        self.code = {e: [] for e in ENGS}
        self.cnt = {e: 0 for e in ENGS}
        self.known = {e: {} for e in ENGS}
        self.dma_cnt = {}
        self.dma_rr = {e: 0 for e in ENGS}
        self.semkeys = list(ENGS)
        for e in ("sync", "gpsimd", "scalar"):
            for j in range(NDMA):
                self.semkeys.append("d_%s_%d" % (e, j))
        self.arena_off = 0
        self.arena = None
        self.psum = None

    def _waits(self, eng, reads, writes):
        toks = {}
        for (buf, sub), is_w in [(r, False) for r in reads] + [(w, True) for w in writes]:
            for rsub, rw, sk, val in buf.recs:
                if (rsub is None or sub is None or rsub == sub) and (is_w or rw):
                    if toks.get(sk, 0) < val:
                        toks[sk] = val
        kn = self.known[eng]
        for sk, val in toks.items():
            if sk == "tensor" and eng == "tensor":
                continue
            if kn.get(sk, 0) >= val:
                continue
            kn[sk] = val
            self.code[eng].append(("w", sk, val))

    def _record(self, tok, reads, writes):
        sk, val = tok
        for buf, sub in writes:
            if sub is None:
                buf.recs = [[None, True, sk, val]]
            else:
                buf.recs = [r for r in buf.recs if r[0] != sub]
                buf.recs.append([sub, True, sk, val])
        for buf, sub in reads:
            for r in buf.recs:
                if r[0] == sub and (not r[1]) and r[2] == sk:
                    r[3] = val
                    break
            else:
                buf.recs.append([sub, False, sk, val])

    def op(self, eng, fn, reads=(), writes=()):
        reads = [_norm(r) for r in reads]
        writes = [_norm(w) for w in writes]
        self._waits(eng, reads, writes)
        self.cnt[eng] += 1
        self.code[eng].append(("o", fn, eng, 1))
        self._record((eng, self.cnt[eng]), reads, writes)

    def dma(self, eng, out, in_, reads=(), writes=(), **kw):
        reads = [_norm(r) for r in reads]
        writes = [_norm(w) for w in writes]
        self._waits(eng, reads, writes)
        j = self.dma_rr[eng]
        self.dma_rr[eng] = (j + 1) % NDMA
        sk = "d_%s_%d" % (eng, j)
        prev = self.dma_cnt.get(sk, 0)
        if prev and self.known[eng].get(sk, 0) < prev:
            self.known[eng][sk] = prev
            self.code[eng].append(("w", sk, prev))
        self.dma_cnt[sk] = prev + 16
        self.code[eng].append(("o", lambda e: e.dma_start(out=out, in_=in_, **kw), sk, 16))
        self._record((sk, prev + 16), reads, writes)

    def barrier(self):
        for e in ENGS:
            kn = self.known[e]
            for o in ENGS:
                if o != e and self.cnt[o] > kn.get(o, 0):
                    kn[o] = self.cnt[o]
                    self.code[e].append(("w", o, self.cnt[o]))
            for sk, v in self.dma_cnt.items():
                if v > kn.get(sk, 0):
                    kn[sk] = v
                    self.code[e].append(("w", sk, v))
        self.arena_off = 0

    def alloc(self, name, shape_free, dtype):
        n = 1
        for s in shape_free:
            n *= s
        nbytes = n * (2 if dtype == BF16 else 4)
        nw = (nbytes + 3) // 4
        nw = (nw + 7) // 8 * 8
        off = self.arena_off
        assert off + nw <= self.arena_words, ("arena overflow", name, off, nw)
        self.arena_off = off + nw
        ap = self.arena[:, off:off + nw]
        if dtype == BF16:
            ap = ap.bitcast(BF16)[:, 0:n]
        else:
            ap = ap[:, 0:n]
        if len(shape_free) == 2:
            ap = ap.rearrange("p (a b) -> p a b", a=shape_free[0])
        elif len(shape_free) == 3:
            ap = ap.rearrange("p (a b c) -> p a b c", a=shape_free[0], b=shape_free[1])
        return Tile(ap, Buf(name))

    def emit(self, st):
        nc = self.nc
        sems = {k: st.enter_context(nc.semaphore(k)) for k in self.semkeys}
        self.barrier()
        block = st.enter_context(nc.Block())
        for eng in ENGS:
            code = self.code[eng]

            def body(e, code=code):
                for it in code:
                    if it[0] == "w":
                        e.wait_ge(sems[it[1]], it[2])
                    else:
                        it[1](e).then_inc(sems[it[2]], it[3])

            getattr(block, eng)(body)


def colT(v):
    v = np.asarray(v, np.float32).reshape(-1)
    assert v.size % 128 == 0
    return np.ascontiguousarray(v.reshape(-1, 128).T)


def pack_cols(inp):
    offs = {}
    parts = []
    pos = [0]

    def add(name, n, arr_fn):
        offs[name] = pos[0]
        pos[0] += n
        if inp is not None:
            a = arr_fn()
            assert a.shape == (128, n), (name, a.shape, n)
            parts.append(a)

    for i in range(DEPTH):
        for j in range(3):
            add("pre%d_%d" % (i, j), 16, lambda: colT(inp["norm_pre"][i, j]))
            add("post%d_%d" % (i, j), 16, lambda: colT(inp["norm_post"][i, j]))
        add("nmem%d" % i, 16, lambda: colT(inp["norm_mem"][i]))
        for k in range(3):
            add("fcw%d_%d" % (i, k), 88, lambda: colT(inp["ffn_conv_w"][i, k]))
        add("fcb%d" % i, 88, lambda: colT(inp["ffn_conv_b"][i]))
    add("pscale", 16, lambda: colT(inp["pool_scale"][0]))
    add("sgu_binu", 32, lambda: colT(inp["sgu_b_in"][0][:4096]))
    add("ssd_cw", 4 * 48, lambda: np.concatenate([colT(inp["ssd_conv_w"][0, k]) for k in range(4)], axis=1))
    add("ssd_cb", 48, lambda: colT(inp["ssd_conv_b"][0]))
    add("ssd_ng", 32, lambda: colT(inp["ssd_norm_g"][0]))
    add("ssd_dcol", 32, lambda: colT(np.repeat(inp["ssd_d"][0], 64)))
    arr = np.ascontiguousarray(np.concatenate(parts, axis=1)) if inp is not None else None
    return offs, pos[0], arr


def pack_rows(inp):
    offs = {}
    parts = []
    pos = [0]

    def add(name, n, fn):
        offs[name] = pos[0]
        pos[0] += n
        if inp is not None:
            a = np.asarray(fn(), np.float32).reshape(-1)
            assert a.size == n, (name, a.size, n)
            parts.append(a)

    add("sgu_binv", 4096, lambda: inp["sgu_b_in"][0][4096:])
    add("sgu_lng", 4096, lambda: inp["sgu_ln_g"][0])
    add("sgu_bsp", 1024, lambda: inp["sgu_b_spatial"][0])
    add("ssd_dtb", 64, lambda: inp["ssd_dt_bias"][0])
    add("ssd_alog", 64, lambda: inp["ssd_a_log"][0])
    arr = np.ascontiguousarray(np.concatenate(parts)) if inp is not None else None
    return offs, pos[0], arr


WEIGHTS = {
    "ffn_up": ("ffn_w_up", (DEPTH, D, 2 * FFN_H)),
    "ffn_dn": ("ffn_w_down", (DEPTH, FFN_H, D)),
    "xa_q": ("xa_w_q", (DEPTH, D, 512)),
    "xa_kv": ("xa_w_kv", (DEPTH, D, 1024)),
    "xa_o": ("xa_w_o", (DEPTH, 512, D)),
    "ssd_in": ("ssd_w_in", (1, D, 10304)),
    "ssd_out": ("ssd_w_out", (1, 4096, D)),
    "nsa_in": ("nsa_w_in", (1, D, 5168)),
    "nsa_w1": ("nsa_cmp_w1", (1, 2, 4096, 256)),
    "nsa_w2": ("nsa_cmp_w2", (1, 2, 256, 128)),
    "nsa_out": ("nsa_w_out", (1, D, D)),
    "sgu_in": ("sgu_w_in", (1, D, 8192)),
    "sgu_out": ("sgu_w_out", (1, 4096, D)),
    "pool_in": ("pool_w_in", (1, D, D)),
    "pool_grp": ("pool_w_group", (1, 4, 512, 512)),
    "pool_out": ("pool_w_out", (1, D, D)),
}
STAGE_W = {
    "ffn": ["ffn_up", "ffn_dn"],
    "xa": ["xa_q", "xa_kv", "xa_o"],
    "ssd": ["ssd_in", "ssd_out"],
    "nsa": ["nsa_in", "nsa_w1", "nsa_w2", "nsa_out"],
    "sgu": ["sgu_in", "sgu_out"],
    "pool": ["pool_in", "pool_grp", "pool_out"],
}
LAST_NEEDED = []


def make_in_map(inp, b, L, nc_needed):
    _, _, cols = pack_cols(inp)
    _, _, rows = pack_rows(inp)
    m = {"xT": np.ascontiguousarray(inp["x"][b, :L].T), "memT": np.ascontiguousarray(inp["mem"][b].T), "cols": cols,
         "rows": rows,
         "sgu_wsT": np.ascontiguousarray(inp["sgu_w_spatial"][0].transpose(2, 0, 1)),
         "nsa_posT": np.ascontiguousarray(inp["nsa_cmp_pos"][0].transpose(2, 0, 1))}
    for name in nc_needed:
        m[WEIGHTS[name][0]] = np.ascontiguousarray(inp[WEIGHTS[name][0]])
    return m


def xa_stage(P, C, li, x_in, xin_buf, x_out, xout_buf, L):
    T = min(512, L)
    nt = L // T
    O = C.off
    hT = P.alloc("hT", (16, T), BF16)
    big = P.alloc("big", (16, T), F32)
    wsl = [P.alloc("w%d" % i, (8192,), BF16) for i in range(3)]
    sq = P.alloc("sq", (2, T), F32)
    rstd = P.alloc("rstd", (T,), F32)
    xc = [P.alloc("xc%d" % i, (T,), F32) for i in range(2)]
    qT = P.alloc("qT", (4, T), BF16)
    oT = P.alloc("oT", (4, T), BF16)
    KT = P.alloc("KT", (4, 256), BF16)
    V = P.alloc("V", (2, 512), BF16)
    pT = [P.alloc("pT%d" % i, (2, T), BF16) for i in range(2)]
    rden = P.alloc("rden", (T,), F32)
    wq = C.wb["xa_q"][li].rearrange("(kc p) f -> p kc f", p=128)
    wkv = C.wb["xa_kv"][li].rearrange("(kc p) f -> p kc f", p=128)
    wo = C.wb["xa_o"][li].rearrange("(kc p) f -> p kc f", p=128)
    R = Rot(P, wsl, banks=(1, 2, 3))
    scale = 128.0 ** -0.5
    pre_norm(P, C, C.memT, C.memT_buf, 0, 256, O["nmem%d" % li], big, hT, sq, rstd, P.psum[0])

    def put_kt(j, ps):
        P.op("scalar", lambda e: e.activation(out=KT[:, j, :], in_=ps[:, 0:256], func=AF.Copy), reads=[ps],
             writes=[KT.s(j)])

    gemm_fm(P, C, R, hT, 16, wkv, C.wbuf["xa_kv"], 0, 4, 256, put_kt)
    slot = R.slot()
    sv = slot.ap.rearrange("p (k f) -> p k f", k=16)
    for a, b in ((0, 8), (8, 16)):
        P.dma("sync", sv[:, a:b, :], wkv[:, a:b, 512:1024], reads=[C.wbuf["xa_kv"]], writes=[slot])
    for mc in range(2):
        ps = R.bank()
        for kc in range(16):
            P.op("tensor", lambda e, kc=kc, ps=ps, mc=mc: e.matmul(ps[:, 0:512], lhsT=hT[:, kc, mc * 128:(mc + 1) * 128],
                                                                 rhs=sv[:, kc, :], start=(kc == 0), stop=(kc == 15)),
                 reads=[slot, hT.s(kc)], writes=[ps])
        P.op("scalar", lambda e, ps=ps, mc=mc: e.activation(out=V[:, mc, :], in_=ps[:, 0:512], func=AF.Copy),
             reads=[ps], writes=[V.s(mc)])
    for ti in range(nt):
        t0 = ti * T
        pre_norm(P, C, x_in, xin_buf, t0, T, O["pre%d_1" % li], big, hT, sq, rstd, P.psum[0])

        def put_q(j, ps):
            P.op("scalar", lambda e: e.activation(out=qT[:, j, 0:T], in_=ps[:, 0:T], func=AF.Copy), reads=[ps],
                 writes=[qT.s(j)])

        gemm_fm(P, C, R, hT, 16, wq, C.wbuf["xa_q"], 0, 4, T, put_q)
        for h in range(4):
            den = P.psum[4 + h % 2]
            o = P.psum[6 + h % 2]
            pt = pT[h % 2]
            for mc in range(2):
                ps = R.bank()
                P.op("tensor", lambda e, ps=ps, mc=mc, h=h: e.matmul(ps[:, 0:T], lhsT=KT[:, h, mc * 128:(mc + 1) * 128],
                                                                   rhs=qT[:, h, 0:T], start=True, stop=True),
                     reads=[KT.s(h), qT.s(h)], writes=[ps])
                P.op("scalar", lambda e, ps=ps, mc=mc, pt=pt: e.activation(out=pt[:, mc, 0:T], in_=ps[:, 0:T], func=AF.Exp,
                                                                          scale=scale), reads=[ps], writes=[pt.s(mc)])
            for mc in range(2):
                P.op("tensor", lambda e, mc=mc, pt=pt, den=den: e.matmul(den[:, 0:T], lhsT=C.ones_b.ap, rhs=pt[:, mc, 0:T],
                                                                       start=(mc == 0), stop=(mc == 1)),
                     reads=[pt.s(mc), C.ones_b], writes=[den])
            for mc in range(2):
                P.op("tensor", lambda e, mc=mc, pt=pt, o=o, h=h: e.matmul(o[:, 0:T], lhsT=V[:, mc, h * 128:(h + 1) * 128],
                                                                        rhs=pt[:, mc, 0:T], start=(mc == 0), stop=(mc == 1)),
                     reads=[pt.s(mc), V.s(mc)], writes=[o])
            P.op("vector", lambda e, den=den: e.reciprocal(out=rden[:, 0:T], in_=den[:, 0:T]), reads=[den], writes=[rden])
            P.op("vector", lambda e, o=o, h=h: e.tensor_tensor(out=oT[:, h, 0:T], in0=o[:, 0:T], in1=rden[:, 0:T],
                                                               op=ALU.mult), reads=[o, rden], writes=[oT.s(h)])
        out_proj(P, C, oT, 4, wo, C.wbuf["xa_o"], wsl, R, big, sq, rstd, T)
        post_residual(P, C, big, rstd, x_in, xin_buf, x_out, xout_buf, t0, T, O["post%d_1" % li], xc)


class Ctx:
    pass


def setup_persistent(P, C, cols_dram, ncols):
    C.ones_f = P.alloc("ones_f", (128,), F32)
    P.op("gpsimd", lambda e: e.memset(C.ones_f.ap, 1.0), writes=[C.ones_f])
    C.ones_b = P.alloc("ones_b", (128,), BF16)
    P.op("gpsimd", lambda e: e.memset(C.ones_b.ap, 1.0), writes=[C.ones_b])
    C.eps = P.alloc("eps", (1,), F32)
    P.op("gpsimd", lambda e: e.memset(C.eps.ap, EPS), writes=[C.eps])
    C.trile = P.alloc("trile", (128,), F32)
    P.op("gpsimd", lambda e: e.affine_select(out=C.trile.ap, in_=C.ones_f.ap, pattern=[[1, 128]], compare_op=ALU.is_ge,
                                             fill=fillreg(e, 0.0), base=0, channel_multiplier=-1), reads=[C.ones_f], writes=[C.trile])
    C.su = P.alloc("su", (128,), F32)
    P.op("gpsimd", lambda e: e.affine_select(out=C.su.ap, in_=C.ones_f.ap, pattern=[[-1, 128]], compare_op=ALU.is_gt,
                                             fill=fillreg(e, 0.0), base=0, channel_multiplier=1), reads=[C.ones_f], writes=[C.su])
    C.ident = P.alloc("ident", (128,), BF16)
    P.op("gpsimd", lambda e: e.affine_select(out=C.ident.ap, in_=C.ones_f.ap, pattern=[[1, 128]],
                                             compare_op=ALU.is_equal, fill=fillreg(e, 0.0), base=0, channel_multiplier=-1),
         reads=[C.ones_f], writes=[C.ident])
    C.cols = P.alloc("cols", (ncols,), F32)
    P.dma("sync", C.cols.ap, cols_dram, writes=[C.cols])
    P.arena_base = P.arena_off


def cast_weights(P, C):
    for name, (src, dst, dbuf) in C.wcast.items():
        n = 1
        for s in src.shape:
            n *= s
        assert n % 2048 == 0
        rows = n // 2048
        sv = src.reshape([rows, 2048]).ap()
        dv = dst.reshape([rows, 2048]).ap()
        r = 0
        while r < rows:
            rr = min(4096, rows - r)
            P.dma("gpsimd", dv[r:r + rr, :], sv[r:r + rr, :], writes=[dbuf])
            r += rr


def rstd_from_ps(P, C, rstd, ps, T, n=D):
    P.op("scalar", lambda e: e.activation(out=rstd[:, 0:T], in_=ps[:, 0:T], func=AF.Sqrt, scale=1.0 / n,
                                          bias=C.eps.ap),
         reads=[ps, C.eps], writes=[rstd])
    P.op("vector", lambda e: e.reciprocal(out=rstd[:, 0:T], in_=rstd[:, 0:T]), reads=[rstd], writes=[rstd])


def pre_norm(P, C, x_dram, xbuf, t0, T, gcol0, xt, hT, sq, rstd, ps):
    xv = x_dram.rearrange("(c p) l -> p c l", p=128)
    for hh in range(2):
        P.dma("sync", xt[:, 8 * hh:8 * hh + 8, 0:T], xv[:, 8 * hh:8 * hh + 8, t0:t0 + T], reads=[xbuf],
              writes=[xt.s(c) for c in range(8 * hh, 8 * hh + 8)])
    for c in range(NCH):
        P.op("scalar", lambda e, c=c: e.activation(out=sq[:, c % 2, 0:T], in_=xt[:, c, 0:T], func=AF.Square),
             reads=[xt.s(c)], writes=[sq.s(c % 2)])
        P.op("tensor", lambda e, c=c: e.matmul(ps[:, 0:T], lhsT=C.ones_f.ap, rhs=sq[:, c % 2, 0:T],
                                               start=(c == 0), stop=(c == NCH - 1)),
             reads=[sq.s(c % 2), C.ones_f], writes=[ps])
    rstd_from_ps(P, C, rstd, ps, T)
    for c in range(NCH):
        P.op("vector", lambda e, c=c: e.scalar_tensor_tensor(out=hT[:, c, 0:T], in0=xt[:, c, 0:T],
                                                             scalar=C.cols[:, gcol0 + c:gcol0 + c + 1],
                                                             in1=rstd[:, 0:T], op0=ALU.mult, op1=ALU.mult),
             reads=[xt.s(c), rstd, C.cols], writes=[hT.s(c)])


def post_residual(P, C, mT, rstd, x_in, xin_buf, x_out, xout_buf, t0, T, gcol0, xc):
    xv = x_in.rearrange("(c p) l -> p c l", p=128)
    ov = x_out.rearrange("(c p) l -> p c l", p=128)
    for c in range(NCH):
        xcc = xc[c % 2]
        P.dma("sync", xcc[:, 0:T], xv[:, c, t0:t0 + T], reads=[xin_buf], writes=[xcc])
        P.op("vector", lambda e, c=c: e.scalar_tensor_tensor(out=mT[:, c, 0:T], in0=mT[:, c, 0:T],
                                                             scalar=C.cols[:, gcol0 + c:gcol0 + c + 1],
                                                             in1=rstd[:, 0:T], op0=ALU.mult, op1=ALU.mult),
             reads=[mT.s(c), rstd, C.cols], writes=[mT.s(c)])
        P.op("vector", lambda e, c=c, xcc=xcc: e.tensor_tensor(out=mT[:, c, 0:T], in0=mT[:, c, 0:T], in1=xcc[:, 0:T],
                                                               op=ALU.add),
             reads=[mT.s(c), xcc], writes=[mT.s(c)])
        P.dma("gpsimd", ov[:, c, t0:t0 + T], mT[:, c, 0:T], reads=[mT.s(c)], writes=[(xout_buf, (t0, c))])


def dbg_dump(P, C, name, tile, shape, dtype):
    if not getattr(C, "dbg", False):
        return
    d = C.nc.dram_tensor("dbg_" + name, [128] + list(shape), dtype, kind="ExternalOutput").ap()
    P.dma("gpsimd", d, tile.ap if isinstance(tile, Tile) else tile, reads=[tile] if isinstance(tile, Tile) else [],
          writes=[Buf("dbg")])


class Rot:
    def __init__(self, P, wsl, banks=(1, 2, 3, 4, 5, 6, 7)):
        self.P, self.wsl, self.banks = P, wsl, banks
        self.si, self.bi = 0, 0

    def slot(self):
        s = self.wsl[self.si % len(self.wsl)]
        self.si += 1
        return s

    def bank(self):
        b = self.P.psum[self.banks[self.bi % len(self.banks)]]
        self.bi += 1
        return b


def out_proj(P, C, gT, KC, wview, wbuf, wsl, st, big, sq, rstd, T):
    R = st if isinstance(st, Rot) else None
    stats = P.psum[0]
    nd = max(1, min(NCH, 8192 // (KC * 128)))
    for d0 in range(0, NCH, nd):
        slot = R.slot()
        sv = slot.ap[:, 0:KC * nd * 128].rearrange("p (k f) -> p k f", k=KC)
        nsplit = 2 if KC >= 2 else 1
        ks = [0, KC // 2, KC] if nsplit == 2 else [0, KC]
        for a, b in zip(ks[:-1], ks[1:]):
            P.dma("sync", sv[:, a:b, :], wview[:, a:b, d0 * 128:(d0 + nd) * 128], reads=[wbuf], writes=[slot])
        for dj in range(nd):
            dc = d0 + dj
            ps = R.bank()
            for kc in range(KC):
                P.op("tensor", lambda e, kc=kc, sv=sv, ps=ps, dj=dj: e.matmul(
                    ps[:, 0:T], lhsT=sv[:, kc, dj * 128:(dj + 1) * 128], rhs=gT[:, kc, 0:T],
                    start=(kc == 0), stop=(kc == KC - 1)), reads=[slot, gT.s(kc)], writes=[ps])
            P.op("scalar", lambda e, dc=dc, ps=ps: e.activation(out=big[:, dc, 0:T], in_=ps[:, 0:T], func=AF.Copy),
                 reads=[ps], writes=[big.s(dc)])
            P.op("scalar", lambda e, dc=dc, ps=ps: e.activation(out=sq[:, dc % 2, 0:T], in_=ps[:, 0:T], func=AF.Square),
                 reads=[ps], writes=[sq.s(dc % 2)])
            P.op("tensor", lambda e, dc=dc: e.matmul(stats[:, 0:T], lhsT=C.ones_f.ap, rhs=sq[:, dc % 2, 0:T],
                                                     start=(dc == 0), stop=(dc == NCH - 1)),
                 reads=[sq.s(dc % 2), C.ones_f], writes=[stats])
    rstd_from_ps(P, C, rstd, stats, T)


def gemm_fm(P, C, R, hT, KC, wview, wbuf, col0, nchunks, T, consume):
    per = max(1, 8192 // (KC * 128))
    j = 0
    while j < nchunks:
        n = min(per, nchunks - j)
        slot = R.slot()
        sv = slot.ap[:, 0:KC * n * 128].rearrange("p (k f) -> p k f", k=KC)
        ks = [0, KC // 2, KC]
        for a, b in zip(ks[:-1], ks[1:]):
            P.dma("sync", sv[:, a:b, :], wview[:, a:b, col0 + j * 128:col0 + (j + n) * 128], reads=[wbuf],
                  writes=[slot])
        for jj in range(n):
            ps = R.bank()
            for kc in range(KC):
                P.op("tensor", lambda e, kc=kc, sv=sv, ps=ps, jj=jj: e.matmul(
                    ps[:, 0:T], lhsT=sv[:, kc, jj * 128:(jj + 1) * 128], rhs=hT[:, kc, 0:T],
                    start=(kc == 0), stop=(kc == KC - 1)), reads=[slot, hT.s(kc)], writes=[ps])
            consume(j + jj, ps)
        j += n


def ffn_stage(P, C, li, x_in, xin_buf, x_out, xout_buf, L):
    T = min(512, L)
    nt = L // T
    O = C.off
    hT = P.alloc("hT", (16, T), BF16)
    big = P.alloc("big", (16, T), F32)
    act = P.alloc("act", (FFN_HC, T), BF16)
    wsl = [P.alloc("w%d" % i, (8192,), BF16) for i in range(3)]
    uext = [P.alloc("ue%d" % i, (T + 2,), F32) for i in range(4)]
    t2 = [P.alloc("t2%d" % i, (T,), F32) for i in range(4)]
    carry = P.alloc("carry", (88, 2), F32)
    sq = P.alloc("sq", (2, T), F32)
    rstd = P.alloc("rstd", (T,), F32)
    xc = [P.alloc("xc%d" % i, (T,), F32) for i in range(2)]
    wup = C.wb["ffn_up"][li].rearrange("(kc p) f -> p kc f", p=128)
    wdn = C.wb["ffn_dn"][li].rearrange("(kc p) f -> p kc f", p=128)
    wupb, wdnb = C.wbuf["ffn_up"], C.wbuf["ffn_dn"]
    st = Rot(P, wsl)
    rot = [0]
    next_bank = st.bank

    for ti in range(nt):
        t0 = ti * T
        pre_norm(P, C, x_in, xin_buf, t0, T, O["pre%d_2" % li], big, hT, sq, rstd, P.psum[0])

        def do_chunk(cidx, slot, j):
            ps = next_bank()
            sv = slot.ap.rearrange("p (k f) -> p k f", k=16)
            for kc in range(16):
                P.op("tensor", lambda e, kc=kc: e.matmul(ps[:, 0:T], lhsT=sv[:, kc, j * 128:(j + 1) * 128],
                                                         rhs=hT[:, kc, 0:T], start=(kc == 0), stop=(kc == 15)),
                     reads=[slot, hT.s(kc)], writes=[ps])
            r = rot[0] % 4
            rot[0] += 1
            ue, tt = uext[r], t2[r]
            if ti == 0:
                P.op("gpsimd", lambda e: e.memset(ue[:, 0:2], 0.0), writes=[ue])
            else:
                P.op("gpsimd", lambda e: e.tensor_copy(out=ue[:, 0:2], in_=carry[:, cidx, :]),
                     reads=[carry.s(cidx)], writes=[ue])
            P.op("scalar", lambda e: e.activation(out=ue[:, 2:T + 2], in_=ps[:, 0:T], func=AF.Copy),
                 reads=[ps], writes=[ue])
            w2 = O["fcw%d_2" % li] + cidx
            bb = O["fcb%d" % li] + cidx
            P.op("scalar", lambda e: e.activation(out=tt[:, 0:T], in_=ps[:, 0:T], func=AF.Identity,
                                                  scale=C.cols[:, w2:w2 + 1], bias=C.cols[:, bb:bb + 1]),
                 reads=[ps, C.cols], writes=[tt])
            P.op("gpsimd", lambda e: e.tensor_copy(out=carry[:, cidx, :], in_=ue[:, T:T + 2]),
                 reads=[ue], writes=[carry.s(cidx)])
            for k, sh in ((1, 1), (0, 0)):
                wk = O["fcw%d_%d" % (li, k)] + cidx
                P.op("vector", lambda e, wk=wk, sh=sh: e.scalar_tensor_tensor(
                    out=tt[:, 0:T], in0=ue[:, sh:sh + T], scalar=C.cols[:, wk:wk + 1], in1=tt[:, 0:T],
                    op0=ALU.mult, op1=ALU.add), reads=[ue, tt, C.cols], writes=[tt])
            return tt

        for grp in range(11):
            sg = st.slot()
            svw = st.slot()
            for slot, c0 in ((sg, grp * 512), (svw, FFN_H + grp * 512)):
                sv = slot.ap.rearrange("p (k f) -> p k f", k=16)
                for hh in range(2):
                    P.dma("sync", sv[:, 8 * hh:8 * hh + 8, :], wup[:, 8 * hh:8 * hh + 8, c0:c0 + 512],
                          reads=[wupb], writes=[slot])
            for j in range(4):
                fc = grp * 4 + j
                tg = do_chunk(fc, sg, j)
                tv = do_chunk(FFN_HC + fc, svw, j)
                P.op("scalar", lambda e, tg=tg: e.activation(out=tg[:, 0:T], in_=tg[:, 0:T], func=AF.Silu),
                     reads=[tg], writes=[tg])
                P.op("gpsimd", lambda e, tg=tg, tv=tv, fc=fc: e.tensor_tensor(out=act[:, fc, 0:T], in0=tg[:, 0:T],
                                                                             in1=tv[:, 0:T], op=ALU.mult),
                     reads=[tg, tv], writes=[act.s(fc)])
        out_proj(P, C, act, FFN_HC, wdn, wdnb, wsl, st, big, sq, rstd, T)
        post_residual(P, C, big, rstd, x_in, xin_buf, x_out, xout_buf, t0, T, O["post%d_2" % li], xc)


def row_bcast(P, C, tile, name, n):
    o = C.roff[name]
    P.dma("sync", tile.ap, C.rows[o:o + n].partition_broadcast(128), writes=[tile])


def pool_stage(P, C, li, x_in, xin_buf, x_out, xout_buf, L):
    T = min(512, L)
    nt = L // T
    O = C.off
    N = T + 15
    hT = P.alloc("hT", (16, T), BF16)
    big = P.alloc("big", (16, T), F32)
    wsl = [P.alloc("w%d" % i, (8192,), BF16) for i in range(2)]
    sq = P.alloc("sq", (2, T), F32)
    rstd = P.alloc("rstd", (T,), F32)
    xc = [P.alloc("xc%d" % i, (T,), F32) for i in range(2)]
    zext = [P.alloc("ze%d" % i, (N,), F32) for i in range(3)]
    sa = [P.alloc("sa%d" % i, (N,), F32) for i in range(3)]
    sb = [P.alloc("sb%d" % i, (N,), F32) for i in range(3)]
    pooled = P.alloc("pooled", (16, T), BF16)
    yT = P.alloc("yT", (16, T), BF16)
    zc = P.alloc("zc", (16, 15), F32)
    wg = P.alloc("wg", (4, 4, 512), BF16)
    win = C.wb["pool_in"][0].rearrange("(kc p) f -> p kc f", p=128)
    wout = C.wb["pool_out"][0].rearrange("(kc p) f -> p kc f", p=128)
    wgv = C.wb["pool_grp"][0].rearrange("g (dc p) e -> p g dc e", p=128)
    for g in range(4):
        P.dma("sync", wg[:, g, :, :], wgv[:, g, :, :], reads=[C.wbuf["pool_grp"]], writes=[wg])
    R = Rot(P, wsl)
    rot = [0]
    for ti in range(nt):
        t0 = ti * T
        pre_norm(P, C, x_in, xin_buf, t0, T, O["pre%d_0" % li], big, hT, sq, rstd, P.psum[0])

        def put_z(c, ps):
            r = rot[0] % 3
            rot[0] += 1
            ze, a, b = zext[r], sa[r], sb[r]
            w = (2, 4, 8, 16)[c // 4]
            if ti == 0:
                P.op("gpsimd", lambda e: e.memset(ze[:, 0:15], 0.0), writes=[ze])
            else:
                P.op("gpsimd", lambda e: e.tensor_copy(out=ze[:, 0:15], in_=zc[:, c, :]), reads=[zc.s(c)], writes=[ze])
            P.op("scalar", lambda e: e.activation(out=ze[:, 15:N], in_=ps[:, 0:T], func=AF.Copy), reads=[ps], writes=[ze])
            P.op("gpsimd", lambda e: e.tensor_copy(out=zc[:, c, :], in_=ze[:, T:N]), reads=[ze], writes=[zc.s(c)])
            cur, lo, sh = ze, 0, 1
            k = 0
            while sh < w:
                dst = a if k % 2 == 0 else b
                nlo = lo + sh
                P.op("vector", lambda e, cur=cur, dst=dst, nlo=nlo, sh=sh: e.tensor_tensor(
                    out=dst[:, nlo:N], in0=cur[:, nlo:N], in1=cur[:, nlo - sh:N - sh], op=ALU.add),
                    reads=[cur], writes=[dst])
                cur, lo, sh, k = dst, nlo, sh * 2, k + 1
            P.op("vector", lambda e, cur=cur: e.scalar_tensor_tensor(out=pooled[:, c, 0:T], in0=cur[:, 15:N], scalar=1.0 / w,
                                                                  in1=ze[:, 15:N], op0=ALU.mult, op1=ALU.subtract),
                 reads=[cur, ze], writes=[pooled.s(c)])
            if ti == 0:
                for t in range(w - 1):
                    P.op("vector", lambda e, cur=cur, t=t: e.scalar_tensor_tensor(
                        out=pooled[:, c, t:t + 1], in0=cur[:, 15 + t:16 + t], scalar=1.0 / (t + 1),
                        in1=ze[:, 15 + t:16 + t], op0=ALU.mult, op1=ALU.subtract), reads=[cur, ze], writes=[pooled.s(c)])

        gemm_fm(P, C, R, hT, 16, win, C.wbuf["pool_in"], 0, 16, T, put_z)
        for g in range(4):
            for ec in range(4):
                ps = R.bank()
                for dc in range(4):
                    P.op("tensor", lambda e, ps=ps, g=g, ec=ec, dc=dc: e.matmul(
                        ps[:, 0:T], lhsT=wg[:, g, dc, ec * 128:(ec + 1) * 128], rhs=pooled[:, 4 * g + dc, 0:T],
                        start=(dc == 0), stop=(dc == 3)), reads=[wg, pooled.s(4 * g + dc)], writes=[ps])
                sc = O["pscale"] + 4 * g + ec
                P.op("scalar", lambda e, ps=ps, g=g, ec=ec, sc=sc: e.activation(
                    out=yT[:, 4 * g + ec, 0:T], in_=ps[:, 0:T], func=AF.Identity, scale=C.cols[:, sc:sc + 1]),
                    reads=[ps, C.cols], writes=[yT.s(4 * g + ec)])
        out_proj(P, C, yT, 16, wout, C.wbuf["pool_out"], wsl, R, big, sq, rstd, T)
        post_residual(P, C, big, rstd, x_in, xin_buf, x_out, xout_buf, t0, T, O["post%d_0" % li], xc)


GELU_NATIVE = True


def gelu_inplace(P, x_ap, xt, tmp_ap, tmpt, eng2="gpsimd"):
    if GELU_NATIVE:
        P.op("scalar", lambda e: e.activation(out=x_ap, in_=x_ap, func=AF.Gelu_apprx_tanh), reads=[xt], writes=[xt])
        return
    P.op("scalar", lambda e: e.activation(out=tmp_ap, in_=x_ap, func=AF.Square), reads=[xt], writes=[tmpt])
    P.op("vector", lambda e: e.tensor_scalar(out=tmp_ap, in0=tmp_ap, scalar1=0.044715, scalar2=1.0, op0=ALU.mult,
                                             op1=ALU.add), reads=[tmpt], writes=[tmpt])
    P.op("vector", lambda e: e.tensor_tensor(out=tmp_ap, in0=tmp_ap, in1=x_ap, op=ALU.mult), reads=[tmpt, xt],
         writes=[tmpt])
    P.op("scalar", lambda e: e.activation(out=tmp_ap, in_=tmp_ap, func=AF.Sigmoid, scale=1.5957691216057308),
         reads=[tmpt], writes=[tmpt])
    P.op(eng2, lambda e: e.tensor_tensor(out=x_ap, in0=x_ap, in1=tmp_ap, op=ALU.mult), reads=[tmpt, xt], writes=[xt])


def sgu_stage(P, C, li, x_in, xin_buf, x_out, xout_buf, L):
    T = min(128, L)
    nt = L // T
    ntc = T // 128
    O = C.off
    hT = P.alloc("hT", (16, T), BF16)
    big = P.alloc("big", (16, T), F32)
    wsl = [P.alloc("w%d" % i, (8192,), BF16) for i in range(3)]
    sq = P.alloc("sq", (2, T), F32)
    rstd = P.alloc("rstd", (T,), F32)
    xc = [P.alloc("xc%d" % i, (T,), F32) for i in range(2)]
    uT = P.alloc("uT", (32, T), BF16)
    uf = [P.alloc("uf%d" % i, (T,), F32) for i in range(2)]
    ut = [P.alloc("ut%d" % i, (T,), F32) for i in range(2)]
    vtm = P.alloc("vtm", (ntc, 4096), F32)
    vtmp = P.alloc("vtmp", (512,), F32)
    vn = P.alloc("vn", (ntc, 4096), BF16)
    gT = P.alloc("gT", (32, T), BF16)
    binb = P.alloc("binb", (4096,), F32)
    lngb = P.alloc("lngb", (4096,), F32)
    bsp = P.alloc("bsp", (8, 128), F32)
    wsf = P.alloc("wsf", (8, 128), F32)
    wmT = P.alloc("wmT", (8, 128), BF16)
    st4 = P.alloc("st4", (8,), F32)
    sp = [P.alloc("sp%d" % i, (128,), F32) for i in range(2)]
    row_bcast(P, C, binb, "sgu_binv", 4096)
    row_bcast(P, C, lngb, "sgu_lng", 4096)
    o = C.roff["sgu_bsp"]
    P.dma("sync", bsp.ap, C.rows[o:o + 1024].partition_broadcast(128).rearrange("p (g t) -> p g t", g=8), writes=[bsp])
    P.dma("sync", wsf.ap, C.sgu_wsT, writes=[wsf])
    P.op("gpsimd", lambda e: e.affine_select(out=wmT.ap, in_=wsf.ap, pattern=[[0, 8], [1, 128]], compare_op=ALU.is_ge,
                                             fill=fillreg(e, 0.0), base=0, channel_multiplier=-1), reads=[wsf], writes=[wmT])
    win = C.wb["sgu_in"][0].rearrange("(kc p) f -> p kc f", p=128)
    wout = C.wb["sgu_out"][0].rearrange("(kc p) f -> p kc f", p=128)
    R = Rot(P, wsl, banks=(1, 2, 3, 4, 5))
    rot = [0]
    for ti in range(nt):
        t0 = ti * T
        pre_norm(P, C, x_in, xin_buf, t0, T, O["pre%d_0" % li], big, hT, sq, rstd, P.psum[0])

        def put_u(j, ps):
            r = rot[0] % 2
            rot[0] += 1
            bc = O["sgu_binu"] + j
            P.op("scalar", lambda e: e.activation(out=uf[r][:, 0:T], in_=ps[:, 0:T], func=AF.Identity,
                                                  bias=C.cols[:, bc:bc + 1]), reads=[ps, C.cols], writes=[uf[r]])
            gelu_inplace(P, uf[r][:, 0:T], uf[r], ut[r][:, 0:T], ut[r])
            P.op("gpsimd", lambda e: e.tensor_copy(out=uT[:, j, 0:T], in_=uf[r][:, 0:T]), reads=[uf[r]], writes=[uT.s(j)])

        gemm_fm(P, C, R, hT, 16, win, C.wbuf["sgu_in"], 0, 32, T, put_u)
        for grp in range(8):
            slot = R.slot()
            sv = slot.ap.rearrange("p (k f) -> p k f", k=16)
            for a, b in ((0, 8), (8, 16)):
                P.dma("sync", sv[:, a:b, :], win[:, a:b, 4096 + grp * 512:4096 + (grp + 1) * 512],
                      reads=[C.wbuf["sgu_in"]], writes=[slot])
            for tc in range(ntc):
                ps = R.bank()
                for kc in range(16):
                    P.op("tensor", lambda e, kc=kc, ps=ps, tc=tc, sv=sv: e.matmul(
                        ps[:, 0:512], lhsT=hT[:, kc, tc * 128:(tc + 1) * 128], rhs=sv[:, kc, :], start=(kc == 0),
                        stop=(kc == 15)), reads=[slot, hT.s(kc)], writes=[ps])
                sub = tc * 8 + grp
                va = vtm[:, tc, grp * 512:(grp + 1) * 512]
                P.op("vector", lambda e, ps=ps, va=va, grp=grp: e.tensor_tensor(
                    out=va, in0=ps[:, 0:512], in1=binb[:, grp * 512:(grp + 1) * 512], op=ALU.add),
                    reads=[ps, binb], writes=[vtm.s(sub)])
                gelu_inplace(P, va, vtm.s(sub), vtmp[:, 0:512], vtmp)
        for tc in range(ntc):
            vv = vtm[:, tc, :]
            P.op("scalar", lambda e, vv=vv, tc=tc: e.activation(out=vn[:, tc, :], in_=vv, func=AF.Copy,
                                                              accum_out=st4[:, 0:1]), reads=[vtm], writes=[vn.s(tc), st4])
            P.op("scalar", lambda e, vv=vv, tc=tc: e.activation(out=vn[:, tc, :], in_=vv, func=AF.Square,
                                                              accum_out=st4[:, 1:2]), reads=[vtm, st4],
                 writes=[vn.s(tc), st4])
            P.op("vector", lambda e: e.tensor_scalar(out=st4[:, 2:3], in0=st4[:, 0:1], scalar1=1.0 / 4096, scalar2=None,
                                                     op0=ALU.mult), reads=[st4], writes=[st4])
            P.op("vector", lambda e: e.tensor_tensor(out=st4[:, 3:4], in0=st4[:, 2:3], in1=st4[:, 2:3], op=ALU.mult),
                 reads=[st4], writes=[st4])
            P.op("vector", lambda e: e.scalar_tensor_tensor(out=st4[:, 4:5], in0=st4[:, 1:2], scalar=1.0 / 4096,
                                                            in1=st4[:, 3:4], op0=ALU.mult, op1=ALU.subtract),
                 reads=[st4], writes=[st4])
            P.op("scalar", lambda e: e.activation(out=st4[:, 5:6], in_=st4[:, 4:5], func=AF.Sqrt, bias=C.eps.ap),
                 reads=[st4, C.eps], writes=[st4])
            P.op("vector", lambda e: e.reciprocal(out=st4[:, 5:6], in_=st4[:, 5:6]), reads=[st4], writes=[st4])
            P.op("vector", lambda e, vv=vv: e.tensor_scalar(out=vv, in0=vv, scalar1=st4[:, 2:3], scalar2=st4[:, 5:6],
                                                            op0=ALU.subtract, op1=ALU.mult), reads=[vtm, st4], writes=[vtm])
            P.op("gpsimd", lambda e, vv=vv, tc=tc: e.tensor_tensor(out=vn[:, tc, :], in0=vv, in1=lngb.ap, op=ALU.mult),
                 reads=[vtm, lngb], writes=[vn.s(tc)])
            for q4 in range(8):
                ps = R.bank()
                for jj in range(4):
                    dcv = q4 * 4 + jj
                    P.op("tensor", lambda e, ps=ps, jj=jj, dcv=dcv, tc=tc, q4=q4: e.matmul(
                        ps[:, jj * 128:(jj + 1) * 128], lhsT=vn[:, tc, dcv * 128:(dcv + 1) * 128], rhs=wmT[:, q4, :],
                        start=True, stop=True), reads=[vn.s(tc), wmT], writes=[ps])
                for jj in range(4):
                    dcv = q4 * 4 + jj
                    s = sp[(q4 * 4 + jj) % 2]
                    P.op("vector", lambda e, ps=ps, jj=jj, s=s, q4=q4: e.tensor_tensor(
                        out=s.ap, in0=ps[:, jj * 128:(jj + 1) * 128], in1=bsp[:, q4, :], op=ALU.add),
                        reads=[ps, bsp], writes=[s])
                    P.op("gpsimd", lambda e, s=s, dcv=dcv, tc=tc: e.tensor_tensor(
                        out=gT[:, dcv, tc * 128:(tc + 1) * 128], in0=s.ap, in1=uT[:, dcv, tc * 128:(tc + 1) * 128],
                        op=ALU.mult), reads=[s, uT.s(dcv)], writes=[gT.s(dcv)])
        out_proj(P, C, gT, 32, wout, C.wbuf["sgu_out"], wsl, R, big, sq, rstd, T)
        post_residual(P, C, big, rstd, x_in, xin_buf, x_out, xout_buf, t0, T, O["post%d_0" % li], xc)


def ssd_stage(P, C, li, x_in, xin_buf, x_out, xout_buf, L):
    T = 128
    nt = L // T
    O = C.off
    hT = P.alloc("hT", (16, T), BF16)
    big = P.alloc("big", (16, T), F32)
    wsl = [P.alloc("w%d" % i, (8192,), BF16) for i in range(2)]
    sq = P.alloc("sq", (2, T), F32)
    rstd = P.alloc("rstd", (T,), F32)
    xc = [P.alloc("xc%d" % i, (T,), F32) for i in range(2)]
    szT = P.alloc("szT", (32, T), BF16)
    xsb = P.alloc("xsb", (32, T), BF16)
    BT = P.alloc("BT", (8, T), BF16)
    CT = P.alloc("CT", (8, T), BF16)
    ue = [P.alloc("ue%d" % i, (T + 3,), F32) for i in range(3)]
    tt = [P.alloc("tt%d" % i, (T,), F32) for i in range(3)]
    carry = P.alloc("carry", (48, 3), F32)
    wdt = P.alloc("wdt", (16, 64), BF16)
    dtb = P.alloc("dtb", (64,), F32)
    Aneg = P.alloc("Aneg", (64,), F32)
    d1 = P.alloc("d1", (64,), F32)
    d2 = P.alloc("d2", (64,), F32)
    dt = P.alloc("dt", (64,), F32)
    aa = P.alloc("aa", (64,), F32)
    dte = P.alloc("dte", (64,), F32)
    dec = P.alloc("dec", (64,), F32)
    x_tm = P.alloc("x_tm", (4096,), BF16)
    B_tm = P.alloc("B_tm", (1024,), BF16)
    xdt = P.alloc("xdt", (4096,), BF16)
    xdtw = P.alloc("xdtw", (4096,), BF16)
    cbm = P.alloc("cbm", (8, 128), BF16)
    R4 = [P.alloc("R4%d" % i, (4, 128), F32) for i in range(2)]
    LT = [P.alloc("LT%d" % i, (4, 128), BF16) for i in range(2)]
    MT = [P.alloc("MT%d" % i, (4, 128), BF16) for i in range(2)]
    Ed = [P.alloc("Ed%d" % i, (4, 128), BF16) for i in range(2)]
    Cd = [P.alloc("Cd%d" % i, (4, 128), BF16) for i in range(2)]
    yT = P.alloc("yT", (32, T), F32)
    S = P.alloc("S", (4096,), F32)
    prevT = P.alloc("prevT", (4096,), BF16)
    gT = P.alloc("gT", (32, T), BF16)
    win = C.wb["ssd_in"][0].rearrange("(kc p) f -> p kc f", p=128)
    wout = C.wb["ssd_out"][0].rearrange("(kc p) f -> p kc f", p=128)
    wib = C.wbuf["ssd_in"]
    P.dma("sync", wdt.ap, win[:, :, 10240:10304], reads=[wib], writes=[wdt])
    row_bcast(P, C, dtb, "ssd_dtb", 64)
    row_bcast(P, C, Aneg, "ssd_alog", 64)
    P.op("scalar", lambda e: e.activation(out=Aneg.ap, in_=Aneg.ap, func=AF.Exp), reads=[Aneg], writes=[Aneg])
    P.op("vector", lambda e: e.tensor_scalar(out=Aneg.ap, in0=Aneg.ap, scalar1=-1.0, scalar2=None, op0=ALU.mult),
         reads=[Aneg], writes=[Aneg])
    P.op("gpsimd", lambda e: e.memset(S.ap, 0.0), writes=[S])
    R = Rot(P, wsl)
    rot = [0]
    cwo, cbo, ngo, dco = O["ssd_cw"], O["ssd_cb"], O["ssd_ng"], O["ssd_dcol"]
    for ti in range(nt):
        t0 = ti * T
        pre_norm(P, C, x_in, xin_buf, t0, T, O["pre%d_0" % li], big, hT, sq, rstd, P.psum[0])
        P.op("gpsimd", lambda e: e.tensor_copy(out=prevT.ap, in_=S.ap), reads=[S], writes=[prevT])

        def put_z(j, ps):
            P.op("scalar", lambda e: e.activation(out=szT[:, j, :], in_=ps[:, 0:T], func=AF.Silu), reads=[ps],
                 writes=[szT.s(j)])

        gemm_fm(P, C, R, hT, 16, win, wib, 0, 32, T, put_z)

        def put_xbc(j, ps):
            r = rot[0] % 3
            rot[0] += 1
            u, t_ = ue[r], tt[r]
            if ti == 0:
                P.op("gpsimd", lambda e: e.memset(u[:, 0:3], 0.0), writes=[u])
            else:
                P.op("gpsimd", lambda e: e.tensor_copy(out=u[:, 0:3], in_=carry[:, j, :]), reads=[carry.s(j)], writes=[u])
            P.op("scalar", lambda e: e.activation(out=u[:, 3:T + 3], in_=ps[:, 0:T], func=AF.Copy), reads=[ps], writes=[u])
            P.op("scalar", lambda e: e.activation(out=t_.ap, in_=ps[:, 0:T], func=AF.Identity,
                                                  scale=C.cols[:, cwo + 3 * 48 + j:cwo + 3 * 48 + j + 1],
                                                  bias=C.cols[:, cbo + j:cbo + j + 1]), reads=[ps, C.cols], writes=[t_])
            P.op("gpsimd", lambda e: e.tensor_copy(out=carry[:, j, :], in_=u[:, T:T + 3]), reads=[u], writes=[carry.s(j)])
            for k in (2, 1, 0):
                wk = cwo + k * 48 + j
                P.op("vector", lambda e, k=k, wk=wk: e.scalar_tensor_tensor(
                    out=t_.ap, in0=u[:, k:k + T], scalar=C.cols[:, wk:wk + 1], in1=t_.ap, op0=ALU.mult, op1=ALU.add),
                    reads=[u, t_, C.cols], writes=[t_])
            if j < 32:
                dst, dd = xsb[:, j, :], xsb.s(j)
            elif j < 40:
                dst, dd = BT[:, j - 32, :], BT.s(j - 32)
            else:
                dst, dd = CT[:, j - 40, :], CT.s(j - 40)
            P.op("scalar", lambda e: e.activation(out=dst, in_=t_.ap, func=AF.Silu), reads=[t_], writes=[dd])

        gemm_fm(P, C, R, hT, 16, win, wib, 4096, 48, T, put_xbc)
        ps = R.bank()
        for kc in range(16):
            P.op("tensor", lambda e, kc=kc, ps=ps: e.matmul(ps[:, 0:64], lhsT=hT[:, kc, :], rhs=wdt[:, kc, :],
                                                          start=(kc == 0), stop=(kc == 15)),
                 reads=[hT.s(kc), wdt], writes=[ps])
        P.op("vector", lambda e, ps=ps: e.tensor_tensor(out=d1.ap, in0=ps[:, 0:64], in1=dtb.ap, op=ALU.add),
             reads=[ps, dtb], writes=[d1])
        P.op("scalar", lambda e: e.activation(out=d2.ap, in_=d1.ap, func=AF.Abs), reads=[d1], writes=[d2])
        P.op("scalar", lambda e: e.activation(out=d2.ap, in_=d2.ap, func=AF.Exp, scale=-1.0), reads=[d2], writes=[d2])
        P.op("scalar", lambda e: e.activation(out=d2.ap, in_=d2.ap, func=AF.Ln, bias=1.0), reads=[d2], writes=[d2])
        P.op("vector", lambda e: e.scalar_tensor_tensor(out=dt.ap, in0=d1.ap, scalar=0.0, in1=d2.ap, op0=ALU.max,
                                                        op1=ALU.add), reads=[d1, d2], writes=[dt])
        P.op("vector", lambda e: e.tensor_tensor(out=aa.ap, in0=dt.ap, in1=Aneg.ap, op=ALU.mult), reads=[dt, Aneg],
             writes=[aa])
        for q in range(5):
            ps = R.bank()
            psb = ps.ap.bitcast(BF16)
            for jj in range(8):
                j = q * 8 + jj
                src_ap = xsb[:, j, :] if j < 32 else BT[:, j - 32, :]
                sd = xsb.s(j) if j < 32 else BT.s(j - 32)
                P.op("tensor", lambda e, psb=psb, jj=jj, src_ap=src_ap: e.transpose(
                    out=psb[:, jj * 128:(jj + 1) * 128], in_=src_ap, identity=C.ident.ap), reads=[sd, C.ident], writes=[ps])
            if q < 4:
                P.op("scalar", lambda e, psb=psb, q=q: e.activation(out=x_tm[:, q * 1024:(q + 1) * 1024], in_=psb[:, 0:1024],
                                                                  func=AF.Copy), reads=[ps], writes=[x_tm])
            else:
                P.op("scalar", lambda e, psb=psb: e.activation(out=B_tm.ap, in_=psb[:, 0:1024], func=AF.Copy), reads=[ps],
                     writes=[B_tm])
        ps = R.bank()
        P.op("tensor", lambda e, ps=ps: e.matmul(ps[:, 0:64], lhsT=C.su.ap, rhs=aa.ap, start=True, stop=True),
             reads=[C.su, aa], writes=[ps])
        P.op("scalar", lambda e, ps=ps: e.activation(out=dte.ap, in_=ps[:, 0:64], func=AF.Exp), reads=[ps], writes=[dte])
        ps = R.bank()
        P.op("tensor", lambda e, ps=ps: e.matmul(ps[:, 0:64], lhsT=C.ones_f.ap, rhs=aa.ap, start=True, stop=True),
             reads=[C.ones_f, aa], writes=[ps])
        P.op("scalar", lambda e, ps=ps: e.activation(out=dec.ap, in_=ps[:, 0:64], func=AF.Exp), reads=[ps], writes=[dec])
        v3 = lambda t_: t_.ap.rearrange("p (h q) -> p h q", h=64)
        bc = lambda t_: t_.ap.unsqueeze(2).broadcast_to([128, 64, 64])
        P.op("vector", lambda e: e.tensor_tensor(out=v3(xdt), in0=v3(x_tm), in1=bc(dt), op=ALU.mult), reads=[x_tm, dt],
             writes=[xdt])
        P.op("gpsimd", lambda e: e.tensor_tensor(out=v3(xdtw), in0=v3(xdt), in1=bc(dte), op=ALU.mult), reads=[xdt, dte],
             writes=[xdtw])
        for g in range(8):
            ps = R.bank()
            P.op("tensor", lambda e, ps=ps, g=g: e.matmul(ps[:, 0:128], lhsT=BT[:, g, :], rhs=CT[:, g, :], start=True,
                                                        stop=True), reads=[BT.s(g), CT.s(g)], writes=[ps])
            P.op("vector", lambda e, ps=ps, g=g: e.tensor_tensor(out=cbm[:, g, :], in0=ps[:, 0:128], in1=C.trile.ap,
                                                               op=ALU.mult), reads=[ps, C.trile], writes=[cbm.s(g)])
        for q in range(16):
            g = q // 2
            r4, lt, mt, ed, cd = R4[q % 2], LT[q % 2], MT[q % 2], Ed[q % 2], Cd[q % 2]
            for i in range(4):
                h = 4 * q + i
                P.op("vector", lambda e, i=i, h=h, r4=r4: e.tensor_scalar(out=r4[:, i, :], in0=C.trile.ap,
                                                                        scalar1=aa[:, h:h + 1], scalar2=None,
                                                                        op0=ALU.mult), reads=[C.trile, aa], writes=[r4])
            r4f = r4.ap.rearrange("p a b -> p (a b)")
            ps1 = R.bank()
            P.op("tensor", lambda e, ps1=ps1, r4f=r4f: e.matmul(ps1[:, 0:512], lhsT=C.su.ap, rhs=r4f, start=True, stop=True),
                 reads=[C.su, r4], writes=[ps1])
            ps2 = R.bank()
            P.op("tensor", lambda e, ps2=ps2, r4f=r4f: e.matmul(ps2[:, 0:512], lhsT=C.ones_f.ap, rhs=r4f, start=True,
                                                              stop=True), reads=[C.ones_f, r4], writes=[ps2])
            P.op("scalar", lambda e, ps1=ps1, lt=lt: e.activation(out=lt.ap.rearrange("p a b -> p (a b)"), in_=ps1[:, 0:512],
                                                                func=AF.Exp), reads=[ps1], writes=[lt])
            P.op("scalar", lambda e, ps2=ps2, ed=ed: e.activation(out=ed.ap.rearrange("p a b -> p (a b)"), in_=ps2[:, 0:512],
                                                                func=AF.Exp), reads=[ps2], writes=[ed])
            P.op("vector", lambda e, lt=lt, mt=mt, g=g: e.tensor_tensor(
                out=mt.ap, in0=lt.ap, in1=cbm[:, g, :].unsqueeze(1).broadcast_to([128, 4, 128]), op=ALU.mult),
                reads=[lt, cbm.s(g)], writes=[mt])
            P.op("gpsimd", lambda e, ed=ed, cd=cd, g=g: e.tensor_tensor(
                out=cd.ap, in0=ed.ap, in1=CT[:, g, :].unsqueeze(1).broadcast_to([128, 4, 128]), op=ALU.mult),
                reads=[ed, CT.s(g)], writes=[cd])
            psy = R.bank()
            for i in range(4):
                h = 4 * q + i
                pc = h // 2
                P.op("tensor", lambda e, psy=psy, i=i, pc=pc, mt=mt: e.matmul(
                    psy[:, i * 128:(i + 1) * 128], lhsT=xdt[:, pc * 128:(pc + 1) * 128], rhs=mt[:, i, :], start=True,
                    stop=False), reads=[xdt, mt], writes=[psy])
                P.op("tensor", lambda e, psy=psy, i=i, pc=pc, cd=cd: e.matmul(
                    psy[:, i * 128:(i + 1) * 128], lhsT=prevT[:, pc * 128:(pc + 1) * 128], rhs=cd[:, i, :], start=False,
                    stop=True), reads=[prevT, cd], writes=[psy])
            for i in range(4):
                h = 4 * q + i
                pc, r0 = h // 2, (h % 2) * 64
                P.op("vector", lambda e, psy=psy, i=i, pc=pc, r0=r0: e.scalar_tensor_tensor(
                    out=yT[r0:r0 + 64, pc, :], in0=xsb[r0:r0 + 64, pc, :], scalar=C.cols[r0:r0 + 64, dco + pc:dco + pc + 1],
                    in1=psy[r0:r0 + 64, i * 128:(i + 1) * 128], op0=ALU.mult, op1=ALU.add),
                    reads=[psy, xsb.s(pc), C.cols], writes=[yT.s(pc)])
        for g in range(8):
            ps = R.bank()
            P.op("tensor", lambda e, ps=ps, g=g: e.matmul(ps[:, 0:512], lhsT=B_tm[:, g * 128:(g + 1) * 128],
                                                        rhs=xdtw[:, g * 512:(g + 1) * 512], start=True, stop=True),
                 reads=[B_tm, xdtw], writes=[ps])
            sv_ = S.ap[:, g * 512:(g + 1) * 512].rearrange("p (h q) -> p h q", h=8)
            P.op("vector", lambda e, g=g, sv_=sv_: e.tensor_tensor(
                out=sv_, in0=sv_, in1=dec[:, 8 * g:8 * g + 8].unsqueeze(2).broadcast_to([128, 8, 64]), op=ALU.mult),
                reads=[S, dec, prevT], writes=[S])
            P.op("vector", lambda e, g=g, ps=ps: e.tensor_tensor(out=S[:, g * 512:(g + 1) * 512],
                                                               in0=S[:, g * 512:(g + 1) * 512], in1=ps[:, 0:512],
                                                               op=ALU.add), reads=[S, ps], writes=[S])
        stats = P.psum[0]
        for j in range(32):
            P.op("gpsimd", lambda e, j=j: e.tensor_tensor(out=yT[:, j, :], in0=yT[:, j, :], in1=szT[:, j, :], op=ALU.mult),
                 reads=[yT.s(j), szT.s(j)], writes=[yT.s(j)])
            P.op("scalar", lambda e, j=j: e.activation(out=sq[:, j % 2, :], in_=yT[:, j, :], func=AF.Square),
                 reads=[yT.s(j)], writes=[sq.s(j % 2)])
            P.op("tensor", lambda e, j=j: e.matmul(stats[:, 0:T], lhsT=C.ones_f.ap, rhs=sq[:, j % 2, :], start=(j == 0),
                                                   stop=(j == 31)), reads=[sq.s(j % 2), C.ones_f], writes=[stats])
        rstd_from_ps(P, C, rstd, stats, T, n=4096)
        for j in range(32):
            P.op("vector", lambda e, j=j: e.scalar_tensor_tensor(out=gT[:, j, :], in0=yT[:, j, :],
                                                                 scalar=C.cols[:, ngo + j:ngo + j + 1], in1=rstd[:, 0:T],
                                                                 op0=ALU.mult, op1=ALU.mult),
                 reads=[yT.s(j), rstd, C.cols], writes=[gT.s(j)])
        out_proj(P, C, gT, 32, wout, C.wbuf["ssd_out"], wsl, R, big, sq, rstd, T)
        post_residual(P, C, big, rstd, x_in, xin_buf, x_out, xout_buf, t0, T, O["post%d_0" % li], xc)


def nsa_stage(P, C, li, x_in, xin_buf, x_out, xout_buf, L):
    O = C.off
    nc = C.nc
    NC = (L - 32) // 16 + 1
    NIC = (NC + 127) // 128
    NSL = L // 64
    scale = 128.0 ** -0.5
    kvT_d = nc.dram_tensor("nsa_kvT", [4, 4, 128, L], BF16, kind="Internal").ap()
    vtm_d = nc.dram_tensor("nsa_vtm", [2, L, 512], BF16, kind="Internal").ap()
    kvT_b, vtm_b = Buf("kvT"), Buf("vtm")
    win = C.wb["nsa_in"][0].rearrange("(kc p) f -> p kc f", p=128)
    wout = C.wb["nsa_out"][0].rearrange("(kc p) f -> p kc f", p=128)
    wib = C.wbuf["nsa_in"]
    kcT = P.alloc("kcT", (4, NIC * 128), BF16)
    vc_tm = P.alloc("vc_tm", (NIC, 4, 128), BF16)
    P.op("gpsimd", lambda e: e.memset(kcT.ap, 0.0), writes=[kcT])
    P.op("gpsimd", lambda e: e.memset(vc_tm.ap, 0.0), writes=[vc_tm])
    mark = P.arena_off

    def phase_a():
        T = min(512, L)
        nt = L // T
        hT = P.alloc("hT", (16, T), BF16)
        big = P.alloc("big", (16, T), F32)
        wsl = [P.alloc("w%d" % i, (8192,), BF16) for i in range(3)]
        sq = P.alloc("sq", (2, T), F32)
        rstd = P.alloc("rstd", (T,), F32)
        stg = [P.alloc("stg%d" % i, (T,), BF16) for i in range(3)]
        R = Rot(P, wsl)
        rot = [0]
        FM = ((0, 2048), (1, 2560), (2, 3072), (3, 4096))
        for ti in range(nt):
            t0 = ti * T
            pre_norm(P, C, x_in, xin_buf, t0, T, O["pre%d_0" % li], big, hT, sq, rstd, P.psum[0])
            for fam, c0 in FM:
                def put(j, ps, fam=fam):
                    s = stg[rot[0] % 3]
                    rot[0] += 1
                    P.op("scalar", lambda e: e.activation(out=s[:, 0:T], in_=ps[:, 0:T], func=AF.Copy), reads=[ps], writes=[s])
                    P.dma("gpsimd", kvT_d[fam, j, :, t0:t0 + T], s[:, 0:T], reads=[s], writes=[kvT_b])
                gemm_fm(P, C, R, hT, 16, win, wib, c0, 4, T, put)
            for f, c0 in ((0, 3584), (1, 4608)):
                slot = R.slot()
                sv = slot.ap.rearrange("p (k f) -> p k f", k=16)
                for a, b in ((0, 8), (8, 16)):
                    P.dma("sync", sv[:, a:b, :], win[:, a:b, c0:c0 + 512], reads=[wib], writes=[slot])
                for tc in range(T // 128):
                    ps = R.bank()
                    for kc in range(16):
                        P.op("tensor", lambda e, kc=kc, ps=ps, tc=tc, sv=sv: e.matmul(
                            ps[:, 0:512], lhsT=hT[:, kc, tc * 128:(tc + 1) * 128], rhs=sv[:, kc, :], start=(kc == 0),
                            stop=(kc == 15)), reads=[slot, hT.s(kc)], writes=[ps])
                    s = stg[rot[0] % 3]
                    rot[0] += 1
                    P.op("scalar", lambda e, s=s, ps=ps: e.activation(out=s[:, 0:512], in_=ps[:, 0:512], func=AF.Copy), reads=[ps],
                         writes=[s])
                    P.dma("gpsimd", vtm_d[f, t0 + tc * 128:t0 + (tc + 1) * 128, :], s[:, 0:512], reads=[s], writes=[vtm_b])
        P.barrier()
        P.arena_off = mark
    phase_a()

    def phase_b():
        w1s = P.alloc("w1s", (32, 256), BF16)
        w2s = P.alloc("w2s", (2, 128), BF16)
        posf = P.alloc("posf", (2, 32), F32)
        posb = P.alloc("posb", (2, 32), BF16)
        pbcol = P.alloc("pbcol", (2,), F32)
        kin = [P.alloc("kin%d" % i, (L,), BF16) for i in range(2)]
        Hg = P.alloc("Hg", (2, NIC * 128), BF16)
        P.dma("sync", posf.ap, C.nsa_posT, writes=[posf])
        P.op("vector", lambda e: e.tensor_copy(out=posb.ap, in_=posf.ap), reads=[posf], writes=[posb])
        P.op("gpsimd", lambda e: e.memset(Hg.ap, 0.0), writes=[Hg])
        Rb = Rot(P, [], banks=(1, 2, 3, 4, 5, 6, 7))
        w1d = C.wb["nsa_w1"][0]
        w2d = C.wb["nsa_w2"][0]
        for fam in range(2):
            P.dma("sync", w1s.ap, w1d[fam].rearrange("(l p) h -> p l h", p=128), reads=[C.wbuf["nsa_w1"]], writes=[w1s])
            P.dma("sync", w2s.ap, w2d[fam].rearrange("(c p) d -> p c d", p=128), reads=[C.wbuf["nsa_w2"]], writes=[w2s])
            for hc in range(2):
                ps = Rb.bank()
                for l in range(32):
                    P.op("tensor", lambda e, ps=ps, l=l, hc=hc, fam=fam: e.matmul(
                        ps[:, 0:1], lhsT=w1s[:, l, hc * 128:(hc + 1) * 128], rhs=posb[:, fam, l:l + 1], start=(l == 0),
                        stop=(l == 31)), reads=[w1s, posb], writes=[ps])
                P.op("scalar", lambda e, ps=ps, hc=hc: e.activation(out=pbcol[:, hc:hc + 1], in_=ps[:, 0:1], func=AF.Copy),
                     reads=[ps], writes=[pbcol])
            for g in range(4):
                kk = kin[g % 2]
                P.dma("sync", kk.ap, kvT_d[fam, g, :, :], reads=[kvT_b], writes=[kk])
                for hc in range(2):
                    ps = Rb.bank()
                    for l in range(32):
                        P.op("tensor", lambda e, ps=ps, l=l, hc=hc, kk=kk: e.matmul(
                            ps[:, 0:NC], lhsT=w1s[:, l, hc * 128:(hc + 1) * 128], rhs=kk[:, l:l + 16 * (NC - 1) + 1:16],
                            start=(l == 0), stop=(l == 31)), reads=[w1s, kk], writes=[ps])
                    P.op("scalar", lambda e, ps=ps, hc=hc: e.activation(out=Hg[:, hc, 0:NC], in_=ps[:, 0:NC],
                                                                      func=AF.Gelu_apprx_tanh, bias=pbcol[:, hc:hc + 1]),
                         reads=[ps, pbcol], writes=[Hg])
                if fam == 0:
                    ps = Rb.bank()
                    for hc in range(2):
                        P.op("tensor", lambda e, ps=ps, hc=hc: e.matmul(ps[:, 0:NC], lhsT=w2s[:, hc, :], rhs=Hg[:, hc, 0:NC],
                                                                      start=(hc == 0), stop=(hc == 1)), reads=[w2s, Hg],
                             writes=[ps])
                    P.op("scalar", lambda e, ps=ps, g=g: e.activation(out=kcT[:, g, 0:NC], in_=ps[:, 0:NC], func=AF.Copy),
                         reads=[ps], writes=[kcT])
                else:
                    for ic in range(NIC):
                        ni = min(128, NC - ic * 128)
                        ps = Rb.bank()
                        for hc in range(2):
                            P.op("tensor", lambda e, ps=ps, hc=hc, ic=ic, ni=ni: e.matmul(
                                ps[0:ni, 0:128], lhsT=Hg[:, hc, ic * 128:ic * 128 + ni], rhs=w2s[:, hc, :], start=(hc == 0),
                                stop=(hc == 1)), reads=[w2s, Hg], writes=[ps])
                        P.op("scalar", lambda e, ps=ps, ic=ic, g=g, ni=ni: e.activation(out=vc_tm[0:ni, ic, g, :],
                                                                                     in_=ps[0:ni, 0:128], func=AF.Copy),
                             reads=[ps], writes=[vc_tm])
        P.barrier()
        P.arena_off = mark
    phase_b()

    def phase_c():
        T = min(256, L)
        nt = L // T
        NTC = T // 128
        hT = P.alloc("hT", (16, T), BF16)
        big = P.alloc("big", (16, T), F32)
        wsl = [P.alloc("w%d" % i, (8192,), BF16) for i in range(2)]
        sq = P.alloc("sq", (2, T), F32)
        rstd = P.alloc("rstd", (T,), F32)
        xc = [P.alloc("xc%d" % i, (T,), F32) for i in range(2)]
        qT = P.alloc("qT", (16, T), BF16)
        oT = P.alloc("oT", (16, T), BF16)
        wg = P.alloc("wg", (16, 48), BF16)
        sgT = P.alloc("sgT", (T,), F32)
        Sel = P.alloc("Sel", (48, 128), F32)
        Em = P.alloc("Em", (L,), BF16)
        cover = P.alloc("cover", (NIC, 64), F32)
        onesT = P.alloc("onesT", (T,), BF16)
        NWC = 512 // 128 + NTC
        wmask = P.alloc("wmask", (NWC, T), BF16)
        cmask = P.alloc("cmask", (NIC, T), BF16)
        kw_s = P.alloc("kw_s", (NWC * 128,), BF16)
        vw_s = P.alloc("vw_s", (NWC, 128), BF16)
        ks_s = P.alloc("ks_s", (L,), BF16)
        vs_s = P.alloc("vs_s", (L // 128, 128), BF16)
        pTt = [P.alloc("pT%d" % i, (T,), BF16) for i in range(3)]
        pmt = [P.alloc("pm%d" % i, (T,), BF16) for i in range(3)]
        mskt = [P.alloc("msk%d" % i, (T,), BF16) for i in range(2)]
        pcs = P.alloc("pcs", (NIC, T), F32)
        pcn = P.alloc("pcn", (T,), F32)
        rden = P.alloc("rden", (T,), F32)
        t1 = P.alloc("t1", (T,), F32)
        t2 = P.alloc("t2", (T,), F32)
        oacc = P.alloc("oacc", (T,), F32)
        imp = P.alloc("imp", (64,), F32)
        s1 = P.alloc("s1", (64,), F32)
        s2 = P.alloc("s2", (64,), F32)
        s3 = P.alloc("s3", (64,), F32)
        m8 = P.alloc("m8", (16,), F32)
        selb = P.alloc("selb", (64,), BF16)
        selT = P.alloc("selT", (4, T), BF16)
        P.dma("sync", wg.ap, win[:, :, 5120:5168], reads=[wib], writes=[wg])
        P.op("gpsimd", lambda e: e.memset(onesT.ap, 1.0), writes=[onesT])
        P.op("gpsimd", lambda e: e.memset(Sel.ap, 1.0), writes=[Sel])
        P.op("gpsimd", lambda e: e.affine_select(out=Sel.ap[0:48], in_=Sel.ap[0:48], pattern=[[1, 48], [0, 128]],
                                                 compare_op=ALU.is_equal, fill=fillreg(e, 0.0), base=0, channel_multiplier=-1),
             reads=[Sel], writes=[Sel])
        P.op("gpsimd", lambda e: e.memset(Em.ap, 1.0), writes=[Em])
        P.op("gpsimd", lambda e: e.affine_select(out=Em.ap[0:NSL], in_=Em.ap[0:NSL], pattern=[[1, L]], compare_op=ALU.is_ge,
                                                 fill=fillreg(e, 0.0), base=0, channel_multiplier=-64), reads=[Em], writes=[Em])
        P.op("gpsimd", lambda e: e.affine_select(out=Em.ap[0:NSL], in_=Em.ap[0:NSL], pattern=[[-1, L]], compare_op=ALU.is_ge,
                                                 fill=fillreg(e, 0.0), base=63, channel_multiplier=64), reads=[Em], writes=[Em])
        P.op("gpsimd", lambda e: e.memset(cover.ap, 1.0), writes=[cover])
        for ic in range(NIC):
            P.op("gpsimd", lambda e, ic=ic: e.affine_select(out=cover[:, ic, :], in_=cover[:, ic, :], pattern=[[-64, 64]],
                                                            compare_op=ALU.is_ge, fill=fillreg(e, 0.0), base=16 * 128 * ic + 31,
                                                            channel_multiplier=16), reads=[cover], writes=[cover])
            P.op("gpsimd", lambda e, ic=ic: e.affine_select(out=cover[:, ic, :], in_=cover[:, ic, :], pattern=[[64, 64]],
                                                            compare_op=ALU.is_ge, fill=fillreg(e, 0.0), base=63 - 16 * 128 * ic,
                                                            channel_multiplier=-16), reads=[cover], writes=[cover])
        for o in range(NWC):
            P.op("gpsimd", lambda e, o=o: e.affine_select(out=wmask[:, o, :], in_=onesT.ap, pattern=[[1, T]],
                                                          compare_op=ALU.is_ge, fill=fillreg(e, 0.0), base=512 - 128 * o,
                                                          channel_multiplier=-1), reads=[onesT], writes=[wmask])
            P.op("gpsimd", lambda e, o=o: e.affine_select(out=wmask[:, o, :], in_=wmask[:, o, :], pattern=[[-1, T]],
                                                          compare_op=ALU.is_gt, fill=fillreg(e, 0.0), base=128 * o,
                                                          channel_multiplier=1), reads=[wmask], writes=[wmask])
        R = Rot(P, wsl, banks=(5, 6, 7))
        tiny = 1e-30

        def finish_branch(hq, br, den, o, first, last):
            P.op("vector", lambda e: e.tensor_scalar(out=rden.ap, in0=den[:, 0:T], scalar1=tiny, scalar2=None, op0=ALU.add),
                 reads=[den], writes=[rden])
            P.op("vector", lambda e: e.reciprocal(out=rden.ap, in_=rden.ap), reads=[rden], writes=[rden])
            gb = R.bank()
            col = hq * 3 + br
            P.op("tensor", lambda e: e.matmul(gb[:, 0:T], lhsT=Sel[0:48, col, :], rhs=sgT[0:48, :], start=True, stop=True),
                 reads=[Sel, sgT], writes=[gb])
            P.op("vector", lambda e: e.tensor_tensor(out=t1.ap, in0=rden.ap, in1=gb[:, 0:T], op=ALU.mult), reads=[rden, gb],
                 writes=[t1])
            if first:
                P.op("vector", lambda e: e.tensor_tensor(out=oacc.ap, in0=t1.ap, in1=o[:, 0:T], op=ALU.mult), reads=[t1, o],
                     writes=[oacc])
            else:
                P.op("vector", lambda e: e.tensor_tensor(out=t2.ap, in0=t1.ap, in1=o[:, 0:T], op=ALU.mult), reads=[t1, o],
                     writes=[t2])
                if last:
                    P.op("vector", lambda e: e.tensor_tensor(out=oT[:, hq, :], in0=oacc.ap, in1=t2.ap, op=ALU.add),
                         reads=[oacc, t2], writes=[oT.s(hq)])
                else:
                    P.op("vector", lambda e: e.tensor_tensor(out=oacc.ap, in0=oacc.ap, in1=t2.ap, op=ALU.add),
                         reads=[oacc, t2], writes=[oacc])

        cnt = [0]

        def attend(hq, keyT_ap, keyT_dep, val_ap_fn, val_dep, mask_ap, mask_dep, den, o, first, last):
            r = cnt[0] % 3
            cnt[0] += 1
            ps = R.bank()
            pt, pm = pTt[r], pmt[r]
            P.op("tensor", lambda e: e.matmul(ps[:, 0:T], lhsT=keyT_ap, rhs=qT[:, hq, :], start=True, stop=True),
                 reads=[keyT_dep, qT.s(hq)], writes=[ps])
            P.op("scalar", lambda e: e.activation(out=pt.ap, in_=ps[:, 0:T], func=AF.Exp, scale=scale), reads=[ps], writes=[pt])
            P.op("vector", lambda e: e.tensor_tensor(out=pm.ap, in0=pt.ap, in1=mask_ap, op=ALU.mult), reads=[pt, mask_dep],
                 writes=[pm])
            P.op("tensor", lambda e: e.matmul(den[:, 0:T], lhsT=C.ones_b.ap, rhs=pm.ap, start=first, stop=last),
                 reads=[pm, C.ones_b], writes=[den])
            P.op("tensor", lambda e: e.matmul(o[:, 0:T], lhsT=val_ap_fn, rhs=pm.ap, start=first, stop=last),
                 reads=[pm, val_dep], writes=[o])
            return pm

        for ti in range(nt):
            t0 = ti * T
            pre_norm(P, C, x_in, xin_buf, t0, T, O["pre%d_0" % li], big, hT, sq, rstd, P.psum[0])

            def put_q(j, ps):
                P.op("scalar", lambda e: e.activation(out=qT[:, j, :], in_=ps[:, 0:T], func=AF.Copy), reads=[ps],
                     writes=[qT.s(j)])

            gemm_fm(P, C, R, hT, 16, win, wib, 0, 16, T, put_q)
            ps = R.bank()
            for kc in range(16):
                P.op("tensor", lambda e, kc=kc, ps=ps: e.matmul(ps[0:48, 0:T], lhsT=wg[:, kc, :], rhs=hT[:, kc, :],
                                                              start=(kc == 0), stop=(kc == 15)), reads=[wg, hT.s(kc)],
                     writes=[ps])
            P.op("scalar", lambda e, ps=ps: e.activation(out=sgT[0:48, :], in_=ps[0:48, 0:T], func=AF.Sigmoid), reads=[ps],
                 writes=[sgT])
            nic_t = 0
            for ic in range(NIC):
                if 16 * 128 * ic + 31 <= t0 + T - 1:
                    nic_t = ic + 1
                    P.op("gpsimd", lambda e, ic=ic, t0=t0: e.affine_select(out=cmask[:, ic, :], in_=onesT.ap, pattern=[[1, T]],
                                                                    compare_op=ALU.is_ge, fill=fillreg(e, 0.0),
                                                                    base=t0 - 31 - 16 * 128 * ic, channel_multiplier=-16),
                         reads=[onesT], writes=[cmask.s(ic)])
            nkc = (t0 + T) // 128
            w_lo = max(0, (t0 - 512) // 128)
            for g in range(4):
                nw = nkc - w_lo
                P.dma("sync", kw_s[:, 0:nw * 128], kvT_d[3, g, :, w_lo * 128:nkc * 128], reads=[kvT_b], writes=[kw_s])
                P.dma("sync", vw_s[:, 0:nw, :], vtm_d[1, w_lo * 128:nkc * 128, g * 128:(g + 1) * 128].rearrange(
                    "(c p) d -> p c d", p=128), reads=[vtm_b], writes=[vw_s])
                P.dma("sync", ks_s[:, 0:nkc * 128], kvT_d[2, g, :, 0:nkc * 128], reads=[kvT_b], writes=[ks_s])
                P.dma("sync", vs_s[:, 0:nkc, :], vtm_d[0, 0:nkc * 128, g * 128:(g + 1) * 128].rearrange(
                    "(c p) d -> p c d", p=128), reads=[vtm_b], writes=[vs_s])
                for j in range(4):
                    hq = 4 * g + j
                    den, o = P.psum[1 + hq % 2], P.psum[3 + hq % 2]
                    if nic_t == 0:
                        P.op("gpsimd", lambda e, hq=hq: e.memset(big[:, hq, :], 0.0), writes=[big.s(hq)])
                        continue
                    pms = []
                    for ic in range(nic_t):
                        pm = attend(hq, kcT[:, g, ic * 128:(ic + 1) * 128], kcT, vc_tm[:, ic, g, :], vc_tm, cmask[:, ic, :],
                                    cmask.s(ic), den, o, ic == 0, ic == nic_t - 1)
                        pms.append(pm)
                    finish_branch(hq, 0, den, o, True, False)
                    if ti == 0 and hq == 0:
                        dbg_dump(P, C, "cmask", cmask, (NIC, T), BF16)
                        dbg_dump(P, C, "pm0", pms[0], (T,), BF16)
                        dbg_dump(P, C, "rden0", rden, (T,), F32)
                        dbg_dump(P, C, "t10", t1, (T,), F32)
                        dbg_dump(P, C, "oacc0", oacc, (T,), F32)
                    for ic in range(nic_t):
                        if j == 0:
                            P.op("vector", lambda e, ic=ic, pm=pms[ic]: e.tensor_tensor(out=pcs[:, ic, :], in0=pm.ap, in1=rden.ap,
                                                                                     op=ALU.mult), reads=[pm, rden],
                                 writes=[pcs.s(ic)])
                        else:
                            P.op("vector", lambda e, ic=ic, pm=pms[ic]: e.tensor_tensor(out=pcn.ap, in0=pm.ap, in1=rden.ap,
                                                                                     op=ALU.mult), reads=[pm, rden], writes=[pcn])
                            P.op("vector", lambda e, ic=ic: e.tensor_tensor(out=pcs[:, ic, :], in0=pcs[:, ic, :], in1=pcn.ap,
                                                                            op=ALU.add), reads=[pcs.s(ic), pcn],
                                 writes=[pcs.s(ic)])
                    P.op("gpsimd", lambda e, hq=hq: e.tensor_copy(out=big[:, hq, :], in_=oacc.ap), reads=[oacc],
                         writes=[big.s(hq)])
                for tc in range(NTC):
                    ps = R.bank()
                    if nic_t == 0:
                        P.op("gpsimd", lambda e: e.memset(imp.ap, 0.0), writes=[imp])
                    else:
                        for ic in range(nic_t):
                            P.op("tensor", lambda e, ps=ps, ic=ic, tc=tc: e.matmul(
                                ps[:, 0:64], lhsT=pcs[:, ic, tc * 128:(tc + 1) * 128], rhs=cover[:, ic, :], start=(ic == 0),
                                stop=(ic == nic_t - 1)), reads=[pcs.s(ic), cover], writes=[ps])
                        P.op("scalar", lambda e, ps=ps: e.activation(out=imp.ap, in_=ps[:, 0:64], func=AF.Copy), reads=[ps],
                             writes=[imp])
                    tb = t0 + tc * 128
                    P.op("gpsimd", lambda e, tb=tb: e.affine_select(out=s1.ap, in_=imp.ap, pattern=[[-64, 64]],
                                                                    compare_op=ALU.is_ge, fill=fillreg(e, 100.0), base=tb - 192,
                                                                    channel_multiplier=1), reads=[imp], writes=[s1])
                    P.op("gpsimd", lambda e, tb=tb: e.affine_select(out=s2.ap, in_=s1.ap, pattern=[[-64, 64]],
                                                                    compare_op=ALU.is_ge, fill=fillreg(e, -1.0), base=tb,
                                                                    channel_multiplier=1), reads=[s1], writes=[s2])
                    P.op("gpsimd", lambda e: e.memset(s2[:, 0:1], 100.0), reads=[s2], writes=[s2])
                    P.op("vector", lambda e: e.max(out=m8[:, 0:8], in_=s2[:, 0:NSL]), reads=[s2], writes=[m8])
                    P.op("vector", lambda e: e.match_replace(out=s3[:, 0:NSL], in_to_replace=m8[:, 0:8], in_values=s2[:, 0:NSL],
                                                             imm_value=-2.0), reads=[s2, m8], writes=[s3])
                    P.op("vector", lambda e: e.max(out=m8[:, 8:16], in_=s3[:, 0:NSL]), reads=[s3, m8], writes=[m8])
                    P.op("vector", lambda e: e.tensor_scalar(out=selb.ap, in0=s2.ap, scalar1=m8[:, 15:16], scalar2=None,
                                                             op0=ALU.is_ge), reads=[s2, m8], writes=[selb])
                    ps2 = R.bank()
                    psb = ps2.ap.bitcast(BF16)
                    P.op("tensor", lambda e, psb=psb: e.transpose(out=psb[0:64, 0:128], in_=selb.ap, identity=C.ident.ap),
                         reads=[selb, C.ident], writes=[ps2])
                    P.op("scalar", lambda e, psb=psb, tc=tc, g=g: e.activation(out=selT[0:64, g, tc * 128:(tc + 1) * 128],
                                                                              in_=psb[0:64, 0:128], func=AF.Copy), reads=[ps2],
                         writes=[selT])
                def sel_mask(kc_, g=g, t0=t0):
                    ps = R.bank()
                    m = mskt[kc_ % 2]
                    P.op("tensor", lambda e: e.matmul(ps[:, 0:T], lhsT=Em[0:NSL, kc_ * 128:(kc_ + 1) * 128], rhs=selT[0:NSL, g, :],
                                                      start=True, stop=True), reads=[Em, selT], writes=[ps])
                    P.op("scalar", lambda e: e.activation(out=m.ap, in_=ps[:, 0:T], func=AF.Copy), reads=[ps], writes=[m])
                    if kc_ * 128 + 127 > t0:
                        P.op("gpsimd", lambda e: e.affine_select(out=m.ap, in_=m.ap, pattern=[[1, T]], compare_op=ALU.is_ge,
                                                                 fill=fillreg(e, 0.0), base=t0 - kc_ * 128, channel_multiplier=-1),
                             reads=[m], writes=[m])
                    return m

                dens = [P.psum[1], P.psum[2]]
                os_ = [P.psum[3], P.psum[4]]
                for jp in range(2):
                    for kc_ in range(nkc):
                        m = sel_mask(kc_)
                        for jj in range(2):
                            hq = 4 * g + 2 * jp + jj
                            attend(hq, ks_s[:, kc_ * 128:(kc_ + 1) * 128], ks_s, vs_s[:, kc_, :], vs_s, m.ap, m, dens[jj],
                                   os_[jj], kc_ == 0, kc_ == nkc - 1)
                    for jj in range(2):
                        hq = 4 * g + 2 * jp + jj
                        P.op("gpsimd", lambda e, hq=hq: e.tensor_copy(out=oacc.ap, in_=big[:, hq, :]), reads=[big.s(hq)],
                             writes=[oacc])
                        finish_branch(hq, 1, dens[jj], os_[jj], False, False)
                        den, o = dens[jj], os_[jj]
                        for wi in range(nw):
                            kc_ = w_lo + wi
                            oidx = kc_ - (t0 - 512) // 128
                            attend(hq, kw_s[:, wi * 128:(wi + 1) * 128], kw_s, vw_s[:, wi, :], vw_s, wmask[:, oidx, :], wmask, den, o,
                                   wi == 0, wi == nw - 1)
                        finish_branch(hq, 2, den, o, False, True)
            if ti == 0:
                dbg_dump(P, C, "oT", oT, (16, T), BF16)
                dbg_dump(P, C, "cmp", big, (16, T), F32)
                dbg_dump(P, C, "qT", qT, (16, T), BF16)
                dbg_dump(P, C, "sgT", sgT, (T,), F32)
                dbg_dump(P, C, "kcT", kcT, (4, NIC * 128), BF16)
                dbg_dump(P, C, "vc", vc_tm, (NIC, 4, 128), BF16)
                dbg_dump(P, C, "selT", selT, (4, T), BF16)
            out_proj(P, C, oT, 16, wout, C.wbuf["nsa_out"], wsl, R, big, sq, rstd, T)
            post_residual(P, C, big, rstd, x_in, xin_buf, x_out, xout_buf, t0, T, O["post%d_0" % li], xc)
    phase_c()


STAGES = {"nsa": nsa_stage, "ssd": ssd_stage, "ffn": ffn_stage, "xa": xa_stage, "pool": pool_stage, "sgu": sgu_stage}


def build(L, plan, wshapes=None, dbg=False):
    nc = bass.Bass("TRN2", target_bir_lowering=False)
    _FILL.clear()
    C = Ctx()
    offs, ncols, _ = pack_cols(None)
    C.off = offs
    xT = nc.dram_tensor("xT", [D, L], F32, kind="ExternalInput")
    cols_d = nc.dram_tensor("cols", [128, ncols], F32, kind="ExternalInput")
    yT = nc.dram_tensor("yT", [D, L], F32, kind="ExternalOutput")
    xs = nc.dram_tensor("xs", [D, L], F32, kind="Internal")
    xs1 = nc.dram_tensor("xs1", [D, L], F32, kind="Internal")
    roffs, nrows, _ = pack_rows(None)
    C.roff = roffs
    rows_d = nc.dram_tensor("rows", [nrows], F32, kind="ExternalInput")
    C.rows = rows_d.ap()
    C.sgu_wsT = nc.dram_tensor("sgu_wsT", [128, 8, 128], F32, kind="ExternalInput").ap()
    C.nsa_posT = nc.dram_tensor("nsa_posT", [128, 2, 32], F32, kind="ExternalInput").ap()
    C.nc = nc
    C.dbg = dbg
    memT = nc.dram_tensor("memT", [D, 256], F32, kind="ExternalInput")
    C.memT, C.memT_buf = memT.ap(), Buf("memT")
    needed = set()
    for stg in plan:
        needed |= set(STAGE_W[stg[0]])
    LAST_NEEDED[:] = sorted(needed)
    C.wb, C.wbuf, C.wcast = {}, {}, {}
    for name in sorted(needed):
        key, shp = WEIGHTS[name]
        src = nc.dram_tensor(key, list(shp), F32, kind="ExternalInput")
        dst = nc.dram_tensor(name + "_bf", list(shp), BF16, kind="Internal")
        C.wb[name] = dst.ap()
        C.wbuf[name] = Buf(name)
        C.wcast[name] = (src, dst, C.wbuf[name])
    P = Prog(nc)
    with ExitStack() as st:
        P.arena_words = 47000
        P.arena = st.enter_context(nc.sbuf_tensor("arena", [128, P.arena_words], F32))
        P.psum = []
        for i in range(8):
            t = st.enter_context(nc.psum_tensor("ps%d" % i, [128, 512], F32))
            P.psum.append(Tile(t[:, :], Buf("ps%d" % i)))
        setup_persistent(P, C, cols_d.ap(), ncols)
        cast_weights(P, C)
        P.barrier()
        P.arena_off = P.arena_base
        aps = {"xT": xT.ap(), "xs": xs.ap(), "xs1": xs1.ap(), "yT": yT.ap()}
        for stg in plan:
            kind, li, src, dst = stg
            STAGES[kind](P, C, li, aps[src], Buf(src), aps[dst], Buf(dst), L)
            P.barrier()
            P.arena_off = P.arena_base
        P.emit(st)
    return nc


FULL_PLAN = []
_kinds = ["ssd", "nsa", "sgu", "pool"]
_cur = "xT"
for _i in range(DEPTH):
    for _k in (_kinds[_i], "xa", "ffn"):
        _last = (_i == DEPTH - 1 and _k == "ffn")
        _dst = "yT" if _last else ("xs" if _cur != "xs" else "xs1")
        FULL_PLAN.append((_k, _i, _cur, _dst))
        _cur = _dst

_NC_CACHE = {}


def kernel(**inputs):
    inp = {k: np.asarray(v) for k, v in inputs.items()}
    B, L, _ = inp["x"].shape
    if L not in _NC_CACHE:
        _NC_CACHE[L] = (build(L, FULL_PLAN), list(LAST_NEEDED))
    nc, needed = _NC_CACHE[L]
    in_maps = [make_in_map(inp, b, L, needed) for b in range(B)]
    res = run_bass_kernel_spmd(nc, in_maps, core_ids=list(range(B)))
    out = np.stack([np.asarray(res.results[b]["yT"]).T for b in range(B)], axis=0)
    return np.ascontiguousarray(out.astype(np.float32))
```

```python
from contextlib import ExitStack
import numpy as np
import concourse.bass as bass
import concourse.mybir as mybir
from concourse.bass_utils import run_bass_kernel_spmd

F32 = mybir.dt.float32
BF16 = mybir.dt.bfloat16
AF = mybir.ActivationFunctionType
ALU = mybir.AluOpType

D = 2048
NCH = 16
DEPTH = 4
FFN_H = 5632
FFN_HC = 44
EPS = 1e-6
ENGS = ("tensor", "vector", "scalar", "gpsimd", "sync")
NDMA = 8


_FILL = {}


def fillreg(e, v):
    k = (id(e), float(v))
    if k not in _FILL:
        _FILL[k] = e.to_reg(float(v))
    return _FILL[k]


class Buf:
    __slots__ = ("name", "recs")

    def __init__(self, name):
        self.name = name
        self.recs = []


def _norm(x):
    if isinstance(x, Tile):
        return (x.buf, x.sub)
    return x if isinstance(x, tuple) else (x, None)


class Tile:
    __slots__ = ("ap", "buf", "sub")

    def __init__(self, ap, buf, sub=None):
        self.ap = ap
        self.buf = buf
        self.sub = sub

    def __getitem__(self, k):
        return self.ap[k]

    def s(self, sub):
        return (self.buf, sub)


class Prog:
    def __init__(self, nc):
        self.nc = nc
        self.code = {e: [] for e in ENGS}
        self.cnt = {e: 0 for e in ENGS}
        self.known = {e: {} for e in ENGS}
        self.dma_cnt = {}
        self.dma_rr = {e: 0 for e in ENGS}
        self.semkeys = list(ENGS)
        for e in ("sync", "gpsimd", "scalar"):
            for j in range(NDMA):
                self.semkeys.append("d_%s_%d" % (e, j))
        self.arena_off = 0
        self.arena = None
        self.psum = None

    def _waits(self, eng, reads, writes):
        toks = {}
        for (buf, sub), is_w in [(r, False) for r in reads] + [(w, True) for w in writes]:
            for rsub, rw, sk, val in buf.recs:
                if (rsub is None or sub is None or rsub == sub) and (is_w or rw):
                    if toks.get(sk, 0) < val:
                        toks[sk] = val
        kn = self.known[eng]
        for sk, val in toks.items():
            if sk == "tensor" and eng == "tensor":
                continue
            if kn.get(sk, 0) >= val:
                continue
            kn[sk] = val
            self.code[eng].append(("w", sk, val))

    def _record(self, tok, reads, writes):
        sk, val = tok
        for buf, sub in writes:
            if sub is None:
                buf.recs = [[None, True, sk, val]]
            else:
                buf.recs = [r for r in buf.recs if r[0] != sub]
                buf.recs.append([sub, True, sk, val])
        for buf, sub in reads:
            for r in buf.recs:
                if r[0] == sub and (not r[1]) and r[2] == sk:
                    r[3] = val
                    break
            else:
                buf.recs.append([sub, False, sk, val])

    def op(self, eng, fn, reads=(), writes=()):
        reads = [_norm(r) for r in reads]
        writes = [_norm(w) for w in writes]
        self._waits(eng, reads, writes)
        self.cnt[eng] += 1
        self.code[eng].append(("o", fn, eng, 1))
        self._record((eng, self.cnt[eng]), reads, writes)

    def dma(self, eng, out, in_, reads=(), writes=(), **kw):
        reads = [_norm(r) for r in reads]
        writes = [_norm(w) for w in writes]
        self._waits(eng, reads, writes)
        j = self.dma_rr[eng]
        self.dma_rr[eng] = (j + 1) % NDMA
        sk = "d_%s_%d" % (eng, j)
        prev = self.dma_cnt.get(sk, 0)
        if prev and self.known[eng].get(sk, 0) < prev:
            self.known[eng][sk] = prev
            self.code[eng].append(("w", sk, prev))
        self.dma_cnt[sk] = prev + 16
        self.code[eng].append(("o", lambda e: e.dma_start(out=out, in_=in_, **kw), sk, 16))
        self._record((sk, prev + 16), reads, writes)

    def barrier(self):
        for e in ENGS:
            kn = self.known[e]
            for o in ENGS:
                if o != e and self.cnt[o] > kn.get(o, 0):
                    kn[o] = self.cnt[o]
                    self.code[e].append(("w", o, self.cnt[o]))
            for sk, v in self.dma_cnt.items():
                if v > kn.get(sk, 0):
                    kn[sk] = v
                    self.code[e].append(("w", sk, v))
        self.arena_off = 0

    def alloc(self, name, shape_free, dtype):
        n = 1
        for s in shape_free:
            n *= s
        nbytes = n * (2 if dtype == BF16 else 4)
        nw = (nbytes + 3) // 4
        nw = (nw + 7) // 8 * 8
        off = self.arena_off
        assert off + nw <= self.arena_words, ("arena overflow", name, off, nw)
        self.arena_off = off + nw
        ap = self.arena[:, off:off + nw]
        if dtype == BF16:
            ap = ap.bitcast(BF16)[:, 0:n]
        else:
            ap = ap[:, 0:n]
        if len(shape_free) == 2:
            ap = ap.rearrange("p (a b) -> p a b", a=shape_free[0])
        elif len(shape_free) == 3:
            ap = ap.rearrange("p (a b c) -> p a b c", a=shape_free[0], b=shape_free[1])
        return Tile(ap, Buf(name))

    def emit(self, st):
        nc = self.nc
        sems = {k: st.enter_context(nc.semaphore(k)) for k in self.semkeys}
        self.barrier()
        block = st.enter_context(nc.Block())
        for eng in ENGS:
            code = self.code[eng]

            def body(e, code=code):
                for it in code:
                    if it[0] == "w":
                        e.wait_ge(sems[it[1]], it[2])
                    else:
                        it[1](e).then_inc(sems[it[2]], it[3])

            getattr(block, eng)(body)


def colT(v):
    v = np.asarray(v, np.float32).reshape(-1)
    assert v.size % 128 == 0
    return np.ascontiguousarray(v.reshape(-1, 128).T)


def pack_cols(inp):
    offs = {}
    parts = []
    pos = [0]

    def add(name, n, arr_fn):
        offs[name] = pos[0]
        pos[0] += n
        if inp is not None:
            a = arr_fn()
            assert a.shape == (128, n), (name, a.shape, n)
            parts.append(a)

    for i in range(DEPTH):
        for j in range(3):
            add("pre%d_%d" % (i, j), 16, lambda: colT(inp["norm_pre"][i, j]))
            add("post%d_%d" % (i, j), 16, lambda: colT(inp["norm_post"][i, j]))
        add("nmem%d" % i, 16, lambda: colT(inp["norm_mem"][i]))
        for k in range(3):
            add("fcw%d_%d" % (i, k), 88, lambda: colT(inp["ffn_conv_w"][i, k]))
        add("fcb%d" % i, 88, lambda: colT(inp["ffn_conv_b"][i]))
    add("pscale", 16, lambda: colT(inp["pool_scale"][0]))
    add("sgu_binu", 32, lambda: colT(inp["sgu_b_in"][0][:4096]))
    add("ssd_cw", 4 * 48, lambda: np.concatenate([colT(inp["ssd_conv_w"][0, k]) for k in range(4)], axis=1))
    add("ssd_cb", 48, lambda: colT(inp["ssd_conv_b"][0]))
    add("ssd_ng", 32, lambda: colT(inp["ssd_norm_g"][0]))
    add("ssd_dcol", 32, lambda: colT(np.repeat(inp["ssd_d"][0], 64)))
    arr = np.ascontiguousarray(np.concatenate(parts, axis=1)) if inp is not None else None
    return offs, pos[0], arr


def pack_rows(inp):
    offs = {}
    parts = []
    pos = [0]

    def add(name, n, fn):
        offs[name] = pos[0]
        pos[0] += n
        if inp is not None:
            a = np.asarray(fn(), np.float32).reshape(-1)
            assert a.size == n, (name, a.size, n)
            parts.append(a)

    add("sgu_binv", 4096, lambda: inp["sgu_b_in"][0][4096:])
    add("sgu_lng", 4096, lambda: inp["sgu_ln_g"][0])
    add("sgu_bsp", 1024, lambda: inp["sgu_b_spatial"][0])
    add("ssd_dtb", 64, lambda: inp["ssd_dt_bias"][0])
    add("ssd_alog", 64, lambda: inp["ssd_a_log"][0])
    arr = np.ascontiguousarray(np.concatenate(parts)) if inp is not None else None
    return offs, pos[0], arr


WEIGHTS = {
    "ffn_up": ("ffn_w_up", (DEPTH, D, 2 * FFN_H)),
    "ffn_dn": ("ffn_w_down", (DEPTH, FFN_H, D)),
    "xa_q": ("xa_w_q", (DEPTH, D, 512)),
    "xa_kv": ("xa_w_kv", (DEPTH, D, 1024)),
    "xa_o": ("xa_w_o", (DEPTH, 512, D)),
    "ssd_in": ("ssd_w_in", (1, D, 10304)),
    "ssd_out": ("ssd_w_out", (1, 4096, D)),
    "nsa_in": ("nsa_w_in", (1, D, 5168)),
    "nsa_w1": ("nsa_cmp_w1", (1, 2, 4096, 256)),
    "nsa_w2": ("nsa_cmp_w2", (1, 2, 256, 128)),
    "nsa_out": ("nsa_w_out", (1, D, D)),
    "sgu_in": ("sgu_w_in", (1, D, 8192)),
    "sgu_out": ("sgu_w_out", (1, 4096, D)),
    "pool_in": ("pool_w_in", (1, D, D)),
    "pool_grp": ("pool_w_group", (1, 4, 512, 512)),
    "pool_out": ("pool_w_out", (1, D, D)),
}
STAGE_W = {
    "ffn": ["ffn_up", "ffn_dn"],
    "xa": ["xa_q", "xa_kv", "xa_o"],
    "ssd": ["ssd_in", "ssd_out"],
    "nsa": ["nsa_in", "nsa_w1", "nsa_w2", "nsa_out"],
    "sgu": ["sgu_in", "sgu_out"],
    "pool": ["pool_in", "pool_grp", "pool_out"],
}
LAST_NEEDED = []


def make_in_map(inp, b, L, nc_needed):
    _, _, cols = pack_cols(inp)
    _, _, rows = pack_rows(inp)
    m = {"xT": np.ascontiguousarray(inp["x"][b, :L].T), "memT": np.ascontiguousarray(inp["mem"][b].T), "cols": cols,
         "rows": rows,
         "sgu_wsT": np.ascontiguousarray(inp["sgu_w_spatial"][0].transpose(2, 0, 1)),
         "nsa_posT": np.ascontiguousarray(inp["nsa_cmp_pos"][0].transpose(2, 0, 1))}
    for name in nc_needed:
        m[WEIGHTS[name][0]] = np.ascontiguousarray(inp[WEIGHTS[name][0]])
    return m


def xa_stage(P, C, li, x_in, xin_buf, x_out, xout_buf, L):
    T = min(512, L)
    nt = L // T
    O = C.off
    hT = P.alloc("hT", (16, T), BF16)
    big = P.alloc("big", (16, T), F32)
    wsl = [P.alloc("w%d" % i, (8192,), BF16) for i in range(3)]
    sq = P.alloc("sq", (2, T), F32)
    rstd = P.alloc("rstd", (T,), F32)
    xc = [P.alloc("xc%d" % i, (T,), F32) for i in range(2)]
    qT = P.alloc("qT", (4, T), BF16)
    oT = P.alloc("oT", (4, T), BF16)
    KT = P.alloc("KT", (4, 256), BF16)
    V = P.alloc("V", (2, 512), BF16)
    pT = [P.alloc("pT%d" % i, (2, T), BF16) for i in range(2)]
    rden = P.alloc("rden", (T,), F32)
    wq = C.wb["xa_q"][li].rearrange("(kc p) f -> p kc f", p=128)
    wkv = C.wb["xa_kv"][li].rearrange("(kc p) f -> p kc f", p=128)
    wo = C.wb["xa_o"][li].rearrange("(kc p) f -> p kc f", p=128)
    R = Rot(P, wsl, banks=(1, 2, 3))
    scale = 128.0 ** -0.5
    pre_norm(P, C, C.memT, C.memT_buf, 0, 256, O["nmem%d" % li], big, hT, sq, rstd, P.psum[0])

    def put_kt(j, ps):
        P.op("scalar", lambda e: e.activation(out=KT[:, j, :], in_=ps[:, 0:256], func=AF.Copy), reads=[ps],
             writes=[KT.s(j)])

    gemm_fm(P, C, R, hT, 16, wkv, C.wbuf["xa_kv"], 0, 4, 256, put_kt)
    slot = R.slot()
    sv = slot.ap.rearrange("p (k f) -> p k f", k=16)
    for a, b in ((0, 8), (8, 16)):
        P.dma("sync", sv[:, a:b, :], wkv[:, a:b, 512:1024], reads=[C.wbuf["xa_kv"]], writes=[slot])
    for mc in range(2):
        ps = R.bank()
        for kc in range(16):
            P.op("tensor", lambda e, kc=kc, ps=ps, mc=mc: e.matmul(ps[:, 0:512], lhsT=hT[:, kc, mc * 128:(mc + 1) * 128],
                                                                 rhs=sv[:, kc, :], start=(kc == 0), stop=(kc == 15)),
                 reads=[slot, hT.s(kc)], writes=[ps])
        P.op("scalar", lambda e, ps=ps, mc=mc: e.activation(out=V[:, mc, :], in_=ps[:, 0:512], func=AF.Copy),
             reads=[ps], writes=[V.s(mc)])
    for ti in range(nt):
        t0 = ti * T
        pre_norm(P, C, x_in, xin_buf, t0, T, O["pre%d_1" % li], big, hT, sq, rstd, P.psum[0])

        def put_q(j, ps):
            P.op("scalar", lambda e: e.activation(out=qT[:, j, 0:T], in_=ps[:, 0:T], func=AF.Copy), reads=[ps],
                 writes=[qT.s(j)])

        gemm_fm(P, C, R, hT, 16, wq, C.wbuf["xa_q"], 0, 4, T, put_q)
        def xa_p1(h):
            pt = pT[h % 2]
            for mc in range(2):
                ps = R.bank()
                P.op("tensor", lambda e, ps=ps, mc=mc, h=h: e.matmul(ps[:, 0:T], lhsT=KT[:, h, mc * 128:(mc + 1) * 128],
                                                                   rhs=qT[:, h, 0:T], start=True, stop=True),
                     reads=[KT.s(h), qT.s(h)], writes=[ps])
                P.op("scalar", lambda e, ps=ps, mc=mc, pt=pt: e.activation(out=pt[:, mc, 0:T], in_=ps[:, 0:T], func=AF.Exp,
                                                                          scale=scale), reads=[ps], writes=[pt.s(mc)])

        def xa_p2(h):
            den = P.psum[4 + h % 2]
            o = P.psum[6 + h % 2]
            pt = pT[h % 2]
            for mc in range(2):
                P.op("tensor", lambda e, mc=mc, pt=pt, den=den: e.matmul(den[:, 0:T], lhsT=C.ones_b.ap, rhs=pt[:, mc, 0:T],
                                                                       start=(mc == 0), stop=(mc == 1)),
                     reads=[pt.s(mc), C.ones_b], writes=[den])
            for mc in range(2):
                P.op("tensor", lambda e, mc=mc, pt=pt, o=o, h=h: e.matmul(o[:, 0:T], lhsT=V[:, mc, h * 128:(h + 1) * 128],
                                                                        rhs=pt[:, mc, 0:T], start=(mc == 0), stop=(mc == 1)),
                     reads=[pt.s(mc), V.s(mc)], writes=[o])
            P.op("vector", lambda e, den=den: e.reciprocal(out=rden[:, 0:T], in_=den[:, 0:T]), reads=[den], writes=[rden])
            P.op("vector", lambda e, o=o, h=h: e.tensor_tensor(out=oT[:, h, 0:T], in0=o[:, 0:T], in1=rden[:, 0:T],
                                                               op=ALU.mult), reads=[o, rden], writes=[oT.s(h)])

        xa_p1(0)
        for h in range(4):
            if h + 1 < 4:
                xa_p1(h + 1)
            xa_p2(h)
        out_proj(P, C, oT, 4, wo, C.wbuf["xa_o"], wsl, R, big, sq, rstd, T)
        post_residual(P, C, big, rstd, x_in, xin_buf, x_out, xout_buf, t0, T, O["post%d_1" % li], xc)


class Ctx:
    pass


def setup_persistent(P, C, cols_dram, ncols):
    C.ones_f = P.alloc("ones_f", (128,), F32)
    P.op("gpsimd", lambda e: e.memset(C.ones_f.ap, 1.0), writes=[C.ones_f])
    C.ones_b = P.alloc("ones_b", (128,), BF16)
    P.op("gpsimd", lambda e: e.memset(C.ones_b.ap, 1.0), writes=[C.ones_b])
    C.eps = P.alloc("eps", (1,), F32)
    P.op("gpsimd", lambda e: e.memset(C.eps.ap, EPS), writes=[C.eps])
    C.trile = P.alloc("trile", (128,), F32)
    P.op("gpsimd", lambda e: e.affine_select(out=C.trile.ap, in_=C.ones_f.ap, pattern=[[1, 128]], compare_op=ALU.is_ge,
                                             fill=fillreg(e, 0.0), base=0, channel_multiplier=-1), reads=[C.ones_f], writes=[C.trile])
    C.su = P.alloc("su", (128,), F32)
    P.op("gpsimd", lambda e: e.affine_select(out=C.su.ap, in_=C.ones_f.ap, pattern=[[-1, 128]], compare_op=ALU.is_gt,
                                             fill=fillreg(e, 0.0), base=0, channel_multiplier=1), reads=[C.ones_f], writes=[C.su])
    C.ident = P.alloc("ident", (128,), BF16)
    P.op("gpsimd", lambda e: e.affine_select(out=C.ident.ap, in_=C.ones_f.ap, pattern=[[1, 128]],
                                             compare_op=ALU.is_equal, fill=fillreg(e, 0.0), base=0, channel_multiplier=-1),
         reads=[C.ones_f], writes=[C.ident])
    C.cols = P.alloc("cols", (ncols,), F32)
    P.dma("sync", C.cols.ap, cols_dram, writes=[C.cols])
    P.arena_base = P.arena_off


def cast_weights(P, C):
    for name, (src, dst, dbuf) in C.wcast.items():
        n = 1
        for s in src.shape:
            n *= s
        assert n % 2048 == 0
        rows = n // 2048
        sv = src.reshape([rows, 2048]).ap()
        dv = dst.reshape([rows, 2048]).ap()
        r = 0
        while r < rows:
            rr = min(4096, rows - r)
            P.dma("gpsimd", dv[r:r + rr, :], sv[r:r + rr, :], writes=[dbuf])
            r += rr


def rstd_from_ps(P, C, rstd, ps, T, n=D):
    P.op("scalar", lambda e: e.activation(out=rstd[:, 0:T], in_=ps[:, 0:T], func=AF.Sqrt, scale=1.0 / n,
                                          bias=C.eps.ap),
         reads=[ps, C.eps], writes=[rstd])
    P.op("vector", lambda e: e.reciprocal(out=rstd[:, 0:T], in_=rstd[:, 0:T]), reads=[rstd], writes=[rstd])


def pre_norm(P, C, x_dram, xbuf, t0, T, gcol0, xt, hT, sq, rstd, ps):
    xv = x_dram.rearrange("(c p) l -> p c l", p=128)
    for hh in range(2):
        P.dma("sync", xt[:, 8 * hh:8 * hh + 8, 0:T], xv[:, 8 * hh:8 * hh + 8, t0:t0 + T], reads=[xbuf],
              writes=[xt.s(c) for c in range(8 * hh, 8 * hh + 8)])
    for c in range(NCH):
        P.op("scalar", lambda e, c=c: e.activation(out=sq[:, c % 2, 0:T], in_=xt[:, c, 0:T], func=AF.Square),
             reads=[xt.s(c)], writes=[sq.s(c % 2)])
        P.op("tensor", lambda e, c=c: e.matmul(ps[:, 0:T], lhsT=C.ones_f.ap, rhs=sq[:, c % 2, 0:T],
                                               start=(c == 0), stop=(c == NCH - 1)),
             reads=[sq.s(c % 2), C.ones_f], writes=[ps])
    rstd_from_ps(P, C, rstd, ps, T)
    for c in range(NCH):
        P.op("vector", lambda e, c=c: e.scalar_tensor_tensor(out=hT[:, c, 0:T], in0=xt[:, c, 0:T],
                                                             scalar=C.cols[:, gcol0 + c:gcol0 + c + 1],
                                                             in1=rstd[:, 0:T], op0=ALU.mult, op1=ALU.mult),
             reads=[xt.s(c), rstd, C.cols], writes=[hT.s(c)])


def post_residual(P, C, mT, rstd, x_in, xin_buf, x_out, xout_buf, t0, T, gcol0, xc):
    xv = x_in.rearrange("(c p) l -> p c l", p=128)
    ov = x_out.rearrange("(c p) l -> p c l", p=128)
    for c in range(NCH):
        xcc = xc[c % 2]
        P.dma("sync", xcc[:, 0:T], xv[:, c, t0:t0 + T], reads=[xin_buf], writes=[xcc])
        P.op("vector", lambda e, c=c: e.scalar_tensor_tensor(out=mT[:, c, 0:T], in0=mT[:, c, 0:T],
                                                             scalar=C.cols[:, gcol0 + c:gcol0 + c + 1],
                                                             in1=rstd[:, 0:T], op0=ALU.mult, op1=ALU.mult),
             reads=[mT.s(c), rstd, C.cols], writes=[mT.s(c)])
        P.op("vector", lambda e, c=c, xcc=xcc: e.tensor_tensor(out=mT[:, c, 0:T], in0=mT[:, c, 0:T], in1=xcc[:, 0:T],
                                                               op=ALU.add),
             reads=[mT.s(c), xcc], writes=[mT.s(c)])
        P.dma("gpsimd", ov[:, c, t0:t0 + T], mT[:, c, 0:T], reads=[mT.s(c)], writes=[(xout_buf, (t0, c))])


def dbg_dump(P, C, name, tile, shape, dtype):
    if not getattr(C, "dbg", False):
        return
    d = C.nc.dram_tensor("dbg_" + name, [128] + list(shape), dtype, kind="ExternalOutput").ap()
    P.dma("gpsimd", d, tile.ap if isinstance(tile, Tile) else tile, reads=[tile] if isinstance(tile, Tile) else [],
          writes=[Buf("dbg")])


class Rot:
    def __init__(self, P, wsl, banks=(1, 2, 3, 4, 5, 6, 7)):
        self.P, self.wsl, self.banks = P, wsl, banks
        self.si, self.bi = 0, 0

    def slot(self):
        s = self.wsl[self.si % len(self.wsl)]
        self.si += 1
        return s

    def bank(self):
        b = self.P.psum[self.banks[self.bi % len(self.banks)]]
        self.bi += 1
        return b


def out_proj(P, C, gT, KC, wview, wbuf, wsl, st, big, sq, rstd, T):
    R = st if isinstance(st, Rot) else None
    stats = P.psum[0]
    nd = max(1, min(NCH, 8192 // (KC * 128)))
    for d0 in range(0, NCH, nd):
        slot = R.slot()
        sv = slot.ap[:, 0:KC * nd * 128].rearrange("p (k f) -> p k f", k=KC)
        nsplit = 2 if KC >= 2 else 1
        ks = [0, KC // 2, KC] if nsplit == 2 else [0, KC]
        for a, b in zip(ks[:-1], ks[1:]):
            P.dma("sync", sv[:, a:b, :], wview[:, a:b, d0 * 128:(d0 + nd) * 128], reads=[wbuf], writes=[slot])
        for dj in range(nd):
            dc = d0 + dj
            ps = R.bank()
            for kc in range(KC):
                P.op("tensor", lambda e, kc=kc, sv=sv, ps=ps, dj=dj: e.matmul(
                    ps[:, 0:T], lhsT=sv[:, kc, dj * 128:(dj + 1) * 128], rhs=gT[:, kc, 0:T],
                    start=(kc == 0), stop=(kc == KC - 1)), reads=[slot, gT.s(kc)], writes=[ps])
            P.op("scalar", lambda e, dc=dc, ps=ps: e.activation(out=big[:, dc, 0:T], in_=ps[:, 0:T], func=AF.Copy),
                 reads=[ps], writes=[big.s(dc)])
            P.op("scalar", lambda e, dc=dc, ps=ps: e.activation(out=sq[:, dc % 2, 0:T], in_=ps[:, 0:T], func=AF.Square),
                 reads=[ps], writes=[sq.s(dc % 2)])
            P.op("tensor", lambda e, dc=dc: e.matmul(stats[:, 0:T], lhsT=C.ones_f.ap, rhs=sq[:, dc % 2, 0:T],
                                                     start=(dc == 0), stop=(dc == NCH - 1)),
                 reads=[sq.s(dc % 2), C.ones_f], writes=[stats])
    rstd_from_ps(P, C, rstd, stats, T)


def gemm_fm(P, C, R, hT, KC, wview, wbuf, col0, nchunks, T, consume):
    per = max(1, 8192 // (KC * 128))
    j = 0
    while j < nchunks:
        n = min(per, nchunks - j)
        slot = R.slot()
        sv = slot.ap[:, 0:KC * n * 128].rearrange("p (k f) -> p k f", k=KC)
        ks = [0, KC // 2, KC]
        for a, b in zip(ks[:-1], ks[1:]):
            P.dma("sync", sv[:, a:b, :], wview[:, a:b, col0 + j * 128:col0 + (j + n) * 128], reads=[wbuf],
                  writes=[slot])
        for jj in range(n):
            ps = R.bank()
            for kc in range(KC):
                P.op("tensor", lambda e, kc=kc, sv=sv, ps=ps, jj=jj: e.matmul(
                    ps[:, 0:T], lhsT=sv[:, kc, jj * 128:(jj + 1) * 128], rhs=hT[:, kc, 0:T],
                    start=(kc == 0), stop=(kc == KC - 1)), reads=[slot, hT.s(kc)], writes=[ps])
            consume(j + jj, ps)
        j += n


def ffn_stage(P, C, li, x_in, xin_buf, x_out, xout_buf, L):
    T = min(512, L)
    nt = L // T
    O = C.off
    hT = P.alloc("hT", (16, T), BF16)
    big = P.alloc("big", (16, T), F32)
    act = P.alloc("act", (FFN_HC, T), BF16)
    wsl = [P.alloc("w%d" % i, (8192,), BF16) for i in range(3)]
    uext = [P.alloc("ue%d" % i, (T + 2,), F32) for i in range(4)]
    t2 = [P.alloc("t2%d" % i, (T,), F32) for i in range(4)]
    carry = P.alloc("carry", (88, 2), F32)
    sq = P.alloc("sq", (2, T), F32)
    rstd = P.alloc("rstd", (T,), F32)
    xc = [P.alloc("xc%d" % i, (T,), F32) for i in range(2)]
    wup = C.wb["ffn_up"][li].rearrange("(kc p) f -> p kc f", p=128)
    wdn = C.wb["ffn_dn"][li].rearrange("(kc p) f -> p kc f", p=128)
    wupb, wdnb = C.wbuf["ffn_up"], C.wbuf["ffn_dn"]
    st = Rot(P, wsl)
    rot = [0]
    next_bank = st.bank

    for ti in range(nt):
        t0 = ti * T
        pre_norm(P, C, x_in, xin_buf, t0, T, O["pre%d_2" % li], big, hT, sq, rstd, P.psum[0])

        def do_chunk(cidx, slot, j):
            ps = next_bank()
            sv = slot.ap.rearrange("p (k f) -> p k f", k=16)
            for kc in range(16):
                P.op("tensor", lambda e, kc=kc: e.matmul(ps[:, 0:T], lhsT=sv[:, kc, j * 128:(j + 1) * 128],
                                                         rhs=hT[:, kc, 0:T], start=(kc == 0), stop=(kc == 15)),
                     reads=[slot, hT.s(kc)], writes=[ps])
            r = rot[0] % 4
            rot[0] += 1
            ue, tt = uext[r], t2[r]
            if ti == 0:
                P.op("gpsimd", lambda e: e.memset(ue[:, 0:2], 0.0), writes=[ue])
            else:
                P.op("gpsimd", lambda e: e.tensor_copy(out=ue[:, 0:2], in_=carry[:, cidx, :]),
                     reads=[carry.s(cidx)], writes=[ue])
            P.op("scalar", lambda e: e.activation(out=ue[:, 2:T + 2], in_=ps[:, 0:T], func=AF.Copy),
                 reads=[ps], writes=[ue])
            w2 = O["fcw%d_2" % li] + cidx
            bb = O["fcb%d" % li] + cidx
            P.op("scalar", lambda e: e.activation(out=tt[:, 0:T], in_=ps[:, 0:T], func=AF.Identity,
                                                  scale=C.cols[:, w2:w2 + 1], bias=C.cols[:, bb:bb + 1]),
                 reads=[ps, C.cols], writes=[tt])
            P.op("gpsimd", lambda e: e.tensor_copy(out=carry[:, cidx, :], in_=ue[:, T:T + 2]),
                 reads=[ue], writes=[carry.s(cidx)])
            for k, sh in ((1, 1), (0, 0)):
                wk = O["fcw%d_%d" % (li, k)] + cidx
                P.op("vector", lambda e, wk=wk, sh=sh: e.scalar_tensor_tensor(
                    out=tt[:, 0:T], in0=ue[:, sh:sh + T], scalar=C.cols[:, wk:wk + 1], in1=tt[:, 0:T],
                    op0=ALU.mult, op1=ALU.add), reads=[ue, tt, C.cols], writes=[tt])
            return tt

        for grp in range(11):
            sg = st.slot()
            svw = st.slot()
            for slot, c0 in ((sg, grp * 512), (svw, FFN_H + grp * 512)):
                sv = slot.ap.rearrange("p (k f) -> p k f", k=16)
                for hh in range(2):
                    P.dma("sync", sv[:, 8 * hh:8 * hh + 8, :], wup[:, 8 * hh:8 * hh + 8, c0:c0 + 512],
                          reads=[wupb], writes=[slot])
            for j in range(4):
                fc = grp * 4 + j
                tg = do_chunk(fc, sg, j)
                tv = do_chunk(FFN_HC + fc, svw, j)
                P.op("scalar", lambda e, tg=tg: e.activation(out=tg[:, 0:T], in_=tg[:, 0:T], func=AF.Silu),
                     reads=[tg], writes=[tg])
                P.op("gpsimd", lambda e, tg=tg, tv=tv, fc=fc: e.tensor_tensor(out=act[:, fc, 0:T], in0=tg[:, 0:T],
                                                                             in1=tv[:, 0:T], op=ALU.mult),
                     reads=[tg, tv], writes=[act.s(fc)])
        out_proj(P, C, act, FFN_HC, wdn, wdnb, wsl, st, big, sq, rstd, T)
        post_residual(P, C, big, rstd, x_in, xin_buf, x_out, xout_buf, t0, T, O["post%d_2" % li], xc)


def row_bcast(P, C, tile, name, n):
    o = C.roff[name]
    P.dma("sync", tile.ap, C.rows[o:o + n].partition_broadcast(128), writes=[tile])


def pool_stage(P, C, li, x_in, xin_buf, x_out, xout_buf, L):
    T = min(512, L)
    nt = L // T
    O = C.off
    N = T + 15
    hT = P.alloc("hT", (16, T), BF16)
    big = P.alloc("big", (16, T), F32)
    wsl = [P.alloc("w%d" % i, (8192,), BF16) for i in range(2)]
    sq = P.alloc("sq", (2, T), F32)
    rstd = P.alloc("rstd", (T,), F32)
    xc = [P.alloc("xc%d" % i, (T,), F32) for i in range(2)]
    zext = [P.alloc("ze%d" % i, (N,), F32) for i in range(3)]
    sa = [P.alloc("sa%d" % i, (N,), F32) for i in range(3)]
    sb = [P.alloc("sb%d" % i, (N,), F32) for i in range(3)]
    pooled = P.alloc("pooled", (16, T), BF16)
    yT = P.alloc("yT", (16, T), BF16)
    zc = P.alloc("zc", (16, 15), F32)
    wg = P.alloc("wg", (4, 4, 512), BF16)
    win = C.wb["pool_in"][0].rearrange("(kc p) f -> p kc f", p=128)
    wout = C.wb["pool_out"][0].rearrange("(kc p) f -> p kc f", p=128)
    wgv = C.wb["pool_grp"][0].rearrange("g (dc p) e -> p g dc e", p=128)
    for g in range(4):
        P.dma("sync", wg[:, g, :, :], wgv[:, g, :, :], reads=[C.wbuf["pool_grp"]], writes=[wg])
    R = Rot(P, wsl)
    rot = [0]
    for ti in range(nt):
        t0 = ti * T
        pre_norm(P, C, x_in, xin_buf, t0, T, O["pre%d_0" % li], big, hT, sq, rstd, P.psum[0])

        def put_z(c, ps):
            r = rot[0] % 3
            rot[0] += 1
            ze, a, b = zext[r], sa[r], sb[r]
            w = (2, 4, 8, 16)[c // 4]
            if ti == 0:
                P.op("gpsimd", lambda e: e.memset(ze[:, 0:15], 0.0), writes=[ze])
            else:
                P.op("gpsimd", lambda e: e.tensor_copy(out=ze[:, 0:15], in_=zc[:, c, :]), reads=[zc.s(c)], writes=[ze])
            P.op("scalar", lambda e: e.activation(out=ze[:, 15:N], in_=ps[:, 0:T], func=AF.Copy), reads=[ps], writes=[ze])
            P.op("gpsimd", lambda e: e.tensor_copy(out=zc[:, c, :], in_=ze[:, T:N]), reads=[ze], writes=[zc.s(c)])
            cur, lo, sh = ze, 0, 1
            k = 0
            while sh < w:
                dst = a if k % 2 == 0 else b
                nlo = lo + sh
                P.op("vector", lambda e, cur=cur, dst=dst, nlo=nlo, sh=sh: e.tensor_tensor(
                    out=dst[:, nlo:N], in0=cur[:, nlo:N], in1=cur[:, nlo - sh:N - sh], op=ALU.add),
                    reads=[cur], writes=[dst])
                cur, lo, sh, k = dst, nlo, sh * 2, k + 1
            P.op("vector", lambda e, cur=cur: e.scalar_tensor_tensor(out=pooled[:, c, 0:T], in0=cur[:, 15:N], scalar=1.0 / w,
                                                                  in1=ze[:, 15:N], op0=ALU.mult, op1=ALU.subtract),
                 reads=[cur, ze], writes=[pooled.s(c)])
            if ti == 0:
                for t in range(w - 1):
                    P.op("vector", lambda e, cur=cur, t=t: e.scalar_tensor_tensor(
                        out=pooled[:, c, t:t + 1], in0=cur[:, 15 + t:16 + t], scalar=1.0 / (t + 1),
                        in1=ze[:, 15 + t:16 + t], op0=ALU.mult, op1=ALU.subtract), reads=[cur, ze], writes=[pooled.s(c)])

        gemm_fm(P, C, R, hT, 16, win, C.wbuf["pool_in"], 0, 16, T, put_z)
        for g in range(4):
            for ec in range(4):
                ps = R.bank()
                for dc in range(4):
                    P.op("tensor", lambda e, ps=ps, g=g, ec=ec, dc=dc: e.matmul(
                        ps[:, 0:T], lhsT=wg[:, g, dc, ec * 128:(ec + 1) * 128], rhs=pooled[:, 4 * g + dc, 0:T],
                        start=(dc == 0), stop=(dc == 3)), reads=[wg, pooled.s(4 * g + dc)], writes=[ps])
                sc = O["pscale"] + 4 * g + ec
                P.op("scalar", lambda e, ps=ps, g=g, ec=ec, sc=sc: e.activation(
                    out=yT[:, 4 * g + ec, 0:T], in_=ps[:, 0:T], func=AF.Identity, scale=C.cols[:, sc:sc + 1]),
                    reads=[ps, C.cols], writes=[yT.s(4 * g + ec)])
        out_proj(P, C, yT, 16, wout, C.wbuf["pool_out"], wsl, R, big, sq, rstd, T)
        post_residual(P, C, big, rstd, x_in, xin_buf, x_out, xout_buf, t0, T, O["post%d_0" % li], xc)


GELU_NATIVE = True


def gelu_inplace(P, x_ap, xt, tmp_ap, tmpt, eng2="gpsimd"):
    if GELU_NATIVE:
        P.op("scalar", lambda e: e.activation(out=x_ap, in_=x_ap, func=AF.Gelu_apprx_tanh), reads=[xt], writes=[xt])
        return
    P.op("scalar", lambda e: e.activation(out=tmp_ap, in_=x_ap, func=AF.Square), reads=[xt], writes=[tmpt])
    P.op("vector", lambda e: e.tensor_scalar(out=tmp_ap, in0=tmp_ap, scalar1=0.044715, scalar2=1.0, op0=ALU.mult,
                                             op1=ALU.add), reads=[tmpt], writes=[tmpt])
    P.op("vector", lambda e: e.tensor_tensor(out=tmp_ap, in0=tmp_ap, in1=x_ap, op=ALU.mult), reads=[tmpt, xt],
         writes=[tmpt])
    P.op("scalar", lambda e: e.activation(out=tmp_ap, in_=tmp_ap, func=AF.Sigmoid, scale=1.5957691216057308),
         reads=[tmpt], writes=[tmpt])
    P.op(eng2, lambda e: e.tensor_tensor(out=x_ap, in0=x_ap, in1=tmp_ap, op=ALU.mult), reads=[tmpt, xt], writes=[xt])


def sgu_stage(P, C, li, x_in, xin_buf, x_out, xout_buf, L):
    T = min(128, L)
    nt = L // T
    ntc = T // 128
    O = C.off
    hT = P.alloc("hT", (16, T), BF16)
    big = P.alloc("big", (16, T), F32)
    wsl = [P.alloc("w%d" % i, (8192,), BF16) for i in range(3)]
    sq = P.alloc("sq", (2, T), F32)
    rstd = P.alloc("rstd", (T,), F32)
    xc = [P.alloc("xc%d" % i, (T,), F32) for i in range(2)]
    uT = P.alloc("uT", (32, T), BF16)
    uf = [P.alloc("uf%d" % i, (T,), F32) for i in range(2)]
    ut = [P.alloc("ut%d" % i, (T,), F32) for i in range(2)]
    vtm = P.alloc("vtm", (ntc, 4096), F32)
    vtmp = P.alloc("vtmp", (512,), F32)
    vn = P.alloc("vn", (ntc, 4096), BF16)
    gT = P.alloc("gT", (32, T), BF16)
    binb = P.alloc("binb", (4096,), F32)
    lngb = P.alloc("lngb", (4096,), F32)
    bsp = P.alloc("bsp", (8, 128), F32)
    wsf = P.alloc("wsf", (8, 128), F32)
    wmT = P.alloc("wmT", (8, 128), BF16)
    st4 = P.alloc("st4", (8,), F32)
    sp = [P.alloc("sp%d" % i, (128,), F32) for i in range(2)]
    row_bcast(P, C, binb, "sgu_binv", 4096)
    row_bcast(P, C, lngb, "sgu_lng", 4096)
    o = C.roff["sgu_bsp"]
    P.dma("sync", bsp.ap, C.rows[o:o + 1024].partition_broadcast(128).rearrange("p (g t) -> p g t", g=8), writes=[bsp])
    P.dma("sync", wsf.ap, C.sgu_wsT, writes=[wsf])
    P.op("gpsimd", lambda e: e.affine_select(out=wmT.ap, in_=wsf.ap, pattern=[[0, 8], [1, 128]], compare_op=ALU.is_ge,
                                             fill=fillreg(e, 0.0), base=0, channel_multiplier=-1), reads=[wsf], writes=[wmT])
    win = C.wb["sgu_in"][0].rearrange("(kc p) f -> p kc f", p=128)
    wout = C.wb["sgu_out"][0].rearrange("(kc p) f -> p kc f", p=128)
    R = Rot(P, wsl, banks=(1, 2, 3, 4, 5))
    rot = [0]
    for ti in range(nt):
        t0 = ti * T
        pre_norm(P, C, x_in, xin_buf, t0, T, O["pre%d_0" % li], big, hT, sq, rstd, P.psum[0])

        def put_u(j, ps):
            r = rot[0] % 2
            rot[0] += 1
            bc = O["sgu_binu"] + j
            P.op("scalar", lambda e: e.activation(out=uf[r][:, 0:T], in_=ps[:, 0:T], func=AF.Identity,
                                                  bias=C.cols[:, bc:bc + 1]), reads=[ps, C.cols], writes=[uf[r]])
            gelu_inplace(P, uf[r][:, 0:T], uf[r], ut[r][:, 0:T], ut[r])
            P.op("gpsimd", lambda e: e.tensor_copy(out=uT[:, j, 0:T], in_=uf[r][:, 0:T]), reads=[uf[r]], writes=[uT.s(j)])

        gemm_fm(P, C, R, hT, 16, win, C.wbuf["sgu_in"], 0, 32, T, put_u)
        for grp in range(8):
            slot = R.slot()
            sv = slot.ap.rearrange("p (k f) -> p k f", k=16)
            for a, b in ((0, 8), (8, 16)):
                P.dma("sync", sv[:, a:b, :], win[:, a:b, 4096 + grp * 512:4096 + (grp + 1) * 512],
                      reads=[C.wbuf["sgu_in"]], writes=[slot])
            for tc in range(ntc):
                ps = R.bank()
                for kc in range(16):
                    P.op("tensor", lambda e, kc=kc, ps=ps, tc=tc, sv=sv: e.matmul(
                        ps[:, 0:512], lhsT=hT[:, kc, tc * 128:(tc + 1) * 128], rhs=sv[:, kc, :], start=(kc == 0),
                        stop=(kc == 15)), reads=[slot, hT.s(kc)], writes=[ps])
                sub = tc * 8 + grp
                va = vtm[:, tc, grp * 512:(grp + 1) * 512]
                P.op("vector", lambda e, ps=ps, va=va, grp=grp: e.tensor_tensor(
                    out=va, in0=ps[:, 0:512], in1=binb[:, grp * 512:(grp + 1) * 512], op=ALU.add),
                    reads=[ps, binb], writes=[vtm.s(sub)])
                gelu_inplace(P, va, vtm.s(sub), vtmp[:, 0:512], vtmp)
        for tc in range(ntc):
            vv = vtm[:, tc, :]
            P.op("scalar", lambda e, vv=vv, tc=tc: e.activation(out=vn[:, tc, :], in_=vv, func=AF.Copy,
                                                              accum_out=st4[:, 0:1]), reads=[vtm], writes=[vn.s(tc), st4])
            P.op("scalar", lambda e, vv=vv, tc=tc: e.activation(out=vn[:, tc, :], in_=vv, func=AF.Square,
                                                              accum_out=st4[:, 1:2]), reads=[vtm, st4],
                 writes=[vn.s(tc), st4])
            P.op("vector", lambda e: e.tensor_scalar(out=st4[:, 2:3], in0=st4[:, 0:1], scalar1=1.0 / 4096, scalar2=None,
                                                     op0=ALU.mult), reads=[st4], writes=[st4])
            P.op("vector", lambda e: e.tensor_tensor(out=st4[:, 3:4], in0=st4[:, 2:3], in1=st4[:, 2:3], op=ALU.mult),
                 reads=[st4], writes=[st4])
            P.op("vector", lambda e: e.scalar_tensor_tensor(out=st4[:, 4:5], in0=st4[:, 1:2], scalar=1.0 / 4096,
                                                            in1=st4[:, 3:4], op0=ALU.mult, op1=ALU.subtract),
                 reads=[st4], writes=[st4])
            P.op("scalar", lambda e: e.activation(out=st4[:, 5:6], in_=st4[:, 4:5], func=AF.Sqrt, bias=C.eps.ap),
                 reads=[st4, C.eps], writes=[st4])
            P.op("vector", lambda e: e.reciprocal(out=st4[:, 5:6], in_=st4[:, 5:6]), reads=[st4], writes=[st4])
            P.op("vector", lambda e, vv=vv: e.tensor_scalar(out=vv, in0=vv, scalar1=st4[:, 2:3], scalar2=st4[:, 5:6],
                                                            op0=ALU.subtract, op1=ALU.mult), reads=[vtm, st4], writes=[vtm])
            P.op("gpsimd", lambda e, vv=vv, tc=tc: e.tensor_tensor(out=vn[:, tc, :], in0=vv, in1=lngb.ap, op=ALU.mult),
                 reads=[vtm, lngb], writes=[vn.s(tc)])
            for q4 in range(8):
                ps = R.bank()
                for jj in range(4):
                    dcv = q4 * 4 + jj
                    P.op("tensor", lambda e, ps=ps, jj=jj, dcv=dcv, tc=tc, q4=q4: e.matmul(
                        ps[:, jj * 128:(jj + 1) * 128], lhsT=vn[:, tc, dcv * 128:(dcv + 1) * 128], rhs=wmT[:, q4, :],
                        start=True, stop=True), reads=[vn.s(tc), wmT], writes=[ps])
                for jj in range(4):
                    dcv = q4 * 4 + jj
                    s = sp[(q4 * 4 + jj) % 2]
                    P.op("vector", lambda e, ps=ps, jj=jj, s=s, q4=q4: e.tensor_tensor(
                        out=s.ap, in0=ps[:, jj * 128:(jj + 1) * 128], in1=bsp[:, q4, :], op=ALU.add),
                        reads=[ps, bsp], writes=[s])
                    P.op("gpsimd", lambda e, s=s, dcv=dcv, tc=tc: e.tensor_tensor(
                        out=gT[:, dcv, tc * 128:(tc + 1) * 128], in0=s.ap, in1=uT[:, dcv, tc * 128:(tc + 1) * 128],
                        op=ALU.mult), reads=[s, uT.s(dcv)], writes=[gT.s(dcv)])
        out_proj(P, C, gT, 32, wout, C.wbuf["sgu_out"], wsl, R, big, sq, rstd, T)
        post_residual(P, C, big, rstd, x_in, xin_buf, x_out, xout_buf, t0, T, O["post%d_0" % li], xc)


def ssd_stage(P, C, li, x_in, xin_buf, x_out, xout_buf, L):
    T = 128
    nt = L // T
    O = C.off
    hT = P.alloc("hT", (16, T), BF16)
    big = P.alloc("big", (16, T), F32)
    wsl = [P.alloc("w%d" % i, (8192,), BF16) for i in range(2)]
    sq = P.alloc("sq", (2, T), F32)
    rstd = P.alloc("rstd", (T,), F32)
    xc = [P.alloc("xc%d" % i, (T,), F32) for i in range(2)]
    szT = P.alloc("szT", (32, T), BF16)
    xsb = P.alloc("xsb", (32, T), BF16)
    BT = P.alloc("BT", (8, T), BF16)
    CT = P.alloc("CT", (8, T), BF16)
    ue = [P.alloc("ue%d" % i, (T + 3,), F32) for i in range(3)]
    tt = [P.alloc("tt%d" % i, (T,), F32) for i in range(3)]
    carry = P.alloc("carry", (48, 3), F32)
    wdt = P.alloc("wdt", (16, 64), BF16)
    dtb = P.alloc("dtb", (64,), F32)
    Aneg = P.alloc("Aneg", (64,), F32)
    d1 = P.alloc("d1", (64,), F32)
    d2 = P.alloc("d2", (64,), F32)
    dt = P.alloc("dt", (64,), F32)
    aa = P.alloc("aa", (64,), F32)
    dte = P.alloc("dte", (64,), F32)
    dec = P.alloc("dec", (64,), F32)
    x_tm = P.alloc("x_tm", (4096,), BF16)
    B_tm = P.alloc("B_tm", (1024,), BF16)
    xdt = P.alloc("xdt", (4096,), BF16)
    xdtw = P.alloc("xdtw", (4096,), BF16)
    cbm = P.alloc("cbm", (8, 128), BF16)
    R4 = [P.alloc("R4%d" % i, (4, 128), F32) for i in range(2)]
    LT = [P.alloc("LT%d" % i, (4, 128), BF16) for i in range(2)]
    MT = [P.alloc("MT%d" % i, (4, 128), BF16) for i in range(2)]
    Ed = [P.alloc("Ed%d" % i, (4, 128), BF16) for i in range(2)]
    Cd = [P.alloc("Cd%d" % i, (4, 128), BF16) for i in range(2)]
    yT = P.alloc("yT", (32, T), F32)
    S = P.alloc("S", (4096,), F32)
    prevT = P.alloc("prevT", (4096,), BF16)
    gT = P.alloc("gT", (32, T), BF16)
    win = C.wb["ssd_in"][0].rearrange("(kc p) f -> p kc f", p=128)
    wout = C.wb["ssd_out"][0].rearrange("(kc p) f -> p kc f", p=128)
    wib = C.wbuf["ssd_in"]
    P.dma("sync", wdt.ap, win[:, :, 10240:10304], reads=[wib], writes=[wdt])
    row_bcast(P, C, dtb, "ssd_dtb", 64)
    row_bcast(P, C, Aneg, "ssd_alog", 64)
    P.op("scalar", lambda e: e.activation(out=Aneg.ap, in_=Aneg.ap, func=AF.Exp), reads=[Aneg], writes=[Aneg])
    P.op("vector", lambda e: e.tensor_scalar(out=Aneg.ap, in0=Aneg.ap, scalar1=-1.0, scalar2=None, op0=ALU.mult),
         reads=[Aneg], writes=[Aneg])
    P.op("gpsimd", lambda e: e.memset(S.ap, 0.0), writes=[S])
    R = Rot(P, wsl)
    rot = [0]
    cwo, cbo, ngo, dco = O["ssd_cw"], O["ssd_cb"], O["ssd_ng"], O["ssd_dcol"]
    for ti in range(nt):
        t0 = ti * T
        pre_norm(P, C, x_in, xin_buf, t0, T, O["pre%d_0" % li], big, hT, sq, rstd, P.psum[0])
        P.op("gpsimd", lambda e: e.tensor_copy(out=prevT.ap, in_=S.ap), reads=[S], writes=[prevT])

        def put_z(j, ps):
            P.op("scalar", lambda e: e.activation(out=szT[:, j, :], in_=ps[:, 0:T], func=AF.Silu), reads=[ps],
                 writes=[szT.s(j)])

        gemm_fm(P, C, R, hT, 16, win, wib, 0, 32, T, put_z)

        def put_xbc(j, ps):
            r = rot[0] % 3
            rot[0] += 1
            u, t_ = ue[r], tt[r]
            if ti == 0:
                P.op("gpsimd", lambda e: e.memset(u[:, 0:3], 0.0), writes=[u])
            else:
                P.op("gpsimd", lambda e: e.tensor_copy(out=u[:, 0:3], in_=carry[:, j, :]), reads=[carry.s(j)], writes=[u])
            P.op("scalar", lambda e: e.activation(out=u[:, 3:T + 3], in_=ps[:, 0:T], func=AF.Copy), reads=[ps], writes=[u])
            P.op("scalar", lambda e: e.activation(out=t_.ap, in_=ps[:, 0:T], func=AF.Identity,
                                                  scale=C.cols[:, cwo + 3 * 48 + j:cwo + 3 * 48 + j + 1],
                                                  bias=C.cols[:, cbo + j:cbo + j + 1]), reads=[ps, C.cols], writes=[t_])
            P.op("gpsimd", lambda e: e.tensor_copy(out=carry[:, j, :], in_=u[:, T:T + 3]), reads=[u], writes=[carry.s(j)])
            for k in (2, 1, 0):
                wk = cwo + k * 48 + j
                P.op("vector", lambda e, k=k, wk=wk: e.scalar_tensor_tensor(
                    out=t_.ap, in0=u[:, k:k + T], scalar=C.cols[:, wk:wk + 1], in1=t_.ap, op0=ALU.mult, op1=ALU.add),
                    reads=[u, t_, C.cols], writes=[t_])
            if j < 32:
                dst, dd = xsb[:, j, :], xsb.s(j)
            elif j < 40:
                dst, dd = BT[:, j - 32, :], BT.s(j - 32)
            else:
                dst, dd = CT[:, j - 40, :], CT.s(j - 40)
            P.op("scalar", lambda e: e.activation(out=dst, in_=t_.ap, func=AF.Silu), reads=[t_], writes=[dd])

        gemm_fm(P, C, R, hT, 16, win, wib, 4096, 48, T, put_xbc)
        ps = R.bank()
        for kc in range(16):
            P.op("tensor", lambda e, kc=kc, ps=ps: e.matmul(ps[:, 0:64], lhsT=hT[:, kc, :], rhs=wdt[:, kc, :],
                                                          start=(kc == 0), stop=(kc == 15)),
                 reads=[hT.s(kc), wdt], writes=[ps])
        P.op("vector", lambda e, ps=ps: e.tensor_tensor(out=d1.ap, in0=ps[:, 0:64], in1=dtb.ap, op=ALU.add),
             reads=[ps, dtb], writes=[d1])
        P.op("scalar", lambda e: e.activation(out=d2.ap, in_=d1.ap, func=AF.Abs), reads=[d1], writes=[d2])
        P.op("scalar", lambda e: e.activation(out=d2.ap, in_=d2.ap, func=AF.Exp, scale=-1.0), reads=[d2], writes=[d2])
        P.op("scalar", lambda e: e.activation(out=d2.ap, in_=d2.ap, func=AF.Ln, bias=1.0), reads=[d2], writes=[d2])
        P.op("vector", lambda e: e.scalar_tensor_tensor(out=dt.ap, in0=d1.ap, scalar=0.0, in1=d2.ap, op0=ALU.max,
                                                        op1=ALU.add), reads=[d1, d2], writes=[dt])
        P.op("vector", lambda e: e.tensor_tensor(out=aa.ap, in0=dt.ap, in1=Aneg.ap, op=ALU.mult), reads=[dt, Aneg],
             writes=[aa])
        for q in range(5):
            ps = R.bank()
            psb = ps.ap.bitcast(BF16)
            for jj in range(8):
                j = q * 8 + jj
                src_ap = xsb[:, j, :] if j < 32 else BT[:, j - 32, :]
                sd = xsb.s(j) if j < 32 else BT.s(j - 32)
                P.op("tensor", lambda e, psb=psb, jj=jj, src_ap=src_ap: e.transpose(
                    out=psb[:, jj * 128:(jj + 1) * 128], in_=src_ap, identity=C.ident.ap), reads=[sd, C.ident], writes=[ps])
            if q < 4:
                P.op("scalar", lambda e, psb=psb, q=q: e.activation(out=x_tm[:, q * 1024:(q + 1) * 1024], in_=psb[:, 0:1024],
                                                                  func=AF.Copy), reads=[ps], writes=[x_tm])
            else:
                P.op("scalar", lambda e, psb=psb: e.activation(out=B_tm.ap, in_=psb[:, 0:1024], func=AF.Copy), reads=[ps],
                     writes=[B_tm])
        ps = R.bank()
        P.op("tensor", lambda e, ps=ps: e.matmul(ps[:, 0:64], lhsT=C.su.ap, rhs=aa.ap, start=True, stop=True),
             reads=[C.su, aa], writes=[ps])
        P.op("scalar", lambda e, ps=ps: e.activation(out=dte.ap, in_=ps[:, 0:64], func=AF.Exp), reads=[ps], writes=[dte])
        ps = R.bank()
        P.op("tensor", lambda e, ps=ps: e.matmul(ps[:, 0:64], lhsT=C.ones_f.ap, rhs=aa.ap, start=True, stop=True),
             reads=[C.ones_f, aa], writes=[ps])
        P.op("scalar", lambda e, ps=ps: e.activation(out=dec.ap, in_=ps[:, 0:64], func=AF.Exp), reads=[ps], writes=[dec])
        v3 = lambda t_: t_.ap.rearrange("p (h q) -> p h q", h=64)
        bc = lambda t_: t_.ap.unsqueeze(2).broadcast_to([128, 64, 64])
        P.op("vector", lambda e: e.tensor_tensor(out=v3(xdt), in0=v3(x_tm), in1=bc(dt), op=ALU.mult), reads=[x_tm, dt],
             writes=[xdt])
        P.op("gpsimd", lambda e: e.tensor_tensor(out=v3(xdtw), in0=v3(xdt), in1=bc(dte), op=ALU.mult), reads=[xdt, dte],
             writes=[xdtw])
        for g in range(8):
            ps = R.bank()
            P.op("tensor", lambda e, ps=ps, g=g: e.matmul(ps[:, 0:128], lhsT=BT[:, g, :], rhs=CT[:, g, :], start=True,
                                                        stop=True), reads=[BT.s(g), CT.s(g)], writes=[ps])
            P.op("vector", lambda e, ps=ps, g=g: e.tensor_tensor(out=cbm[:, g, :], in0=ps[:, 0:128], in1=C.trile.ap,
                                                               op=ALU.mult), reads=[ps, C.trile], writes=[cbm.s(g)])
        def ssd_s1(q):
            g = q // 2
            r4, lt, mt, ed, cd = R4[q % 2], LT[q % 2], MT[q % 2], Ed[q % 2], Cd[q % 2]
            for i in range(4):
                h = 4 * q + i
                P.op("vector", lambda e, i=i, h=h, r4=r4: e.tensor_scalar(out=r4[:, i, :], in0=C.trile.ap,
                                                                        scalar1=aa[:, h:h + 1], scalar2=None,
                                                                        op0=ALU.mult), reads=[C.trile, aa], writes=[r4])
            r4f = r4.ap.rearrange("p a b -> p (a b)")
            ps1 = R.bank()
            P.op("tensor", lambda e, ps1=ps1, r4f=r4f: e.matmul(ps1[:, 0:512], lhsT=C.su.ap, rhs=r4f, start=True, stop=True),
                 reads=[C.su, r4], writes=[ps1])
            ps2 = R.bank()
            P.op("tensor", lambda e, ps2=ps2, r4f=r4f: e.matmul(ps2[:, 0:512], lhsT=C.ones_f.ap, rhs=r4f, start=True,
                                                              stop=True), reads=[C.ones_f, r4], writes=[ps2])
            P.op("scalar", lambda e, ps1=ps1, lt=lt: e.activation(out=lt.ap.rearrange("p a b -> p (a b)"), in_=ps1[:, 0:512],
                                                                func=AF.Exp), reads=[ps1], writes=[lt])
            P.op("scalar", lambda e, ps2=ps2, ed=ed: e.activation(out=ed.ap.rearrange("p a b -> p (a b)"), in_=ps2[:, 0:512],
                                                                func=AF.Exp), reads=[ps2], writes=[ed])
            P.op("vector", lambda e, lt=lt, mt=mt, g=g: e.tensor_tensor(
                out=mt.ap, in0=lt.ap, in1=cbm[:, g, :].unsqueeze(1).broadcast_to([128, 4, 128]), op=ALU.mult),
                reads=[lt, cbm.s(g)], writes=[mt])
            P.op("gpsimd", lambda e, ed=ed, cd=cd, g=g: e.tensor_tensor(
                out=cd.ap, in0=ed.ap, in1=CT[:, g, :].unsqueeze(1).broadcast_to([128, 4, 128]), op=ALU.mult),
                reads=[ed, CT.s(g)], writes=[cd])

        def ssd_s2(q):
            g = q // 2
            mt, cd = MT[q % 2], Cd[q % 2]
            psy = R.bank()
            for i in range(4):
                h = 4 * q + i
                pc = h // 2
                P.op("tensor", lambda e, psy=psy, i=i, pc=pc, mt=mt: e.matmul(
                    psy[:, i * 128:(i + 1) * 128], lhsT=xdt[:, pc * 128:(pc + 1) * 128], rhs=mt[:, i, :], start=True,
                    stop=False), reads=[xdt, mt], writes=[psy])
                P.op("tensor", lambda e, psy=psy, i=i, pc=pc, cd=cd: e.matmul(
                    psy[:, i * 128:(i + 1) * 128], lhsT=prevT[:, pc * 128:(pc + 1) * 128], rhs=cd[:, i, :], start=False,
                    stop=True), reads=[prevT, cd], writes=[psy])
            for i in range(4):
                h = 4 * q + i
                pc, r0 = h // 2, (h % 2) * 64
                P.op("vector", lambda e, psy=psy, i=i, pc=pc, r0=r0: e.scalar_tensor_tensor(
                    out=yT[r0:r0 + 64, pc, :], in0=xsb[r0:r0 + 64, pc, :], scalar=C.cols[r0:r0 + 64, dco + pc:dco + pc + 1],
                    in1=psy[r0:r0 + 64, i * 128:(i + 1) * 128], op0=ALU.mult, op1=ALU.add),
                    reads=[psy, xsb.s(pc), C.cols], writes=[yT.s(pc)])

        ssd_s1(0)
        for q in range(16):
            if q + 1 < 16:
                ssd_s1(q + 1)
            ssd_s2(q)
        for g in range(8):
            ps = R.bank()
            P.op("tensor", lambda e, ps=ps, g=g: e.matmul(ps[:, 0:512], lhsT=B_tm[:, g * 128:(g + 1) * 128],
                                                        rhs=xdtw[:, g * 512:(g + 1) * 512], start=True, stop=True),
                 reads=[B_tm, xdtw], writes=[ps])
            sv_ = S.ap[:, g * 512:(g + 1) * 512].rearrange("p (h q) -> p h q", h=8)
            P.op("vector", lambda e, g=g, sv_=sv_: e.tensor_tensor(
                out=sv_, in0=sv_, in1=dec[:, 8 * g:8 * g + 8].unsqueeze(2).broadcast_to([128, 8, 64]), op=ALU.mult),
                reads=[S, dec, prevT], writes=[S])
            P.op("vector", lambda e, g=g, ps=ps: e.tensor_tensor(out=S[:, g * 512:(g + 1) * 512],
                                                               in0=S[:, g * 512:(g + 1) * 512], in1=ps[:, 0:512],
                                                               op=ALU.add), reads=[S, ps], writes=[S])
        stats = P.psum[0]
        for j in range(32):
            P.op("gpsimd", lambda e, j=j: e.tensor_tensor(out=yT[:, j, :], in0=yT[:, j, :], in1=szT[:, j, :], op=ALU.mult),
                 reads=[yT.s(j), szT.s(j)], writes=[yT.s(j)])
            P.op("scalar", lambda e, j=j: e.activation(out=sq[:, j % 2, :], in_=yT[:, j, :], func=AF.Square),
                 reads=[yT.s(j)], writes=[sq.s(j % 2)])
            P.op("tensor", lambda e, j=j: e.matmul(stats[:, 0:T], lhsT=C.ones_f.ap, rhs=sq[:, j % 2, :], start=(j == 0),
                                                   stop=(j == 31)), reads=[sq.s(j % 2), C.ones_f], writes=[stats])
        rstd_from_ps(P, C, rstd, stats, T, n=4096)
        for j in range(32):
            P.op("vector", lambda e, j=j: e.scalar_tensor_tensor(out=gT[:, j, :], in0=yT[:, j, :],
                                                                 scalar=C.cols[:, ngo + j:ngo + j + 1], in1=rstd[:, 0:T],
                                                                 op0=ALU.mult, op1=ALU.mult),
                 reads=[yT.s(j), rstd, C.cols], writes=[gT.s(j)])
        out_proj(P, C, gT, 32, wout, C.wbuf["ssd_out"], wsl, R, big, sq, rstd, T)
        post_residual(P, C, big, rstd, x_in, xin_buf, x_out, xout_buf, t0, T, O["post%d_0" % li], xc)


def nsa_stage(P, C, li, x_in, xin_buf, x_out, xout_buf, L):
    O = C.off
    nc = C.nc
    NC = (L - 32) // 16 + 1
    NIC = (NC + 127) // 128
    NSL = L // 64
    scale = 128.0 ** -0.5
    kvT_d = nc.dram_tensor("nsa_kvT", [4, 4, 128, L], BF16, kind="Internal").ap()
    vtm_d = nc.dram_tensor("nsa_vtm", [2, L, 512], BF16, kind="Internal").ap()
    kvT_b, vtm_b = Buf("kvT"), Buf("vtm")
    win = C.wb["nsa_in"][0].rearrange("(kc p) f -> p kc f", p=128)
    wout = C.wb["nsa_out"][0].rearrange("(kc p) f -> p kc f", p=128)
    wib = C.wbuf["nsa_in"]
    kcT = P.alloc("kcT", (4, NIC * 128), BF16)
    vc_tm = P.alloc("vc_tm", (NIC, 4, 128), BF16)
    P.op("gpsimd", lambda e: e.memset(kcT.ap, 0.0), writes=[kcT])
    P.op("gpsimd", lambda e: e.memset(vc_tm.ap, 0.0), writes=[vc_tm])
    mark = P.arena_off

    def phase_a():
        T = min(512, L)
        nt = L // T
        hT = P.alloc("hT", (16, T), BF16)
        big = P.alloc("big", (16, T), F32)
        wsl = [P.alloc("w%d" % i, (8192,), BF16) for i in range(3)]
        sq = P.alloc("sq", (2, T), F32)
        rstd = P.alloc("rstd", (T,), F32)
        stg = [P.alloc("stg%d" % i, (T,), BF16) for i in range(3)]
        R = Rot(P, wsl)
        rot = [0]
        FM = ((0, 2048), (1, 2560), (2, 3072), (3, 4096))
        for ti in range(nt):
            t0 = ti * T
            pre_norm(P, C, x_in, xin_buf, t0, T, O["pre%d_0" % li], big, hT, sq, rstd, P.psum[0])
            for fam, c0 in FM:
                def put(j, ps, fam=fam):
                    s = stg[rot[0] % 3]
                    rot[0] += 1
                    P.op("scalar", lambda e: e.activation(out=s[:, 0:T], in_=ps[:, 0:T], func=AF.Copy), reads=[ps], writes=[s])
                    P.dma("gpsimd", kvT_d[fam, j, :, t0:t0 + T], s[:, 0:T], reads=[s], writes=[kvT_b])
                gemm_fm(P, C, R, hT, 16, win, wib, c0, 4, T, put)
            for f, c0 in ((0, 3584), (1, 4608)):
                slot = R.slot()
                sv = slot.ap.rearrange("p (k f) -> p k f", k=16)
                for a, b in ((0, 8), (8, 16)):
                    P.dma("sync", sv[:, a:b, :], win[:, a:b, c0:c0 + 512], reads=[wib], writes=[slot])
                for tc in range(T // 128):
                    ps = R.bank()
                    for kc in range(16):
                        P.op("tensor", lambda e, kc=kc, ps=ps, tc=tc, sv=sv: e.matmul(
                            ps[:, 0:512], lhsT=hT[:, kc, tc * 128:(tc + 1) * 128], rhs=sv[:, kc, :], start=(kc == 0),
                            stop=(kc == 15)), reads=[slot, hT.s(kc)], writes=[ps])
                    s = stg[rot[0] % 3]
                    rot[0] += 1
                    P.op("scalar", lambda e, s=s, ps=ps: e.activation(out=s[:, 0:512], in_=ps[:, 0:512], func=AF.Copy), reads=[ps],
                         writes=[s])
                    P.dma("gpsimd", vtm_d[f, t0 + tc * 128:t0 + (tc + 1) * 128, :], s[:, 0:512], reads=[s], writes=[vtm_b])
        P.barrier()
        P.arena_off = mark
    phase_a()

    def phase_b():
        w1s = P.alloc("w1s", (32, 256), BF16)
        w2s = P.alloc("w2s", (2, 128), BF16)
        posf = P.alloc("posf", (2, 32), F32)
        posb = P.alloc("posb", (2, 32), BF16)
        pbcol = P.alloc("pbcol", (2,), F32)
        kin = [P.alloc("kin%d" % i, (L,), BF16) for i in range(2)]
        Hg = P.alloc("Hg", (2, NIC * 128), BF16)
        P.dma("sync", posf.ap, C.nsa_posT, writes=[posf])
        P.op("vector", lambda e: e.tensor_copy(out=posb.ap, in_=posf.ap), reads=[posf], writes=[posb])
        P.op("gpsimd", lambda e: e.memset(Hg.ap, 0.0), writes=[Hg])
        Rb = Rot(P, [], banks=(1, 2, 3, 4, 5, 6, 7))
        w1d = C.wb["nsa_w1"][0]
        w2d = C.wb["nsa_w2"][0]
        for fam in range(2):
            P.dma("sync", w1s.ap, w1d[fam].rearrange("(l p) h -> p l h", p=128), reads=[C.wbuf["nsa_w1"]], writes=[w1s])
            P.dma("sync", w2s.ap, w2d[fam].rearrange("(c p) d -> p c d", p=128), reads=[C.wbuf["nsa_w2"]], writes=[w2s])
            for hc in range(2):
                ps = Rb.bank()
                for l in range(32):
                    P.op("tensor", lambda e, ps=ps, l=l, hc=hc, fam=fam: e.matmul(
                        ps[:, 0:1], lhsT=w1s[:, l, hc * 128:(hc + 1) * 128], rhs=posb[:, fam, l:l + 1], start=(l == 0),
                        stop=(l == 31)), reads=[w1s, posb], writes=[ps])
                P.op("scalar", lambda e, ps=ps, hc=hc: e.activation(out=pbcol[:, hc:hc + 1], in_=ps[:, 0:1], func=AF.Copy),
                     reads=[ps], writes=[pbcol])
            for g in range(4):
                kk = kin[g % 2]
                P.dma("sync", kk.ap, kvT_d[fam, g, :, :], reads=[kvT_b], writes=[kk])
                for hc in range(2):
                    ps = Rb.bank()
                    for l in range(32):
                        P.op("tensor", lambda e, ps=ps, l=l, hc=hc, kk=kk: e.matmul(
                            ps[:, 0:NC], lhsT=w1s[:, l, hc * 128:(hc + 1) * 128], rhs=kk[:, l:l + 16 * (NC - 1) + 1:16],
                            start=(l == 0), stop=(l == 31)), reads=[w1s, kk], writes=[ps])
                    P.op("scalar", lambda e, ps=ps, hc=hc: e.activation(out=Hg[:, hc, 0:NC], in_=ps[:, 0:NC],
                                                                      func=AF.Gelu_apprx_tanh, bias=pbcol[:, hc:hc + 1]),
                         reads=[ps, pbcol], writes=[Hg])
                if fam == 0:
                    ps = Rb.bank()
                    for hc in range(2):
                        P.op("tensor", lambda e, ps=ps, hc=hc: e.matmul(ps[:, 0:NC], lhsT=w2s[:, hc, :], rhs=Hg[:, hc, 0:NC],
                                                                      start=(hc == 0), stop=(hc == 1)), reads=[w2s, Hg],
                             writes=[ps])
                    P.op("scalar", lambda e, ps=ps, g=g: e.activation(out=kcT[:, g, 0:NC], in_=ps[:, 0:NC], func=AF.Copy),
                         reads=[ps], writes=[kcT])
                else:
                    for ic in range(NIC):
                        ni = min(128, NC - ic * 128)
                        ps = Rb.bank()
                        for hc in range(2):
                            P.op("tensor", lambda e, ps=ps, hc=hc, ic=ic, ni=ni: e.matmul(
                                ps[0:ni, 0:128], lhsT=Hg[:, hc, ic * 128:ic * 128 + ni], rhs=w2s[:, hc, :], start=(hc == 0),
                                stop=(hc == 1)), reads=[w2s, Hg], writes=[ps])
                        P.op("scalar", lambda e, ps=ps, ic=ic, g=g, ni=ni: e.activation(out=vc_tm[0:ni, ic, g, :],
                                                                                     in_=ps[0:ni, 0:128], func=AF.Copy),
                             reads=[ps], writes=[vc_tm])
        P.barrier()
        P.arena_off = mark
    phase_b()

    def phase_c():
        T = min(256, L)
        nt = L // T
        NTC = T // 128
        hT = P.alloc("hT", (16, T), BF16)
        big = P.alloc("big", (16, T), F32)
        wsl = [P.alloc("w%d" % i, (8192,), BF16) for i in range(2)]
        sq = P.alloc("sq", (2, T), F32)
        rstd = P.alloc("rstd", (T,), F32)
        xc = [P.alloc("xc%d" % i, (T,), F32) for i in range(2)]
        qT = P.alloc("qT", (16, T), BF16)
        oT = P.alloc("oT", (16, T), BF16)
        wg = P.alloc("wg", (16, 48), BF16)
        sgT = P.alloc("sgT", (T,), F32)
        Sel = P.alloc("Sel", (48, 128), F32)
        Em = P.alloc("Em", (L,), BF16)
        cover = P.alloc("cover", (NIC, 64), F32)
        onesT = P.alloc("onesT", (T,), BF16)
        NWC = 512 // 128 + NTC
        wmask = P.alloc("wmask", (NWC, T), BF16)
        cmask = P.alloc("cmask", (NIC, T), BF16)
        kw_s = P.alloc("kw_s", (NWC * 128,), BF16)
        vw_s = P.alloc("vw_s", (NWC, 128), BF16)
        ks_s = P.alloc("ks_s", (L,), BF16)
        vs_s = P.alloc("vs_s", (L // 128, 128), BF16)
        pTt = [P.alloc("pT%d" % i, (T,), BF16) for i in range(6)]
        pmt = [P.alloc("pm%d" % i, (T,), BF16) for i in range(6)]
        mskt = [P.alloc("msk%d" % i, (T,), BF16) for i in range(2)]
        pcs = P.alloc("pcs", (NIC, T), F32)
        pcn = P.alloc("pcn", (T,), F32)
        rden = P.alloc("rden", (T,), F32)
        t1 = P.alloc("t1", (T,), F32)
        t2 = P.alloc("t2", (T,), F32)
        oacc = P.alloc("oacc", (T,), F32)
        imp = P.alloc("imp", (64,), F32)
        s1 = P.alloc("s1", (64,), F32)
        s2 = P.alloc("s2", (64,), F32)
        s3 = P.alloc("s3", (64,), F32)
        m8 = P.alloc("m8", (16,), F32)
        selb = P.alloc("selb", (64,), BF16)
        selT = P.alloc("selT", (4, T), BF16)
        P.dma("sync", wg.ap, win[:, :, 5120:5168], reads=[wib], writes=[wg])
        P.op("gpsimd", lambda e: e.memset(onesT.ap, 1.0), writes=[onesT])
        P.op("gpsimd", lambda e: e.memset(Sel.ap, 1.0), writes=[Sel])
        P.op("gpsimd", lambda e: e.affine_select(out=Sel.ap[0:48], in_=Sel.ap[0:48], pattern=[[1, 48], [0, 128]],
                                                 compare_op=ALU.is_equal, fill=fillreg(e, 0.0), base=0, channel_multiplier=-1),
             reads=[Sel], writes=[Sel])
        P.op("gpsimd", lambda e: e.memset(Em.ap, 1.0), writes=[Em])
        P.op("gpsimd", lambda e: e.affine_select(out=Em.ap[0:NSL], in_=Em.ap[0:NSL], pattern=[[1, L]], compare_op=ALU.is_ge,
                                                 fill=fillreg(e, 0.0), base=0, channel_multiplier=-64), reads=[Em], writes=[Em])
        P.op("gpsimd", lambda e: e.affine_select(out=Em.ap[0:NSL], in_=Em.ap[0:NSL], pattern=[[-1, L]], compare_op=ALU.is_ge,
                                                 fill=fillreg(e, 0.0), base=63, channel_multiplier=64), reads=[Em], writes=[Em])
        P.op("gpsimd", lambda e: e.memset(cover.ap, 1.0), writes=[cover])
        for ic in range(NIC):
            P.op("gpsimd", lambda e, ic=ic: e.affine_select(out=cover[:, ic, :], in_=cover[:, ic, :], pattern=[[-64, 64]],
                                                            compare_op=ALU.is_ge, fill=fillreg(e, 0.0), base=16 * 128 * ic + 31,
                                                            channel_multiplier=16), reads=[cover], writes=[cover])
            P.op("gpsimd", lambda e, ic=ic: e.affine_select(out=cover[:, ic, :], in_=cover[:, ic, :], pattern=[[64, 64]],
                                                            compare_op=ALU.is_ge, fill=fillreg(e, 0.0), base=63 - 16 * 128 * ic,
                                                            channel_multiplier=-16), reads=[cover], writes=[cover])
        for o in range(NWC):
            P.op("gpsimd", lambda e, o=o: e.affine_select(out=wmask[:, o, :], in_=onesT.ap, pattern=[[1, T]],
                                                          compare_op=ALU.is_ge, fill=fillreg(e, 0.0), base=512 - 128 * o,
                                                          channel_multiplier=-1), reads=[onesT], writes=[wmask])
            P.op("gpsimd", lambda e, o=o: e.affine_select(out=wmask[:, o, :], in_=wmask[:, o, :], pattern=[[-1, T]],
                                                          compare_op=ALU.is_gt, fill=fillreg(e, 0.0), base=128 * o,
                                                          channel_multiplier=1), reads=[wmask], writes=[wmask])
        R = Rot(P, wsl, banks=(5, 6, 7))
        tiny = 1e-30
        if T <= 256:
            sslots = []
            for b_ in (5, 6, 7):
                for hh_ in range(2):
                    sslots.append(Tile(P.psum[b_].ap[:, hh_ * 256:(hh_ + 1) * 256], P.psum[b_].buf, hh_))
        else:
            sslots = [P.psum[b_] for b_ in (5, 6, 7)]
        sidx = [0]

        def sbank():
            s_ = sslots[sidx[0] % len(sslots)]
            sidx[0] += 1
            return s_

        def finish_branch(hq, br, den, o, first, last):
            P.op("vector", lambda e: e.tensor_scalar(out=rden.ap, in0=den[:, 0:T], scalar1=tiny, scalar2=None, op0=ALU.add),
                 reads=[den], writes=[rden])
            P.op("vector", lambda e: e.reciprocal(out=rden.ap, in_=rden.ap), reads=[rden], writes=[rden])
            gb = R.bank()
            col = hq * 3 + br
            P.op("tensor", lambda e: e.matmul(gb[:, 0:T], lhsT=Sel[0:48, col, :], rhs=sgT[0:48, :], start=True, stop=True),
                 reads=[Sel, sgT], writes=[gb])
            P.op("vector", lambda e: e.tensor_tensor(out=t1.ap, in0=rden.ap, in1=gb[:, 0:T], op=ALU.mult), reads=[rden, gb],
                 writes=[t1])
            if first:
                P.op("vector", lambda e: e.tensor_tensor(out=oacc.ap, in0=t1.ap, in1=o[:, 0:T], op=ALU.mult), reads=[t1, o],
                     writes=[oacc])
            else:
                P.op("vector", lambda e: e.tensor_tensor(out=t2.ap, in0=t1.ap, in1=o[:, 0:T], op=ALU.mult), reads=[t1, o],
                     writes=[t2])
                if last:
                    P.op("vector", lambda e: e.tensor_tensor(out=oT[:, hq, :], in0=oacc.ap, in1=t2.ap, op=ALU.add),
                         reads=[oacc, t2], writes=[oT.s(hq)])
                else:
                    P.op("vector", lambda e: e.tensor_tensor(out=oacc.ap, in0=oacc.ap, in1=t2.ap, op=ALU.add),
                         reads=[oacc, t2], writes=[oacc])

        cnt = [0]

        pend = []
        LAG = 2

        def drain(n):
            while len(pend) > n:
                pend.pop(0)()

        def defer(fn):
            pend.append(fn)

        def attend(hq, keyT_ap, keyT_dep, val_ap_fn, val_dep, mask_ap, mask_dep, den, o, first, last):
            r = cnt[0] % 4
            cnt[0] += 1
            ps = R.bank()
            pt, pm = pTt[r], pmt[r]
            P.op("tensor", lambda e: e.matmul(ps[:, 0:T], lhsT=keyT_ap, rhs=qT[:, hq, :], start=True, stop=True),
                 reads=[keyT_dep, qT.s(hq)], writes=[ps])
            P.op("scalar", lambda e: e.activation(out=pt.ap, in_=ps[:, 0:T], func=AF.Exp, scale=scale), reads=[ps], writes=[pt])
            P.op("vector", lambda e: e.tensor_tensor(out=pm.ap, in0=pt.ap, in1=mask_ap, op=ALU.mult), reads=[pt, mask_dep],
                 writes=[pm])

            def part2():
                P.op("tensor", lambda e: e.matmul(den[:, 0:T], lhsT=C.ones_b.ap, rhs=pm.ap, start=first, stop=last),
                     reads=[pm, C.ones_b], writes=[den])
                P.op("tensor", lambda e: e.matmul(o[:, 0:T], lhsT=val_ap_fn, rhs=pm.ap, start=first, stop=last),
                     reads=[pm, val_dep], writes=[o])

            defer(part2)
            drain(LAG)
            return pm

        for ti in range(nt):
            t0 = ti * T
            pre_norm(P, C, x_in, xin_buf, t0, T, O["pre%d_0" % li], big, hT, sq, rstd, P.psum[0])

            def put_q(j, ps):
                P.op("scalar", lambda e: e.activation(out=qT[:, j, :], in_=ps[:, 0:T], func=AF.Copy), reads=[ps],
                     writes=[qT.s(j)])

            gemm_fm(P, C, R, hT, 16, win, wib, 0, 16, T, put_q)
            ps = R.bank()
            for kc in range(16):
                P.op("tensor", lambda e, kc=kc, ps=ps: e.matmul(ps[0:48, 0:T], lhsT=wg[:, kc, :], rhs=hT[:, kc, :],
                                                              start=(kc == 0), stop=(kc == 15)), reads=[wg, hT.s(kc)],
                     writes=[ps])
            P.op("scalar", lambda e, ps=ps: e.activation(out=sgT[0:48, :], in_=ps[0:48, 0:T], func=AF.Sigmoid), reads=[ps],
                 writes=[sgT])
            nic_t = 0
            for ic in range(NIC):
                if 16 * 128 * ic + 31 <= t0 + T - 1:
                    nic_t = ic + 1
                    P.op("gpsimd", lambda e, ic=ic, t0=t0: e.affine_select(out=cmask[:, ic, :], in_=onesT.ap, pattern=[[1, T]],
                                                                    compare_op=ALU.is_ge, fill=fillreg(e, 0.0),
                                                                    base=t0 - 31 - 16 * 128 * ic, channel_multiplier=-16),
                         reads=[onesT], writes=[cmask.s(ic)])
            nkc = (t0 + T) // 128
            w_lo = max(0, (t0 - 512) // 128)
            for g in range(4):
                nw = nkc - w_lo
                P.dma("sync", kw_s[:, 0:nw * 128], kvT_d[3, g, :, w_lo * 128:nkc * 128], reads=[kvT_b], writes=[kw_s])
                P.dma("sync", vw_s[:, 0:nw, :], vtm_d[1, w_lo * 128:nkc * 128, g * 128:(g + 1) * 128].rearrange(
                    "(c p) d -> p c d", p=128), reads=[vtm_b], writes=[vw_s])
                P.dma("sync", ks_s[:, 0:nkc * 128], kvT_d[2, g, :, 0:nkc * 128], reads=[kvT_b], writes=[ks_s])
                P.dma("sync", vs_s[:, 0:nkc, :], vtm_d[0, 0:nkc * 128, g * 128:(g + 1) * 128].rearrange(
                    "(c p) d -> p c d", p=128), reads=[vtm_b], writes=[vs_s])
                for j in range(4):
                    hq = 4 * g + j
                    den, o = P.psum[1 + hq % 2], P.psum[3 + hq % 2]
                    if nic_t == 0:
                        P.op("gpsimd", lambda e, hq=hq: e.memset(big[:, hq, :], 0.0), writes=[big.s(hq)])
                        continue
                    pms = []
                    for ic in range(nic_t):
                        pm = attend(hq, kcT[:, g, ic * 128:(ic + 1) * 128], kcT, vc_tm[:, ic, g, :], vc_tm, cmask[:, ic, :],
                                    cmask.s(ic), den, o, ic == 0, ic == nic_t - 1)
                        pms.append(pm)
                    drain(0)
                    finish_branch(hq, 0, den, o, True, False)
                    if ti == 0 and hq == 0:
                        dbg_dump(P, C, "cmask", cmask, (NIC, T), BF16)
                        dbg_dump(P, C, "pm0", pms[0], (T,), BF16)
                        dbg_dump(P, C, "rden0", rden, (T,), F32)
                        dbg_dump(P, C, "t10", t1, (T,), F32)
                        dbg_dump(P, C, "oacc0", oacc, (T,), F32)
                    for ic in range(nic_t):
                        if j == 0:
                            P.op("vector", lambda e, ic=ic, pm=pms[ic]: e.tensor_tensor(out=pcs[:, ic, :], in0=pm.ap, in1=rden.ap,
                                                                                     op=ALU.mult), reads=[pm, rden],
                                 writes=[pcs.s(ic)])
                        else:
                            P.op("vector", lambda e, ic=ic, pm=pms[ic]: e.tensor_tensor(out=pcn.ap, in0=pm.ap, in1=rden.ap,
                                                                                     op=ALU.mult), reads=[pm, rden], writes=[pcn])
                            P.op("vector", lambda e, ic=ic: e.tensor_tensor(out=pcs[:, ic, :], in0=pcs[:, ic, :], in1=pcn.ap,
                                                                            op=ALU.add), reads=[pcs.s(ic), pcn],
                                 writes=[pcs.s(ic)])
                    P.op("gpsimd", lambda e, hq=hq: e.tensor_copy(out=big[:, hq, :], in_=oacc.ap), reads=[oacc],
                         writes=[big.s(hq)])
                for tc in range(NTC):
                    ps = R.bank()
                    if nic_t == 0:
                        P.op("gpsimd", lambda e: e.memset(imp.ap, 0.0), writes=[imp])
                    else:
                        for ic in range(nic_t):
                            P.op("tensor", lambda e, ps=ps, ic=ic, tc=tc: e.matmul(
                                ps[:, 0:64], lhsT=pcs[:, ic, tc * 128:(tc + 1) * 128], rhs=cover[:, ic, :], start=(ic == 0),
                                stop=(ic == nic_t - 1)), reads=[pcs.s(ic), cover], writes=[ps])
                        P.op("scalar", lambda e, ps=ps: e.activation(out=imp.ap, in_=ps[:, 0:64], func=AF.Copy), reads=[ps],
                             writes=[imp])
                    tb = t0 + tc * 128
                    P.op("gpsimd", lambda e, tb=tb: e.affine_select(out=s1.ap, in_=imp.ap, pattern=[[-64, 64]],
                                                                    compare_op=ALU.is_ge, fill=fillreg(e, 100.0), base=tb - 192,
                                                                    channel_multiplier=1), reads=[imp], writes=[s1])
                    P.op("gpsimd", lambda e, tb=tb: e.affine_select(out=s2.ap, in_=s1.ap, pattern=[[-64, 64]],
                                                                    compare_op=ALU.is_ge, fill=fillreg(e, -1.0), base=tb,
                                                                    channel_multiplier=1), reads=[s1], writes=[s2])
                    P.op("gpsimd", lambda e: e.memset(s2[:, 0:1], 100.0), reads=[s2], writes=[s2])
                    P.op("vector", lambda e: e.max(out=m8[:, 0:8], in_=s2[:, 0:NSL]), reads=[s2], writes=[m8])
                    P.op("vector", lambda e: e.match_replace(out=s3[:, 0:NSL], in_to_replace=m8[:, 0:8], in_values=s2[:, 0:NSL],
                                                             imm_value=-2.0), reads=[s2, m8], writes=[s3])
                    P.op("vector", lambda e: e.max(out=m8[:, 8:16], in_=s3[:, 0:NSL]), reads=[s3, m8], writes=[m8])
                    P.op("vector", lambda e: e.tensor_scalar(out=selb.ap, in0=s2.ap, scalar1=m8[:, 15:16], scalar2=None,
                                                             op0=ALU.is_ge), reads=[s2, m8], writes=[selb])
                    ps2 = R.bank()
                    psb = ps2.ap.bitcast(BF16)
                    P.op("tensor", lambda e, psb=psb: e.transpose(out=psb[0:64, 0:128], in_=selb.ap, identity=C.ident.ap),
                         reads=[selb, C.ident], writes=[ps2])
                    P.op("scalar", lambda e, psb=psb, tc=tc, g=g: e.activation(out=selT[0:64, g, tc * 128:(tc + 1) * 128],
                                                                              in_=psb[0:64, 0:128], func=AF.Copy), reads=[ps2],
                         writes=[selT])
                def sel_mask(kc_, g=g, t0=t0):
                    ps = R.bank()
                    m = mskt[kc_ % 2]
                    P.op("tensor", lambda e: e.matmul(ps[:, 0:T], lhsT=Em[0:NSL, kc_ * 128:(kc_ + 1) * 128], rhs=selT[0:NSL, g, :],
                                                      start=True, stop=True), reads=[Em, selT], writes=[ps])
                    P.op("scalar", lambda e: e.activation(out=m.ap, in_=ps[:, 0:T], func=AF.Copy), reads=[ps], writes=[m])
                    if kc_ * 128 + 127 > t0:
                        P.op("gpsimd", lambda e: e.affine_select(out=m.ap, in_=m.ap, pattern=[[1, T]], compare_op=ALU.is_ge,
                                                                 fill=fillreg(e, 0.0), base=t0 - kc_ * 128, channel_multiplier=-1),
                             reads=[m], writes=[m])
                    return m

                dens = [P.psum[1], P.psum[2]]
                os_ = [P.psum[3], P.psum[4]]
                for jp in range(2):
                    for kc_ in range(nkc):
                        m = sel_mask(kc_)
                        for jj in range(2):
                            hq = 4 * g + 2 * jp + jj
                            attend(hq, ks_s[:, kc_ * 128:(kc_ + 1) * 128], ks_s, vs_s[:, kc_, :], vs_s, m.ap, m, dens[jj],
                                   os_[jj], kc_ == 0, kc_ == nkc - 1)
                    for jj in range(2):
                        hq = 4 * g + 2 * jp + jj

                        def fin_sel(hq=hq, jj=jj):
                            P.op("gpsimd", lambda e: e.tensor_copy(out=oacc.ap, in_=big[:, hq, :]), reads=[big.s(hq)],
                                 writes=[oacc])
                            finish_branch(hq, 1, dens[jj], os_[jj], False, False)

                        defer(fin_sel)
                        den, o = dens[jj], os_[jj]
                        for wi in range(nw):
                            kc_ = w_lo + wi
                            oidx = kc_ - (t0 - 512) // 128
                            attend(hq, kw_s[:, wi * 128:(wi + 1) * 128], kw_s, vw_s[:, wi, :], vw_s, wmask[:, oidx, :], wmask, den, o,
                                   wi == 0, wi == nw - 1)
                        defer(lambda hq=hq, den=den, o=o: finish_branch(hq, 2, den, o, False, True))
                drain(0)
            if ti == 0:
                dbg_dump(P, C, "oT", oT, (16, T), BF16)
                dbg_dump(P, C, "cmp", big, (16, T), F32)
                dbg_dump(P, C, "qT", qT, (16, T), BF16)
                dbg_dump(P, C, "sgT", sgT, (T,), F32)
                dbg_dump(P, C, "kcT", kcT, (4, NIC * 128), BF16)
                dbg_dump(P, C, "vc", vc_tm, (NIC, 4, 128), BF16)
                dbg_dump(P, C, "selT", selT, (4, T), BF16)
            out_proj(P, C, oT, 16, wout, C.wbuf["nsa_out"], wsl, R, big, sq, rstd, T)
            post_residual(P, C, big, rstd, x_in, xin_buf, x_out, xout_buf, t0, T, O["post%d_0" % li], xc)
    phase_c()


STAGES = {"nsa": nsa_stage, "ssd": ssd_stage, "ffn": ffn_stage, "xa": xa_stage, "pool": pool_stage, "sgu": sgu_stage}


def build(L, plan, wshapes=None, dbg=False):
    nc = bass.Bass("TRN2", target_bir_lowering=False)
    _FILL.clear()
    C = Ctx()
    offs, ncols, _ = pack_cols(None)
    C.off = offs
    xT = nc.dram_tensor("xT", [D, L], F32, kind="ExternalInput")
    cols_d = nc.dram_tensor("cols", [128, ncols], F32, kind="ExternalInput")
    yT = nc.dram_tensor("yT", [D, L], F32, kind="ExternalOutput")
    xs = nc.dram_tensor("xs", [D, L], F32, kind="Internal")
    xs1 = nc.dram_tensor("xs1", [D, L], F32, kind="Internal")
    roffs, nrows, _ = pack_rows(None)
    C.roff = roffs
    rows_d = nc.dram_tensor("rows", [nrows], F32, kind="ExternalInput")
    C.rows = rows_d.ap()
    C.sgu_wsT = nc.dram_tensor("sgu_wsT", [128, 8, 128], F32, kind="ExternalInput").ap()
    C.nsa_posT = nc.dram_tensor("nsa_posT", [128, 2, 32], F32, kind="ExternalInput").ap()
    C.nc = nc
    C.dbg = dbg
    memT = nc.dram_tensor("memT", [D, 256], F32, kind="ExternalInput")
    C.memT, C.memT_buf = memT.ap(), Buf("memT")
    needed = set()
    for stg in plan:
        needed |= set(STAGE_W[stg[0]])
    LAST_NEEDED[:] = sorted(needed)
    C.wb, C.wbuf, C.wcast = {}, {}, {}
    for name in sorted(needed):
        key, shp = WEIGHTS[name]
        src = nc.dram_tensor(key, list(shp), F32, kind="ExternalInput")
        dst = nc.dram_tensor(name + "_bf", list(shp), BF16, kind="Internal")
        C.wb[name] = dst.ap()
        C.wbuf[name] = Buf(name)
        C.wcast[name] = (src, dst, C.wbuf[name])
    P = Prog(nc)
    with ExitStack() as st:
        P.arena_words = 47000
        P.arena = st.enter_context(nc.sbuf_tensor("arena", [128, P.arena_words], F32))
        P.psum = []
        for i in range(8):
            t = st.enter_context(nc.psum_tensor("ps%d" % i, [128, 512], F32))
            P.psum.append(Tile(t[:, :], Buf("ps%d" % i)))
        setup_persistent(P, C, cols_d.ap(), ncols)
        cast_weights(P, C)
        P.barrier()
        P.arena_off = P.arena_base
        aps = {"xT": xT.ap(), "xs": xs.ap(), "xs1": xs1.ap(), "yT": yT.ap()}
        for stg in plan:
            kind, li, src, dst = stg
            STAGES[kind](P, C, li, aps[src], Buf(src), aps[dst], Buf(dst), L)
            P.barrier()
            P.arena_off = P.arena_base
        P.emit(st)
    return nc


FULL_PLAN = []
_kinds = ["ssd", "nsa", "sgu", "pool"]
_cur = "xT"
for _i in range(DEPTH):
    for _k in (_kinds[_i], "xa", "ffn"):
        _last = (_i == DEPTH - 1 and _k == "ffn")
        _dst = "yT" if _last else ("xs" if _cur != "xs" else "xs1")
        FULL_PLAN.append((_k, _i, _cur, _dst))
        _cur = _dst

_NC_CACHE = {}


def kernel(**inputs):
    inp = {k: np.asarray(v) for k, v in inputs.items()}
    B, L, _ = inp["x"].shape
    if L not in _NC_CACHE:
        _NC_CACHE[L] = (build(L, FULL_PLAN), list(LAST_NEEDED))
    nc, needed = _NC_CACHE[L]
    in_maps = [make_in_map(inp, b, L, needed) for b in range(B)]
    res = run_bass_kernel_spmd(nc, in_maps, core_ids=list(range(B)))
    out = np.stack([np.asarray(res.results[b]["yT"]).T for b in range(B)], axis=0)
    return np.ascontiguousarray(out.astype(np.float32))
```

```python
from contextlib import ExitStack
import numpy as np
import concourse.bass as bass
import concourse.mybir as mybir
from concourse.bass_utils import run_bass_kernel_spmd

F32 = mybir.dt.float32
BF16 = mybir.dt.bfloat16
AF = mybir.ActivationFunctionType
ALU = mybir.AluOpType

D = 2048
NCH = 16
DEPTH = 4
FFN_H = 5632
FFN_HC = 44
EPS = 1e-6
ENGS = ("tensor", "vector", "scalar", "gpsimd", "sync")
NDMA = 8


_FILL = {}


def fillreg(e, v):
    k = (id(e), float(v))
    if k not in _FILL:
        _FILL[k] = e.to_reg(float(v))
    return _FILL[k]


class Buf:
    __slots__ = ("name", "recs")

    def __init__(self, name):
        self.name = name
        self.recs = []


def _norm(x):
    if isinstance(x, Tile):
        return (x.buf, x.sub)
    return x if isinstance(x, tuple) else (x, None)


class Tile:
    __slots__ = ("ap", "buf", "sub")

    def __init__(self, ap, buf, sub=None):
        self.ap = ap
        self.buf = buf
        self.sub = sub

    def __getitem__(self, k):
        return self.ap[k]

    def s(self, sub):
        return (self.buf, sub)


class Prog:
    def __init__(self, nc):
        self.nc = nc
        self.code = {e: [] for e in ENGS}
        self.cnt = {e: 0 for e in ENGS}
        self.known = {e: {} for e in ENGS}
        self.dma_cnt = {}
        self.dma_rr = {e: 0 for e in ENGS}
        self.semkeys = list(ENGS)
        for e in ("sync", "gpsimd", "scalar"):
            for j in range(NDMA):
                self.semkeys.append("d_%s_%d" % (e, j))
        self.arena_off = 0
        self.arena = None
        self.psum = None

    def _waits(self, eng, reads, writes):
        toks = {}
        for (buf, sub), is_w in [(r, False) for r in reads] + [(w, True) for w in writes]:
            for rsub, rw, sk, val in buf.recs:
                if (rsub is None or sub is None or rsub == sub) and (is_w or rw):
                    if toks.get(sk, 0) < val:
                        toks[sk] = val
        kn = self.known[eng]
        for sk, val in toks.items():
            if sk == "tensor" and eng == "tensor":
                continue
            if kn.get(sk, 0) >= val:
                continue
            kn[sk] = val
            self.code[eng].append(("w", sk, val))

    def _record(self, tok, reads, writes):
        sk, val = tok
        for buf, sub in writes:
            if sub is None:
                buf.recs = [[None, True, sk, val]]
            else:
                buf.recs = [r for r in buf.recs if r[0] != sub]
                buf.recs.append([sub, True, sk, val])
        for buf, sub in reads:
            for r in buf.recs:
                if r[0] == sub and (not r[1]) and r[2] == sk:
                    r[3] = val
                    break
            else:
                buf.recs.append([sub, False, sk, val])

    def op(self, eng, fn, reads=(), writes=()):
        reads = [_norm(r) for r in reads]
        writes = [_norm(w) for w in writes]
        self._waits(eng, reads, writes)
        self.cnt[eng] += 1
        self.code[eng].append(("o", fn, eng, 1))
        self._record((eng, self.cnt[eng]), reads, writes)

    def dma(self, eng, out, in_, reads=(), writes=(), **kw):
        reads = [_norm(r) for r in reads]
        writes = [_norm(w) for w in writes]
        self._waits(eng, reads, writes)
        j = self.dma_rr[eng]
        self.dma_rr[eng] = (j + 1) % NDMA
        sk = "d_%s_%d" % (eng, j)
        prev = self.dma_cnt.get(sk, 0)
        if prev and self.known[eng].get(sk, 0) < prev:
            self.known[eng][sk] = prev
            self.code[eng].append(("w", sk, prev))
        self.dma_cnt[sk] = prev + 16
        self.code[eng].append(("o", lambda e: e.dma_start(out=out, in_=in_, **kw), sk, 16))
        self._record((sk, prev + 16), reads, writes)

    def barrier(self):
        for e in ENGS:
            kn = self.known[e]
            for o in ENGS:
                if o != e and self.cnt[o] > kn.get(o, 0):
                    kn[o] = self.cnt[o]
                    self.code[e].append(("w", o, self.cnt[o]))
            for sk, v in self.dma_cnt.items():
                if v > kn.get(sk, 0):
                    kn[sk] = v
                    self.code[e].append(("w", sk, v))
        self.arena_off = 0

    def alloc(self, name, shape_free, dtype):
        n = 1
        for s in shape_free:
            n *= s
        nbytes = n * (2 if dtype == BF16 else 4)
        nw = (nbytes + 3) // 4
        nw = (nw + 7) // 8 * 8
        off = self.arena_off
        assert off + nw <= self.arena_words, ("arena overflow", name, off, nw)
        self.arena_off = off + nw
        ap = self.arena[:, off:off + nw]
        if dtype == BF16:
            ap = ap.bitcast(BF16)[:, 0:n]
        else:
            ap = ap[:, 0:n]
        if len(shape_free) == 2:
            ap = ap.rearrange("p (a b) -> p a b", a=shape_free[0])
        elif len(shape_free) == 3:
            ap = ap.rearrange("p (a b c) -> p a b c", a=shape_free[0], b=shape_free[1])
        return Tile(ap, Buf(name))

    def emit(self, st):
        nc = self.nc
        sems = {k: st.enter_context(nc.semaphore(k)) for k in self.semkeys}
        self.barrier()
        block = st.enter_context(nc.Block())
        for eng in ENGS:
            code = self.code[eng]

            def body(e, code=code):
                for it in code:
                    if it[0] == "w":
                        e.wait_ge(sems[it[1]], it[2])
                    else:
                        it[1](e).then_inc(sems[it[2]], it[3])

            getattr(block, eng)(body)


def colT(v):
    v = np.asarray(v, np.float32).reshape(-1)
    assert v.size % 128 == 0
    return np.ascontiguousarray(v.reshape(-1, 128).T)


def pack_cols(inp):
    offs = {}
    parts = []
    pos = [0]

    def add(name, n, arr_fn):
        offs[name] = pos[0]
        pos[0] += n
        if inp is not None:
            a = arr_fn()
            assert a.shape == (128, n), (name, a.shape, n)
            parts.append(a)

    for i in range(DEPTH):
        for j in range(3):
            add("pre%d_%d" % (i, j), 16, lambda: colT(inp["norm_pre"][i, j]))
            add("post%d_%d" % (i, j), 16, lambda: colT(inp["norm_post"][i, j]))
        add("nmem%d" % i, 16, lambda: colT(inp["norm_mem"][i]))
        for k in range(3):
            add("fcw%d_%d" % (i, k), 88, lambda: colT(inp["ffn_conv_w"][i, k]))
        add("fcb%d" % i, 88, lambda: colT(inp["ffn_conv_b"][i]))
    add("pscale", 16, lambda: colT(inp["pool_scale"][0]))
    add("sgu_binu", 32, lambda: colT(inp["sgu_b_in"][0][:4096]))
    add("sgu_lngc", 32, lambda: colT(inp["sgu_ln_g"][0]))
    add("ssd_cw", 4 * 48, lambda: np.concatenate([colT(inp["ssd_conv_w"][0, k]) for k in range(4)], axis=1))
    add("ssd_cb", 48, lambda: colT(inp["ssd_conv_b"][0]))
    add("ssd_ng", 32, lambda: colT(inp["ssd_norm_g"][0]))
    add("ssd_dcol", 32, lambda: colT(np.repeat(inp["ssd_d"][0], 64)))
    arr = np.ascontiguousarray(np.concatenate(parts, axis=1)) if inp is not None else None
    return offs, pos[0], arr


def pack_rows(inp):
    offs = {}
    parts = []
    pos = [0]

    def add(name, n, fn):
        offs[name] = pos[0]
        pos[0] += n
        if inp is not None:
            a = np.asarray(fn(), np.float32).reshape(-1)
            assert a.size == n, (name, a.size, n)
            parts.append(a)

    add("sgu_binv", 4096, lambda: inp["sgu_b_in"][0][4096:])
    add("sgu_lng", 4096, lambda: inp["sgu_ln_g"][0])
    add("sgu_bsp", 1024, lambda: inp["sgu_b_spatial"][0])
    add("ssd_dtb", 64, lambda: inp["ssd_dt_bias"][0])
    add("ssd_alog", 64, lambda: inp["ssd_a_log"][0])
    arr = np.ascontiguousarray(np.concatenate(parts)) if inp is not None else None
    return offs, pos[0], arr


WEIGHTS = {
    "ffn_up": ("ffn_w_up", (DEPTH, D, 2 * FFN_H)),
    "ffn_dn": ("ffn_w_down", (DEPTH, FFN_H, D)),
    "xa_q": ("xa_w_q", (DEPTH, D, 512)),
    "xa_kv": ("xa_w_kv", (DEPTH, D, 1024)),
    "xa_o": ("xa_w_o", (DEPTH, 512, D)),
    "ssd_in": ("ssd_w_in", (1, D, 10304)),
    "ssd_out": ("ssd_w_out", (1, 4096, D)),
    "nsa_in": ("nsa_w_in", (1, D, 5168)),
    "nsa_w1": ("nsa_cmp_w1", (1, 2, 4096, 256)),
    "nsa_w2": ("nsa_cmp_w2", (1, 2, 256, 128)),
    "nsa_out": ("nsa_w_out", (1, D, D)),
    "sgu_in": ("sgu_w_in", (1, D, 8192)),
    "sgu_out": ("sgu_w_out", (1, 4096, D)),
    "pool_in": ("pool_w_in", (1, D, D)),
    "pool_grp": ("pool_w_group", (1, 4, 512, 512)),
    "pool_out": ("pool_w_out", (1, D, D)),
}
STAGE_W = {
    "ffn": ["ffn_up", "ffn_dn"],
    "xa": ["xa_q", "xa_kv", "xa_o"],
    "ssd": ["ssd_in", "ssd_out"],
    "nsa": ["nsa_in", "nsa_w1", "nsa_w2", "nsa_out"],
    "sgu": ["sgu_in", "sgu_out"],
    "pool": ["pool_in", "pool_grp", "pool_out"],
}
LAST_NEEDED = []


def make_in_map(inp, b, L, nc_needed):
    _, _, cols = pack_cols(inp)
    _, _, rows = pack_rows(inp)
    m = {"xT": np.ascontiguousarray(inp["x"][b, :L].T), "memT": np.ascontiguousarray(inp["mem"][b].T), "cols": cols,
         "rows": rows,
         "sgu_wsT": np.ascontiguousarray(inp["sgu_w_spatial"][0].transpose(2, 0, 1)),
         "nsa_posT": np.ascontiguousarray(inp["nsa_cmp_pos"][0].transpose(2, 0, 1))}
    for name in nc_needed:
        m[WEIGHTS[name][0]] = np.ascontiguousarray(inp[WEIGHTS[name][0]])
    return m


def xa_stage(P, C, li, x_in, xin_buf, x_out, xout_buf, L):
    T = min(512, L)
    nt = L // T
    O = C.off
    hT = P.alloc("hT", (16, T), BF16)
    big = P.alloc("big", (16, T), F32)
    wsl = [P.alloc("w%d" % i, (8192,), BF16) for i in range(3)]
    sq = P.alloc("sq", (2, T), F32)
    rstd = P.alloc("rstd", (T,), F32)
    xc = [P.alloc("xc%d" % i, (T,), F32) for i in range(2)]
    qT = P.alloc("qT", (4, T), BF16)
    oT = P.alloc("oT", (4, T), BF16)
    KT = P.alloc("KT", (4, 256), BF16)
    V = P.alloc("V", (2, 512), BF16)
    pT = [P.alloc("pT%d" % i, (2, T), BF16) for i in range(2)]
    rden = P.alloc("rden", (T,), F32)
    wq = C.wb["xa_q"][li].rearrange("(kc p) f -> p kc f", p=128)
    wkv = C.wb["xa_kv"][li].rearrange("(kc p) f -> p kc f", p=128)
    wo = C.wb["xa_o"][li].rearrange("(kc p) f -> p kc f", p=128)
    R = Rot(P, wsl, banks=(1, 2, 3))
    scale = 128.0 ** -0.5
    pre_norm(P, C, C.memT, C.memT_buf, 0, 256, O["nmem%d" % li], big, hT, sq, rstd, P.psum[0])

    def put_kt(j, ps):
        P.op("scalar", lambda e: e.activation(out=KT[:, j, :], in_=ps[:, 0:256], func=AF.Copy), reads=[ps],
             writes=[KT.s(j)])

    gemm_fm(P, C, R, hT, 16, wkv, C.wbuf["xa_kv"], 0, 4, 256, put_kt)
    slot = R.slot()
    sv = slot.ap.rearrange("p (k f) -> p k f", k=16)
    for a, b in ((0, 8), (8, 16)):
        P.dma("sync", sv[:, a:b, :], wkv[:, a:b, 512:1024], reads=[C.wbuf["xa_kv"]], writes=[slot])
    for mc in range(2):
        ps = R.bank()
        for kc in range(16):
            P.op("tensor", lambda e, kc=kc, ps=ps, mc=mc: e.matmul(ps[:, 0:512], lhsT=hT[:, kc, mc * 128:(mc + 1) * 128],
                                                                 rhs=sv[:, kc, :], start=(kc == 0), stop=(kc == 15)),
                 reads=[slot, hT.s(kc)], writes=[ps])
        P.op("scalar", lambda e, ps=ps, mc=mc: e.activation(out=V[:, mc, :], in_=ps[:, 0:512], func=AF.Copy),
             reads=[ps], writes=[V.s(mc)])
    for ti in range(nt):
        t0 = ti * T
        pre_norm(P, C, x_in, xin_buf, t0, T, O["pre%d_1" % li], big, hT, sq, rstd, P.psum[0])

        def put_q(j, ps):
            P.op("scalar", lambda e: e.activation(out=qT[:, j, 0:T], in_=ps[:, 0:T], func=AF.Copy), reads=[ps],
                 writes=[qT.s(j)])

        gemm_fm(P, C, R, hT, 16, wq, C.wbuf["xa_q"], 0, 4, T, put_q)
        def xa_p1(h):
            pt = pT[h % 2]
            for mc in range(2):
                ps = R.bank()
                P.op("tensor", lambda e, ps=ps, mc=mc, h=h: e.matmul(ps[:, 0:T], lhsT=KT[:, h, mc * 128:(mc + 1) * 128],
                                                                   rhs=qT[:, h, 0:T], start=True, stop=True),
                     reads=[KT.s(h), qT.s(h)], writes=[ps])
                P.op("scalar", lambda e, ps=ps, mc=mc, pt=pt: e.activation(out=pt[:, mc, 0:T], in_=ps[:, 0:T], func=AF.Exp,
                                                                          scale=scale), reads=[ps], writes=[pt.s(mc)])

        def xa_p2(h):
            den = P.psum[4 + h % 2]
            o = P.psum[6 + h % 2]
            pt = pT[h % 2]
            for mc in range(2):
                P.op("tensor", lambda e, mc=mc, pt=pt, den=den: e.matmul(den[:, 0:T], lhsT=C.ones_b.ap, rhs=pt[:, mc, 0:T],
                                                                       start=(mc == 0), stop=(mc == 1)),
                     reads=[pt.s(mc), C.ones_b], writes=[den])
            for mc in range(2):
                P.op("tensor", lambda e, mc=mc, pt=pt, o=o, h=h: e.matmul(o[:, 0:T], lhsT=V[:, mc, h * 128:(h + 1) * 128],
                                                                        rhs=pt[:, mc, 0:T], start=(mc == 0), stop=(mc == 1)),
                     reads=[pt.s(mc), V.s(mc)], writes=[o])
            P.op("vector", lambda e, den=den: e.reciprocal(out=rden[:, 0:T], in_=den[:, 0:T]), reads=[den], writes=[rden])
            P.op("vector", lambda e, o=o, h=h: e.tensor_tensor(out=oT[:, h, 0:T], in0=o[:, 0:T], in1=rden[:, 0:T],
                                                               op=ALU.mult), reads=[o, rden], writes=[oT.s(h)])

        xa_p1(0)
        for h in range(4):
            if h + 1 < 4:
                xa_p1(h + 1)
            xa_p2(h)
        out_proj(P, C, oT, 4, wo, C.wbuf["xa_o"], wsl, R, big, sq, rstd, T)
        post_residual(P, C, big, rstd, x_in, xin_buf, x_out, xout_buf, t0, T, O["post%d_1" % li], xc)


class Ctx:
    pass


def setup_persistent(P, C, cols_dram, ncols):
    C.ones_f = P.alloc("ones_f", (128,), F32)
    P.op("gpsimd", lambda e: e.memset(C.ones_f.ap, 1.0), writes=[C.ones_f])
    C.ones_b = P.alloc("ones_b", (128,), BF16)
    P.op("gpsimd", lambda e: e.memset(C.ones_b.ap, 1.0), writes=[C.ones_b])
    C.eps = P.alloc("eps", (1,), F32)
    P.op("gpsimd", lambda e: e.memset(C.eps.ap, EPS), writes=[C.eps])
    C.trile = P.alloc("trile", (128,), F32)
    P.op("gpsimd", lambda e: e.affine_select(out=C.trile.ap, in_=C.ones_f.ap, pattern=[[1, 128]], compare_op=ALU.is_ge,
                                             fill=fillreg(e, 0.0), base=0, channel_multiplier=-1), reads=[C.ones_f], writes=[C.trile])
    C.su = P.alloc("su", (128,), F32)
    P.op("gpsimd", lambda e: e.affine_select(out=C.su.ap, in_=C.ones_f.ap, pattern=[[-1, 128]], compare_op=ALU.is_gt,
                                             fill=fillreg(e, 0.0), base=0, channel_multiplier=1), reads=[C.ones_f], writes=[C.su])
    C.ident = P.alloc("ident", (128,), BF16)
    P.op("gpsimd", lambda e: e.affine_select(out=C.ident.ap, in_=C.ones_f.ap, pattern=[[1, 128]],
                                             compare_op=ALU.is_equal, fill=fillreg(e, 0.0), base=0, channel_multiplier=-1),
         reads=[C.ones_f], writes=[C.ident])
    C.cols = P.alloc("cols", (ncols,), F32)
    P.dma("sync", C.cols.ap, cols_dram, writes=[C.cols])
    P.arena_base = P.arena_off


def cast_weights(P, C):
    for name, (src, dst, dbuf) in C.wcast.items():
        n = 1
        for s in src.shape:
            n *= s
        assert n % 2048 == 0
        rows = n // 2048
        sv = src.reshape([rows, 2048]).ap()
        dv = dst.reshape([rows, 2048]).ap()
        r = 0
        while r < rows:
            rr = min(4096, rows - r)
            P.dma("gpsimd", dv[r:r + rr, :], sv[r:r + rr, :], writes=[dbuf])
            r += rr


def rstd_from_ps(P, C, rstd, ps, T, n=D):
    P.op("scalar", lambda e: e.activation(out=rstd[:, 0:T], in_=ps[:, 0:T], func=AF.Sqrt, scale=1.0 / n,
                                          bias=C.eps.ap),
         reads=[ps, C.eps], writes=[rstd])
    P.op("vector", lambda e: e.reciprocal(out=rstd[:, 0:T], in_=rstd[:, 0:T]), reads=[rstd], writes=[rstd])


def pre_norm(P, C, x_dram, xbuf, t0, T, gcol0, xt, hT, sq, rstd, ps):
    xv = x_dram.rearrange("(c p) l -> p c l", p=128)
    for hh in range(2):
        P.dma("sync", xt[:, 8 * hh:8 * hh + 8, 0:T], xv[:, 8 * hh:8 * hh + 8, t0:t0 + T], reads=[xbuf],
              writes=[xt.s(c) for c in range(8 * hh, 8 * hh + 8)])
    for c in range(NCH):
        P.op("scalar", lambda e, c=c: e.activation(out=sq[:, c % 2, 0:T], in_=xt[:, c, 0:T], func=AF.Square),
             reads=[xt.s(c)], writes=[sq.s(c % 2)])
        P.op("tensor", lambda e, c=c: e.matmul(ps[:, 0:T], lhsT=C.ones_f.ap, rhs=sq[:, c % 2, 0:T],
                                               start=(c == 0), stop=(c == NCH - 1)),
             reads=[sq.s(c % 2), C.ones_f], writes=[ps])
    rstd_from_ps(P, C, rstd, ps, T)
    for c in range(NCH):
        P.op("vector", lambda e, c=c: e.scalar_tensor_tensor(out=hT[:, c, 0:T], in0=xt[:, c, 0:T],
                                                             scalar=C.cols[:, gcol0 + c:gcol0 + c + 1],
                                                             in1=rstd[:, 0:T], op0=ALU.mult, op1=ALU.mult),
             reads=[xt.s(c), rstd, C.cols], writes=[hT.s(c)])


def post_residual(P, C, mT, rstd, x_in, xin_buf, x_out, xout_buf, t0, T, gcol0, xc):
    xv = x_in.rearrange("(c p) l -> p c l", p=128)
    ov = x_out.rearrange("(c p) l -> p c l", p=128)
    for c in range(NCH):
        xcc = xc[c % 2]
        P.dma("sync", xcc[:, 0:T], xv[:, c, t0:t0 + T], reads=[xin_buf], writes=[xcc])
        P.op("vector", lambda e, c=c: e.scalar_tensor_tensor(out=mT[:, c, 0:T], in0=mT[:, c, 0:T],
                                                             scalar=C.cols[:, gcol0 + c:gcol0 + c + 1],
                                                             in1=rstd[:, 0:T], op0=ALU.mult, op1=ALU.mult),
             reads=[mT.s(c), rstd, C.cols], writes=[mT.s(c)])
        P.op("vector", lambda e, c=c, xcc=xcc: e.tensor_tensor(out=mT[:, c, 0:T], in0=mT[:, c, 0:T], in1=xcc[:, 0:T],
                                                               op=ALU.add),
             reads=[mT.s(c), xcc], writes=[mT.s(c)])
        P.dma("gpsimd", ov[:, c, t0:t0 + T], mT[:, c, 0:T], reads=[mT.s(c)], writes=[(xout_buf, (t0, c))])


def dbg_dump(P, C, name, tile, shape, dtype):
    if not getattr(C, "dbg", False):
        return
    d = C.nc.dram_tensor("dbg_" + name, [128] + list(shape), dtype, kind="ExternalOutput").ap()
    P.dma("gpsimd", d, tile.ap if isinstance(tile, Tile) else tile, reads=[tile] if isinstance(tile, Tile) else [],
          writes=[Buf("dbg")])


class Rot:
    def __init__(self, P, wsl, banks=(1, 2, 3, 4, 5, 6, 7)):
        self.P, self.wsl, self.banks = P, wsl, banks
        self.si, self.bi = 0, 0

    def slot(self):
        s = self.wsl[self.si % len(self.wsl)]
        self.si += 1
        return s

    def bank(self):
        b = self.P.psum[self.banks[self.bi % len(self.banks)]]
        self.bi += 1
        return b


def out_proj(P, C, gT, KC, wview, wbuf, wsl, st, big, sq, rstd, T):
    R = st if isinstance(st, Rot) else None
    stats = P.psum[0]
    nd = max(1, min(NCH, 8192 // (KC * 128)))
    for d0 in range(0, NCH, nd):
        slot = R.slot()
        sv = slot.ap[:, 0:KC * nd * 128].rearrange("p (k f) -> p k f", k=KC)
        nsplit = 2 if KC >= 2 else 1
        ks = [0, KC // 2, KC] if nsplit == 2 else [0, KC]
        for a, b in zip(ks[:-1], ks[1:]):
            P.dma("sync", sv[:, a:b, :], wview[:, a:b, d0 * 128:(d0 + nd) * 128], reads=[wbuf], writes=[slot])
        for dj in range(nd):
            dc = d0 + dj
            ps = R.bank()
            for kc in range(KC):
                P.op("tensor", lambda e, kc=kc, sv=sv, ps=ps, dj=dj: e.matmul(
                    ps[:, 0:T], lhsT=sv[:, kc, dj * 128:(dj + 1) * 128], rhs=gT[:, kc, 0:T],
                    start=(kc == 0), stop=(kc == KC - 1)), reads=[slot, gT.s(kc)], writes=[ps])
            P.op("scalar", lambda e, dc=dc, ps=ps: e.activation(out=big[:, dc, 0:T], in_=ps[:, 0:T], func=AF.Copy),
                 reads=[ps], writes=[big.s(dc)])
            P.op("scalar", lambda e, dc=dc, ps=ps: e.activation(out=sq[:, dc % 2, 0:T], in_=ps[:, 0:T], func=AF.Square),
                 reads=[ps], writes=[sq.s(dc % 2)])
            P.op("tensor", lambda e, dc=dc: e.matmul(stats[:, 0:T], lhsT=C.ones_f.ap, rhs=sq[:, dc % 2, 0:T],
                                                     start=(dc == 0), stop=(dc == NCH - 1)),
                 reads=[sq.s(dc % 2), C.ones_f], writes=[stats])
    rstd_from_ps(P, C, rstd, stats, T)


def gemm_fm(P, C, R, hT, KC, wview, wbuf, col0, nchunks, T, consume):
    per = max(1, 8192 // (KC * 128))
    j = 0
    while j < nchunks:
        n = min(per, nchunks - j)
        slot = R.slot()
        sv = slot.ap[:, 0:KC * n * 128].rearrange("p (k f) -> p k f", k=KC)
        ks = [0, KC // 2, KC]
        for a, b in zip(ks[:-1], ks[1:]):
            P.dma("sync", sv[:, a:b, :], wview[:, a:b, col0 + j * 128:col0 + (j + n) * 128], reads=[wbuf],
                  writes=[slot])
        for jj in range(n):
            ps = R.bank()
            for kc in range(KC):
                P.op("tensor", lambda e, kc=kc, sv=sv, ps=ps, jj=jj: e.matmul(
                    ps[:, 0:T], lhsT=sv[:, kc, jj * 128:(jj + 1) * 128], rhs=hT[:, kc, 0:T],
                    start=(kc == 0), stop=(kc == KC - 1)), reads=[slot, hT.s(kc)], writes=[ps])
            consume(j + jj, ps)
        j += n


def ffn_stage(P, C, li, x_in, xin_buf, x_out, xout_buf, L):
    T = min(512, L)
    nt = L // T
    O = C.off
    hT = P.alloc("hT", (16, T), BF16)
    big = P.alloc("big", (16, T), F32)
    act = P.alloc("act", (FFN_HC, T), BF16)
    wsl = [P.alloc("w%d" % i, (8192,), BF16) for i in range(3)]
    uext = [P.alloc("ue%d" % i, (T + 2,), F32) for i in range(4)]
    t2 = [P.alloc("t2%d" % i, (T,), F32) for i in range(4)]
    carry = P.alloc("carry", (88, 2), F32)
    sq = P.alloc("sq", (2, T), F32)
    rstd = P.alloc("rstd", (T,), F32)
    xc = [P.alloc("xc%d" % i, (T,), F32) for i in range(2)]
    wup = C.wb["ffn_up"][li].rearrange("(kc p) f -> p kc f", p=128)
    wdn = C.wb["ffn_dn"][li].rearrange("(kc p) f -> p kc f", p=128)
    wupb, wdnb = C.wbuf["ffn_up"], C.wbuf["ffn_dn"]
    st = Rot(P, wsl)
    rot = [0]
    next_bank = st.bank

    for ti in range(nt):
        t0 = ti * T
        pre_norm(P, C, x_in, xin_buf, t0, T, O["pre%d_2" % li], big, hT, sq, rstd, P.psum[0])

        def do_chunk(cidx, slot, j):
            ps = next_bank()
            sv = slot.ap.rearrange("p (k f) -> p k f", k=16)
            for kc in range(16):
                P.op("tensor", lambda e, kc=kc: e.matmul(ps[:, 0:T], lhsT=sv[:, kc, j * 128:(j + 1) * 128],
                                                         rhs=hT[:, kc, 0:T], start=(kc == 0), stop=(kc == 15)),
                     reads=[slot, hT.s(kc)], writes=[ps])
            r = rot[0] % 4
            rot[0] += 1
            ue, tt = uext[r], t2[r]
            if ti == 0:
                P.op("gpsimd", lambda e: e.memset(ue[:, 0:2], 0.0), writes=[ue])
            else:
                P.op("gpsimd", lambda e: e.tensor_copy(out=ue[:, 0:2], in_=carry[:, cidx, :]),
                     reads=[carry.s(cidx)], writes=[ue])
            P.op("scalar", lambda e: e.activation(out=ue[:, 2:T + 2], in_=ps[:, 0:T], func=AF.Copy),
                 reads=[ps], writes=[ue])
            w2 = O["fcw%d_2" % li] + cidx
            bb = O["fcb%d" % li] + cidx
            P.op("scalar", lambda e: e.activation(out=tt[:, 0:T], in_=ps[:, 0:T], func=AF.Identity,
                                                  scale=C.cols[:, w2:w2 + 1], bias=C.cols[:, bb:bb + 1]),
                 reads=[ps, C.cols], writes=[tt])
            P.op("gpsimd", lambda e: e.tensor_copy(out=carry[:, cidx, :], in_=ue[:, T:T + 2]),
                 reads=[ue], writes=[carry.s(cidx)])
            for k, sh in ((1, 1), (0, 0)):
                wk = O["fcw%d_%d" % (li, k)] + cidx
                P.op("vector", lambda e, wk=wk, sh=sh: e.scalar_tensor_tensor(
                    out=tt[:, 0:T], in0=ue[:, sh:sh + T], scalar=C.cols[:, wk:wk + 1], in1=tt[:, 0:T],
                    op0=ALU.mult, op1=ALU.add), reads=[ue, tt, C.cols], writes=[tt])
            return tt

        for grp in range(11):
            sg = st.slot()
            svw = st.slot()
            for slot, c0 in ((sg, grp * 512), (svw, FFN_H + grp * 512)):
                sv = slot.ap.rearrange("p (k f) -> p k f", k=16)
                for hh in range(2):
                    P.dma("sync", sv[:, 8 * hh:8 * hh + 8, :], wup[:, 8 * hh:8 * hh + 8, c0:c0 + 512],
                          reads=[wupb], writes=[slot])
            for j in range(4):
                fc = grp * 4 + j
                tg = do_chunk(fc, sg, j)
                tv = do_chunk(FFN_HC + fc, svw, j)
                P.op("scalar", lambda e, tg=tg: e.activation(out=tg[:, 0:T], in_=tg[:, 0:T], func=AF.Silu),
                     reads=[tg], writes=[tg])
                P.op("gpsimd", lambda e, tg=tg, tv=tv, fc=fc: e.tensor_tensor(out=act[:, fc, 0:T], in0=tg[:, 0:T],
                                                                             in1=tv[:, 0:T], op=ALU.mult),
                     reads=[tg, tv], writes=[act.s(fc)])
        out_proj(P, C, act, FFN_HC, wdn, wdnb, wsl, st, big, sq, rstd, T)
        post_residual(P, C, big, rstd, x_in, xin_buf, x_out, xout_buf, t0, T, O["post%d_2" % li], xc)


def row_bcast(P, C, tile, name, n):
    o = C.roff[name]
    P.dma("sync", tile.ap, C.rows[o:o + n].partition_broadcast(128), writes=[tile])


def pool_stage(P, C, li, x_in, xin_buf, x_out, xout_buf, L):
    T = min(512, L)
    nt = L // T
    O = C.off
    N = T + 15
    hT = P.alloc("hT", (16, T), BF16)
    big = P.alloc("big", (16, T), F32)
    wsl = [P.alloc("w%d" % i, (8192,), BF16) for i in range(2)]
    sq = P.alloc("sq", (2, T), F32)
    rstd = P.alloc("rstd", (T,), F32)
    xc = [P.alloc("xc%d" % i, (T,), F32) for i in range(2)]
    zext = [P.alloc("ze%d" % i, (N,), F32) for i in range(3)]
    sa = [P.alloc("sa%d" % i, (N,), F32) for i in range(3)]
    sb = [P.alloc("sb%d" % i, (N,), F32) for i in range(3)]
    pooled = P.alloc("pooled", (16, T), BF16)
    yT = P.alloc("yT", (16, T), BF16)
    zc = P.alloc("zc", (16, 15), F32)
    wg = P.alloc("wg", (4, 4, 512), BF16)
    win = C.wb["pool_in"][0].rearrange("(kc p) f -> p kc f", p=128)
    wout = C.wb["pool_out"][0].rearrange("(kc p) f -> p kc f", p=128)
    wgv = C.wb["pool_grp"][0].rearrange("g (dc p) e -> p g dc e", p=128)
    for g in range(4):
        P.dma("sync", wg[:, g, :, :], wgv[:, g, :, :], reads=[C.wbuf["pool_grp"]], writes=[wg])
    R = Rot(P, wsl)
    rot = [0]
    for ti in range(nt):
        t0 = ti * T
        pre_norm(P, C, x_in, xin_buf, t0, T, O["pre%d_0" % li], big, hT, sq, rstd, P.psum[0])

        def put_z(c, ps):
            r = rot[0] % 3
            rot[0] += 1
            ze, a, b = zext[r], sa[r], sb[r]
            w = (2, 4, 8, 16)[c // 4]
            if ti == 0:
                P.op("gpsimd", lambda e: e.memset(ze[:, 0:15], 0.0), writes=[ze])
            else:
                P.op("gpsimd", lambda e: e.tensor_copy(out=ze[:, 0:15], in_=zc[:, c, :]), reads=[zc.s(c)], writes=[ze])
            P.op("scalar", lambda e: e.activation(out=ze[:, 15:N], in_=ps[:, 0:T], func=AF.Copy), reads=[ps], writes=[ze])
            P.op("gpsimd", lambda e: e.tensor_copy(out=zc[:, c, :], in_=ze[:, T:N]), reads=[ze], writes=[zc.s(c)])
            cur, lo, sh = ze, 0, 1
            k = 0
            while sh < w:
                dst = a if k % 2 == 0 else b
                nlo = lo + sh
                P.op("vector", lambda e, cur=cur, dst=dst, nlo=nlo, sh=sh: e.tensor_tensor(
                    out=dst[:, nlo:N], in0=cur[:, nlo:N], in1=cur[:, nlo - sh:N - sh], op=ALU.add),
                    reads=[cur], writes=[dst])
                cur, lo, sh, k = dst, nlo, sh * 2, k + 1
            P.op("vector", lambda e, cur=cur: e.scalar_tensor_tensor(out=pooled[:, c, 0:T], in0=cur[:, 15:N], scalar=1.0 / w,
                                                                  in1=ze[:, 15:N], op0=ALU.mult, op1=ALU.subtract),
                 reads=[cur, ze], writes=[pooled.s(c)])
            if ti == 0:
                for t in range(w - 1):
                    P.op("vector", lambda e, cur=cur, t=t: e.scalar_tensor_tensor(
                        out=pooled[:, c, t:t + 1], in0=cur[:, 15 + t:16 + t], scalar=1.0 / (t + 1),
                        in1=ze[:, 15 + t:16 + t], op0=ALU.mult, op1=ALU.subtract), reads=[cur, ze], writes=[pooled.s(c)])

        gemm_fm(P, C, R, hT, 16, win, C.wbuf["pool_in"], 0, 16, T, put_z)
        for g in range(4):
            for ec in range(4):
                ps = R.bank()
                for dc in range(4):
                    P.op("tensor", lambda e, ps=ps, g=g, ec=ec, dc=dc: e.matmul(
                        ps[:, 0:T], lhsT=wg[:, g, dc, ec * 128:(ec + 1) * 128], rhs=pooled[:, 4 * g + dc, 0:T],
                        start=(dc == 0), stop=(dc == 3)), reads=[wg, pooled.s(4 * g + dc)], writes=[ps])
                sc = O["pscale"] + 4 * g + ec
                P.op("scalar", lambda e, ps=ps, g=g, ec=ec, sc=sc: e.activation(
                    out=yT[:, 4 * g + ec, 0:T], in_=ps[:, 0:T], func=AF.Identity, scale=C.cols[:, sc:sc + 1]),
                    reads=[ps, C.cols], writes=[yT.s(4 * g + ec)])
        out_proj(P, C, yT, 16, wout, C.wbuf["pool_out"], wsl, R, big, sq, rstd, T)
        post_residual(P, C, big, rstd, x_in, xin_buf, x_out, xout_buf, t0, T, O["post%d_0" % li], xc)


GELU_NATIVE = True


def gelu_inplace(P, x_ap, xt, tmp_ap, tmpt, eng2="gpsimd"):
    if GELU_NATIVE:
        P.op("scalar", lambda e: e.activation(out=x_ap, in_=x_ap, func=AF.Gelu_apprx_tanh), reads=[xt], writes=[xt])
        return
    P.op("scalar", lambda e: e.activation(out=tmp_ap, in_=x_ap, func=AF.Square), reads=[xt], writes=[tmpt])
    P.op("vector", lambda e: e.tensor_scalar(out=tmp_ap, in0=tmp_ap, scalar1=0.044715, scalar2=1.0, op0=ALU.mult,
                                             op1=ALU.add), reads=[tmpt], writes=[tmpt])
    P.op("vector", lambda e: e.tensor_tensor(out=tmp_ap, in0=tmp_ap, in1=x_ap, op=ALU.mult), reads=[tmpt, xt],
         writes=[tmpt])
    P.op("scalar", lambda e: e.activation(out=tmp_ap, in_=tmp_ap, func=AF.Sigmoid, scale=1.5957691216057308),
         reads=[tmpt], writes=[tmpt])
    P.op(eng2, lambda e: e.tensor_tensor(out=x_ap, in0=x_ap, in1=tmp_ap, op=ALU.mult), reads=[tmpt, xt], writes=[xt])


def sgu_stage(P, C, li, x_in, xin_buf, x_out, xout_buf, L):
    T = min(256, L)
    nt = L // T
    ntc = T // 128
    O = C.off
    hT = P.alloc("hT", (16, T), BF16)
    big = P.alloc("big", (16, T), F32)
    wsl = [P.alloc("w%d" % i, (8192,), BF16) for i in range(2)]
    sq = P.alloc("sq", (2, T), F32)
    rstd = P.alloc("rstd", (T,), F32)
    xc = [P.alloc("xc%d" % i, (T,), F32) for i in range(2)]
    uT = P.alloc("uT", (32, T), BF16)
    uf = [P.alloc("uf%d" % i, (T,), F32) for i in range(2)]
    ut = [P.alloc("ut%d" % i, (T if not GELU_NATIVE else 8,), F32) for i in range(2)]
    vtm = P.alloc("vtm", (ntc, 4096), F32)
    vtmp = P.alloc("vtmp", (512 if not GELU_NATIVE else 8,), F32)
    vn = P.alloc("vn", (ntc, 4096), BF16)
    gT = P.alloc("gT", (32, T), BF16)
    binb = P.alloc("binb", (4096,), F32)
    bsp = P.alloc("bsp", (8, 128), F32)
    wsf = P.alloc("wsf", (8, 128), F32)
    wmT = P.alloc("wmT", (8, 128), BF16)
    st4 = P.alloc("st4", (8,), F32)
    sp = [P.alloc("sp%d" % i, (128,), F32) for i in range(2)]
    row_bcast(P, C, binb, "sgu_binv", 4096)
    o = C.roff["sgu_bsp"]
    P.dma("sync", bsp.ap, C.rows[o:o + 1024].partition_broadcast(128).rearrange("p (g t) -> p g t", g=8), writes=[bsp])
    P.dma("sync", wsf.ap, C.sgu_wsT, writes=[wsf])
    P.op("gpsimd", lambda e: e.affine_select(out=wmT.ap, in_=wsf.ap, pattern=[[0, 8], [1, 128]], compare_op=ALU.is_ge,
                                             fill=fillreg(e, 0.0), base=0, channel_multiplier=-1), reads=[wsf], writes=[wmT])
    win = C.wb["sgu_in"][0].rearrange("(kc p) f -> p kc f", p=128)
    wout = C.wb["sgu_out"][0].rearrange("(kc p) f -> p kc f", p=128)
    R = Rot(P, wsl, banks=(1, 2, 3, 4, 5))
    rot = [0]
    for ti in range(nt):
        t0 = ti * T
        pre_norm(P, C, x_in, xin_buf, t0, T, O["pre%d_0" % li], big, hT, sq, rstd, P.psum[0])

        def put_u(j, ps):
            r = rot[0] % 2
            rot[0] += 1
            bc = O["sgu_binu"] + j
            P.op("scalar", lambda e: e.activation(out=uf[r][:, 0:T], in_=ps[:, 0:T], func=AF.Identity,
                                                  bias=C.cols[:, bc:bc + 1]), reads=[ps, C.cols], writes=[uf[r]])
            gelu_inplace(P, uf[r][:, 0:T], uf[r], ut[r].ap, ut[r])
            P.op("gpsimd", lambda e: e.tensor_copy(out=uT[:, j, 0:T], in_=uf[r][:, 0:T]), reads=[uf[r]], writes=[uT.s(j)])

        gemm_fm(P, C, R, hT, 16, win, C.wbuf["sgu_in"], 0, 32, T, put_u)
        for grp in range(8):
            slot = R.slot()
            sv = slot.ap.rearrange("p (k f) -> p k f", k=16)
            for a, b in ((0, 8), (8, 16)):
                P.dma("sync", sv[:, a:b, :], win[:, a:b, 4096 + grp * 512:4096 + (grp + 1) * 512],
                      reads=[C.wbuf["sgu_in"]], writes=[slot])
            for tc in range(ntc):
                ps = R.bank()
                for kc in range(16):
                    P.op("tensor", lambda e, kc=kc, ps=ps, tc=tc, sv=sv: e.matmul(
                        ps[:, 0:512], lhsT=hT[:, kc, tc * 128:(tc + 1) * 128], rhs=sv[:, kc, :], start=(kc == 0),
                        stop=(kc == 15)), reads=[slot, hT.s(kc)], writes=[ps])
                sub = tc * 8 + grp
                va = vtm[:, tc, grp * 512:(grp + 1) * 512]
                P.op("vector", lambda e, ps=ps, va=va, grp=grp: e.tensor_tensor(
                    out=va, in0=ps[:, 0:512], in1=binb[:, grp * 512:(grp + 1) * 512], op=ALU.add),
                    reads=[ps, binb], writes=[vtm.s(sub)])
                gelu_inplace(P, va, vtm.s(sub), vtmp.ap, vtmp)
        for tc in range(ntc):
            vv = vtm[:, tc, :]
            P.op("scalar", lambda e, vv=vv, tc=tc: e.activation(out=vn[:, tc, :], in_=vv, func=AF.Copy,
                                                              accum_out=st4[:, 0:1]), reads=[vtm], writes=[vn.s(tc), st4])
            P.op("scalar", lambda e, vv=vv, tc=tc: e.activation(out=vn[:, tc, :], in_=vv, func=AF.Square,
                                                              accum_out=st4[:, 1:2]), reads=[vtm, st4],
                 writes=[vn.s(tc), st4])
            P.op("vector", lambda e: e.tensor_scalar(out=st4[:, 2:3], in0=st4[:, 0:1], scalar1=1.0 / 4096, scalar2=None,
                                                     op0=ALU.mult), reads=[st4], writes=[st4])
            P.op("vector", lambda e: e.tensor_tensor(out=st4[:, 3:4], in0=st4[:, 2:3], in1=st4[:, 2:3], op=ALU.mult),
                 reads=[st4], writes=[st4])
            P.op("vector", lambda e: e.scalar_tensor_tensor(out=st4[:, 4:5], in0=st4[:, 1:2], scalar=1.0 / 4096,
                                                            in1=st4[:, 3:4], op0=ALU.mult, op1=ALU.subtract),
                 reads=[st4], writes=[st4])
            P.op("scalar", lambda e: e.activation(out=st4[:, 5:6], in_=st4[:, 4:5], func=AF.Sqrt, bias=C.eps.ap),
                 reads=[st4, C.eps], writes=[st4])
            P.op("vector", lambda e: e.reciprocal(out=st4[:, 5:6], in_=st4[:, 5:6]), reads=[st4], writes=[st4])
            P.op("vector", lambda e, vv=vv, tc=tc: e.tensor_scalar(out=vn[:, tc, :], in0=vv, scalar1=st4[:, 2:3],
                                                                   scalar2=st4[:, 5:6], op0=ALU.subtract, op1=ALU.mult),
                 reads=[vtm, st4], writes=[vn.s(tc)])
            for q4 in range(8):
                ps = R.bank()
                for jj in range(4):
                    dcv = q4 * 4 + jj
                    P.op("tensor", lambda e, ps=ps, jj=jj, dcv=dcv, tc=tc, q4=q4: e.matmul(
                        ps[:, jj * 128:(jj + 1) * 128], lhsT=vn[:, tc, dcv * 128:(dcv + 1) * 128], rhs=wmT[:, q4, :],
                        start=True, stop=True), reads=[vn.s(tc), wmT], writes=[ps])
                for jj in range(4):
                    dcv = q4 * 4 + jj
                    s = sp[(q4 * 4 + jj) % 2]
                    lc = O["sgu_lngc"] + dcv
                    P.op("vector", lambda e, ps=ps, jj=jj, s=s, q4=q4, lc=lc: e.scalar_tensor_tensor(
                        out=s.ap, in0=ps[:, jj * 128:(jj + 1) * 128], scalar=C.cols[:, lc:lc + 1], in1=bsp[:, q4, :],
                        op0=ALU.mult, op1=ALU.add), reads=[ps, bsp, C.cols], writes=[s])
                    P.op("gpsimd", lambda e, s=s, dcv=dcv, tc=tc: e.tensor_tensor(
                        out=gT[:, dcv, tc * 128:(tc + 1) * 128], in0=s.ap, in1=uT[:, dcv, tc * 128:(tc + 1) * 128],
                        op=ALU.mult), reads=[s, uT.s(dcv)], writes=[gT.s(dcv)])
        out_proj(P, C, gT, 32, wout, C.wbuf["sgu_out"], wsl, R, big, sq, rstd, T)
        post_residual(P, C, big, rstd, x_in, xin_buf, x_out, xout_buf, t0, T, O["post%d_0" % li], xc)


def ssd_stage(P, C, li, x_in, xin_buf, x_out, xout_buf, L):
    T = 128
    nt = L // T
    O = C.off
    hT = P.alloc("hT", (16, T), BF16)
    big = P.alloc("big", (16, T), F32)
    wsl = [P.alloc("w%d" % i, (8192,), BF16) for i in range(2)]
    sq = P.alloc("sq", (2, T), F32)
    rstd = P.alloc("rstd", (T,), F32)
    xc = [P.alloc("xc%d" % i, (T,), F32) for i in range(2)]
    szT = P.alloc("szT", (32, T), BF16)
    xsb = P.alloc("xsb", (32, T), BF16)
    BT = P.alloc("BT", (8, T), BF16)
    CT = P.alloc("CT", (8, T), BF16)
    ue = [P.alloc("ue%d" % i, (T + 3,), F32) for i in range(3)]
    tt = [P.alloc("tt%d" % i, (T,), F32) for i in range(3)]
    carry = P.alloc("carry", (48, 3), F32)
    wdt = P.alloc("wdt", (16, 64), BF16)
    dtb = P.alloc("dtb", (64,), F32)
    Aneg = P.alloc("Aneg", (64,), F32)
    d1 = P.alloc("d1", (64,), F32)
    d2 = P.alloc("d2", (64,), F32)
    dt = P.alloc("dt", (64,), F32)
    aa = P.alloc("aa", (64,), F32)
    dte = P.alloc("dte", (64,), F32)
    dec = P.alloc("dec", (64,), F32)
    x_tm = P.alloc("x_tm", (4096,), BF16)
    B_tm = P.alloc("B_tm", (1024,), BF16)
    xdt = P.alloc("xdt", (4096,), BF16)
    xdtw = P.alloc("xdtw", (4096,), BF16)
    cbm = P.alloc("cbm", (8, 128), BF16)
    R4 = [P.alloc("R4%d" % i, (4, 128), F32) for i in range(2)]
    LT = [P.alloc("LT%d" % i, (4, 128), BF16) for i in range(2)]
    MT = [P.alloc("MT%d" % i, (4, 128), BF16) for i in range(2)]
    Ed = [P.alloc("Ed%d" % i, (4, 128), BF16) for i in range(2)]
    Cd = [P.alloc("Cd%d" % i, (4, 128), BF16) for i in range(2)]
    yT = P.alloc("yT", (32, T), F32)
    S = P.alloc("S", (4096,), F32)
    prevT = P.alloc("prevT", (4096,), BF16)
    gT = P.alloc("gT", (32, T), BF16)
    win = C.wb["ssd_in"][0].rearrange("(kc p) f -> p kc f", p=128)
    wout = C.wb["ssd_out"][0].rearrange("(kc p) f -> p kc f", p=128)
    wib = C.wbuf["ssd_in"]
    P.dma("sync", wdt.ap, win[:, :, 10240:10304], reads=[wib], writes=[wdt])
    row_bcast(P, C, dtb, "ssd_dtb", 64)
    row_bcast(P, C, Aneg, "ssd_alog", 64)
    P.op("scalar", lambda e: e.activation(out=Aneg.ap, in_=Aneg.ap, func=AF.Exp), reads=[Aneg], writes=[Aneg])
    P.op("vector", lambda e: e.tensor_scalar(out=Aneg.ap, in0=Aneg.ap, scalar1=-1.0, scalar2=None, op0=ALU.mult),
         reads=[Aneg], writes=[Aneg])
    P.op("gpsimd", lambda e: e.memset(S.ap, 0.0), writes=[S])
    R = Rot(P, wsl)
    rot = [0]
    cwo, cbo, ngo, dco = O["ssd_cw"], O["ssd_cb"], O["ssd_ng"], O["ssd_dcol"]
    for ti in range(nt):
        t0 = ti * T
        pre_norm(P, C, x_in, xin_buf, t0, T, O["pre%d_0" % li], big, hT, sq, rstd, P.psum[0])
        P.op("gpsimd", lambda e: e.tensor_copy(out=prevT.ap, in_=S.ap), reads=[S], writes=[prevT])

        def put_z(j, ps):
            P.op("scalar", lambda e: e.activation(out=szT[:, j, :], in_=ps[:, 0:T], func=AF.Silu), reads=[ps],
                 writes=[szT.s(j)])

        gemm_fm(P, C, R, hT, 16, win, wib, 0, 32, T, put_z)

        def put_xbc(j, ps):
            r = rot[0] % 3
            rot[0] += 1
            u, t_ = ue[r], tt[r]
            if ti == 0:
                P.op("gpsimd", lambda e: e.memset(u[:, 0:3], 0.0), writes=[u])
            else:
                P.op("gpsimd", lambda e: e.tensor_copy(out=u[:, 0:3], in_=carry[:, j, :]), reads=[carry.s(j)], writes=[u])
            P.op("scalar", lambda e: e.activation(out=u[:, 3:T + 3], in_=ps[:, 0:T], func=AF.Copy), reads=[ps], writes=[u])
            P.op("scalar", lambda e: e.activation(out=t_.ap, in_=ps[:, 0:T], func=AF.Identity,
                                                  scale=C.cols[:, cwo + 3 * 48 + j:cwo + 3 * 48 + j + 1],
                                                  bias=C.cols[:, cbo + j:cbo + j + 1]), reads=[ps, C.cols], writes=[t_])
            P.op("gpsimd", lambda e: e.tensor_copy(out=carry[:, j, :], in_=u[:, T:T + 3]), reads=[u], writes=[carry.s(j)])
            for k in (2, 1, 0):
                wk = cwo + k * 48 + j
                P.op("vector", lambda e, k=k, wk=wk: e.scalar_tensor_tensor(
                    out=t_.ap, in0=u[:, k:k + T], scalar=C.cols[:, wk:wk + 1], in1=t_.ap, op0=ALU.mult, op1=ALU.add),
                    reads=[u, t_, C.cols], writes=[t_])
            if j < 32:
                dst, dd = xsb[:, j, :], xsb.s(j)
            elif j < 40:
                dst, dd = BT[:, j - 32, :], BT.s(j - 32)
            else:
                dst, dd = CT[:, j - 40, :], CT.s(j - 40)
            P.op("scalar", lambda e: e.activation(out=dst, in_=t_.ap, func=AF.Silu), reads=[t_], writes=[dd])

        gemm_fm(P, C, R, hT, 16, win, wib, 4096, 48, T, put_xbc)
        ps = R.bank()
        for kc in range(16):
            P.op("tensor", lambda e, kc=kc, ps=ps: e.matmul(ps[:, 0:64], lhsT=hT[:, kc, :], rhs=wdt[:, kc, :],
                                                          start=(kc == 0), stop=(kc == 15)),
                 reads=[hT.s(kc), wdt], writes=[ps])
        P.op("vector", lambda e, ps=ps: e.tensor_tensor(out=d1.ap, in0=ps[:, 0:64], in1=dtb.ap, op=ALU.add),
             reads=[ps, dtb], writes=[d1])
        P.op("scalar", lambda e: e.activation(out=d2.ap, in_=d1.ap, func=AF.Abs), reads=[d1], writes=[d2])
        P.op("scalar", lambda e: e.activation(out=d2.ap, in_=d2.ap, func=AF.Exp, scale=-1.0), reads=[d2], writes=[d2])
        P.op("scalar", lambda e: e.activation(out=d2.ap, in_=d2.ap, func=AF.Ln, bias=1.0), reads=[d2], writes=[d2])
        P.op("vector", lambda e: e.scalar_tensor_tensor(out=dt.ap, in0=d1.ap, scalar=0.0, in1=d2.ap, op0=ALU.max,
                                                        op1=ALU.add), reads=[d1, d2], writes=[dt])
        P.op("vector", lambda e: e.tensor_tensor(out=aa.ap, in0=dt.ap, in1=Aneg.ap, op=ALU.mult), reads=[dt, Aneg],
             writes=[aa])
        for q in range(5):
            ps = R.bank()
            psb = ps.ap.bitcast(BF16)
            for jj in range(8):
                j = q * 8 + jj
                src_ap = xsb[:, j, :] if j < 32 else BT[:, j - 32, :]
                sd = xsb.s(j) if j < 32 else BT.s(j - 32)
                P.op("tensor", lambda e, psb=psb, jj=jj, src_ap=src_ap: e.transpose(
                    out=psb[:, jj * 128:(jj + 1) * 128], in_=src_ap, identity=C.ident.ap), reads=[sd, C.ident], writes=[ps])
            if q < 4:
                P.op("scalar", lambda e, psb=psb, q=q: e.activation(out=x_tm[:, q * 1024:(q + 1) * 1024], in_=psb[:, 0:1024],
                                                                  func=AF.Copy), reads=[ps], writes=[x_tm])
            else:
                P.op("scalar", lambda e, psb=psb: e.activation(out=B_tm.ap, in_=psb[:, 0:1024], func=AF.Copy), reads=[ps],
                     writes=[B_tm])
        ps = R.bank()
        P.op("tensor", lambda e, ps=ps: e.matmul(ps[:, 0:64], lhsT=C.su.ap, rhs=aa.ap, start=True, stop=True),
             reads=[C.su, aa], writes=[ps])
        P.op("scalar", lambda e, ps=ps: e.activation(out=dte.ap, in_=ps[:, 0:64], func=AF.Exp), reads=[ps], writes=[dte])
        ps = R.bank()
        P.op("tensor", lambda e, ps=ps: e.matmul(ps[:, 0:64], lhsT=C.ones_f.ap, rhs=aa.ap, start=True, stop=True),
             reads=[C.ones_f, aa], writes=[ps])
        P.op("scalar", lambda e, ps=ps: e.activation(out=dec.ap, in_=ps[:, 0:64], func=AF.Exp), reads=[ps], writes=[dec])
        v3 = lambda t_: t_.ap.rearrange("p (h q) -> p h q", h=64)
        bc = lambda t_: t_.ap.unsqueeze(2).broadcast_to([128, 64, 64])
        P.op("vector", lambda e: e.tensor_tensor(out=v3(xdt), in0=v3(x_tm), in1=bc(dt), op=ALU.mult), reads=[x_tm, dt],
             writes=[xdt])
        P.op("gpsimd", lambda e: e.tensor_tensor(out=v3(xdtw), in0=v3(xdt), in1=bc(dte), op=ALU.mult), reads=[xdt, dte],
             writes=[xdtw])
        for g in range(8):
            ps = R.bank()
            P.op("tensor", lambda e, ps=ps, g=g: e.matmul(ps[:, 0:128], lhsT=BT[:, g, :], rhs=CT[:, g, :], start=True,
                                                        stop=True), reads=[BT.s(g), CT.s(g)], writes=[ps])
            P.op("vector", lambda e, ps=ps, g=g: e.tensor_tensor(out=cbm[:, g, :], in0=ps[:, 0:128], in1=C.trile.ap,
                                                               op=ALU.mult), reads=[ps, C.trile], writes=[cbm.s(g)])
        def ssd_s1(q):
            g = q // 2
            r4, lt, mt, ed, cd = R4[q % 2], LT[q % 2], MT[q % 2], Ed[q % 2], Cd[q % 2]
            for i in range(4):
                h = 4 * q + i
                P.op("vector", lambda e, i=i, h=h, r4=r4: e.tensor_scalar(out=r4[:, i, :], in0=C.trile.ap,
                                                                        scalar1=aa[:, h:h + 1], scalar2=None,
                                                                        op0=ALU.mult), reads=[C.trile, aa], writes=[r4])
            r4f = r4.ap.rearrange("p a b -> p (a b)")
            ps1 = R.bank()
            P.op("tensor", lambda e, ps1=ps1, r4f=r4f: e.matmul(ps1[:, 0:512], lhsT=C.su.ap, rhs=r4f, start=True, stop=True),
                 reads=[C.su, r4], writes=[ps1])
            ps2 = R.bank()
            P.op("tensor", lambda e, ps2=ps2, r4f=r4f: e.matmul(ps2[:, 0:512], lhsT=C.ones_f.ap, rhs=r4f, start=True,
                                                              stop=True), reads=[C.ones_f, r4], writes=[ps2])
            P.op("scalar", lambda e, ps1=ps1, lt=lt: e.activation(out=lt.ap.rearrange("p a b -> p (a b)"), in_=ps1[:, 0:512],
                                                                func=AF.Exp), reads=[ps1], writes=[lt])
            P.op("scalar", lambda e, ps2=ps2, ed=ed: e.activation(out=ed.ap.rearrange("p a b -> p (a b)"), in_=ps2[:, 0:512],
                                                                func=AF.Exp), reads=[ps2], writes=[ed])
            P.op("vector", lambda e, lt=lt, mt=mt, g=g: e.tensor_tensor(
                out=mt.ap, in0=lt.ap, in1=cbm[:, g, :].unsqueeze(1).broadcast_to([128, 4, 128]), op=ALU.mult),
                reads=[lt, cbm.s(g)], writes=[mt])
            P.op("gpsimd", lambda e, ed=ed, cd=cd, g=g: e.tensor_tensor(
                out=cd.ap, in0=ed.ap, in1=CT[:, g, :].unsqueeze(1).broadcast_to([128, 4, 128]), op=ALU.mult),
                reads=[ed, CT.s(g)], writes=[cd])

        def ssd_s2(q):
            g = q // 2
            mt, cd = MT[q % 2], Cd[q % 2]
            psy = R.bank()
            for i in range(4):
                h = 4 * q + i
                pc = h // 2
                P.op("tensor", lambda e, psy=psy, i=i, pc=pc, mt=mt: e.matmul(
                    psy[:, i * 128:(i + 1) * 128], lhsT=xdt[:, pc * 128:(pc + 1) * 128], rhs=mt[:, i, :], start=True,
                    stop=False), reads=[xdt, mt], writes=[psy])
                P.op("tensor", lambda e, psy=psy, i=i, pc=pc, cd=cd: e.matmul(
                    psy[:, i * 128:(i + 1) * 128], lhsT=prevT[:, pc * 128:(pc + 1) * 128], rhs=cd[:, i, :], start=False,
                    stop=True), reads=[prevT, cd], writes=[psy])
            for i in range(4):
                h = 4 * q + i
                pc, r0 = h // 2, (h % 2) * 64
                P.op("vector", lambda e, psy=psy, i=i, pc=pc, r0=r0: e.scalar_tensor_tensor(
                    out=yT[r0:r0 + 64, pc, :], in0=xsb[r0:r0 + 64, pc, :], scalar=C.cols[r0:r0 + 64, dco + pc:dco + pc + 1],
                    in1=psy[r0:r0 + 64, i * 128:(i + 1) * 128], op0=ALU.mult, op1=ALU.add),
                    reads=[psy, xsb.s(pc), C.cols], writes=[yT.s(pc)])

        ssd_s1(0)
        for q in range(16):
            if q + 1 < 16:
                ssd_s1(q + 1)
            ssd_s2(q)
        for g in range(8):
            ps = R.bank()
            P.op("tensor", lambda e, ps=ps, g=g: e.matmul(ps[:, 0:512], lhsT=B_tm[:, g * 128:(g + 1) * 128],
                                                        rhs=xdtw[:, g * 512:(g + 1) * 512], start=True, stop=True),
                 reads=[B_tm, xdtw], writes=[ps])
            sv_ = S.ap[:, g * 512:(g + 1) * 512].rearrange("p (h q) -> p h q", h=8)
            P.op("vector", lambda e, g=g, sv_=sv_: e.tensor_tensor(
                out=sv_, in0=sv_, in1=dec[:, 8 * g:8 * g + 8].unsqueeze(2).broadcast_to([128, 8, 64]), op=ALU.mult),
                reads=[S, dec, prevT], writes=[S])
            P.op("vector", lambda e, g=g, ps=ps: e.tensor_tensor(out=S[:, g * 512:(g + 1) * 512],
                                                               in0=S[:, g * 512:(g + 1) * 512], in1=ps[:, 0:512],
                                                               op=ALU.add), reads=[S, ps], writes=[S])
        stats = P.psum[0]
        for j in range(32):
            P.op("gpsimd", lambda e, j=j: e.tensor_tensor(out=yT[:, j, :], in0=yT[:, j, :], in1=szT[:, j, :], op=ALU.mult),
                 reads=[yT.s(j), szT.s(j)], writes=[yT.s(j)])
            P.op("scalar", lambda e, j=j: e.activation(out=sq[:, j % 2, :], in_=yT[:, j, :], func=AF.Square),
                 reads=[yT.s(j)], writes=[sq.s(j % 2)])
            P.op("tensor", lambda e, j=j: e.matmul(stats[:, 0:T], lhsT=C.ones_f.ap, rhs=sq[:, j % 2, :], start=(j == 0),
                                                   stop=(j == 31)), reads=[sq.s(j % 2), C.ones_f], writes=[stats])
        rstd_from_ps(P, C, rstd, stats, T, n=4096)
        for j in range(32):
            P.op("vector", lambda e, j=j: e.scalar_tensor_tensor(out=gT[:, j, :], in0=yT[:, j, :],
                                                                 scalar=C.cols[:, ngo + j:ngo + j + 1], in1=rstd[:, 0:T],
                                                                 op0=ALU.mult, op1=ALU.mult),
                 reads=[yT.s(j), rstd, C.cols], writes=[gT.s(j)])
        out_proj(P, C, gT, 32, wout, C.wbuf["ssd_out"], wsl, R, big, sq, rstd, T)
        post_residual(P, C, big, rstd, x_in, xin_buf, x_out, xout_buf, t0, T, O["post%d_0" % li], xc)


def nsa_stage(P, C, li, x_in, xin_buf, x_out, xout_buf, L):
    O = C.off
    nc = C.nc
    NC = (L - 32) // 16 + 1
    NIC = (NC + 127) // 128
    NSL = L // 64
    scale = 128.0 ** -0.5
    kvT_d = nc.dram_tensor("nsa_kvT", [4, 4, 128, L], BF16, kind="Internal").ap()
    vtm_d = nc.dram_tensor("nsa_vtm", [2, L, 512], BF16, kind="Internal").ap()
    kvT_b, vtm_b = Buf("kvT"), Buf("vtm")
    win = C.wb["nsa_in"][0].rearrange("(kc p) f -> p kc f", p=128)
    wout = C.wb["nsa_out"][0].rearrange("(kc p) f -> p kc f", p=128)
    wib = C.wbuf["nsa_in"]
    kcT = P.alloc("kcT", (4, NIC * 128), BF16)
    vc_tm = P.alloc("vc_tm", (NIC, 4, 128), BF16)
    P.op("gpsimd", lambda e: e.memset(kcT.ap, 0.0), writes=[kcT])
    P.op("gpsimd", lambda e: e.memset(vc_tm.ap, 0.0), writes=[vc_tm])
    mark = P.arena_off

    def phase_a():
        T = min(512, L)
        nt = L // T
        hT = P.alloc("hT", (16, T), BF16)
        big = P.alloc("big", (16, T), F32)
        wsl = [P.alloc("w%d" % i, (8192,), BF16) for i in range(3)]
        sq = P.alloc("sq", (2, T), F32)
        rstd = P.alloc("rstd", (T,), F32)
        stg = [P.alloc("stg%d" % i, (T,), BF16) for i in range(3)]
        R = Rot(P, wsl)
        rot = [0]
        FM = ((0, 2048), (1, 2560), (2, 3072), (3, 4096))
        for ti in range(nt):
            t0 = ti * T
            pre_norm(P, C, x_in, xin_buf, t0, T, O["pre%d_0" % li], big, hT, sq, rstd, P.psum[0])
            for fam, c0 in FM:
                def put(j, ps, fam=fam):
                    s = stg[rot[0] % 3]
                    rot[0] += 1
                    P.op("scalar", lambda e: e.activation(out=s[:, 0:T], in_=ps[:, 0:T], func=AF.Copy), reads=[ps], writes=[s])
                    P.dma("gpsimd", kvT_d[fam, j, :, t0:t0 + T], s[:, 0:T], reads=[s], writes=[kvT_b])
                gemm_fm(P, C, R, hT, 16, win, wib, c0, 4, T, put)
            for f, c0 in ((0, 3584), (1, 4608)):
                slot = R.slot()
                sv = slot.ap.rearrange("p (k f) -> p k f", k=16)
                for a, b in ((0, 8), (8, 16)):
                    P.dma("sync", sv[:, a:b, :], win[:, a:b, c0:c0 + 512], reads=[wib], writes=[slot])
                for tc in range(T // 128):
                    ps = R.bank()
                    for kc in range(16):
                        P.op("tensor", lambda e, kc=kc, ps=ps, tc=tc, sv=sv: e.matmul(
                            ps[:, 0:512], lhsT=hT[:, kc, tc * 128:(tc + 1) * 128], rhs=sv[:, kc, :], start=(kc == 0),
                            stop=(kc == 15)), reads=[slot, hT.s(kc)], writes=[ps])
                    s = stg[rot[0] % 3]
                    rot[0] += 1
                    P.op("scalar", lambda e, s=s, ps=ps: e.activation(out=s[:, 0:512], in_=ps[:, 0:512], func=AF.Copy), reads=[ps],
                         writes=[s])
                    P.dma("gpsimd", vtm_d[f, t0 + tc * 128:t0 + (tc + 1) * 128, :], s[:, 0:512], reads=[s], writes=[vtm_b])
        P.barrier()
        P.arena_off = mark
    phase_a()

    def phase_b():
        w1s = P.alloc("w1s", (32, 256), BF16)
        w2s = P.alloc("w2s", (2, 128), BF16)
        posf = P.alloc("posf", (2, 32), F32)
        posb = P.alloc("posb", (2, 32), BF16)
        pbcol = P.alloc("pbcol", (2,), F32)
        kin = [P.alloc("kin%d" % i, (L,), BF16) for i in range(2)]
        Hg = P.alloc("Hg", (2, NIC * 128), BF16)
        P.dma("sync", posf.ap, C.nsa_posT, writes=[posf])
        P.op("vector", lambda e: e.tensor_copy(out=posb.ap, in_=posf.ap), reads=[posf], writes=[posb])
        P.op("gpsimd", lambda e: e.memset(Hg.ap, 0.0), writes=[Hg])
        Rb = Rot(P, [], banks=(1, 2, 3, 4, 5, 6, 7))
        w1d = C.wb["nsa_w1"][0]
        w2d = C.wb["nsa_w2"][0]
        for fam in range(2):
            P.dma("sync", w1s.ap, w1d[fam].rearrange("(l p) h -> p l h", p=128), reads=[C.wbuf["nsa_w1"]], writes=[w1s])
            P.dma("sync", w2s.ap, w2d[fam].rearrange("(c p) d -> p c d", p=128), reads=[C.wbuf["nsa_w2"]], writes=[w2s])
            for hc in range(2):
                ps = Rb.bank()
                for l in range(32):
                    P.op("tensor", lambda e, ps=ps, l=l, hc=hc, fam=fam: e.matmul(
                        ps[:, 0:1], lhsT=w1s[:, l, hc * 128:(hc + 1) * 128], rhs=posb[:, fam, l:l + 1], start=(l == 0),
                        stop=(l == 31)), reads=[w1s, posb], writes=[ps])
                P.op("scalar", lambda e, ps=ps, hc=hc: e.activation(out=pbcol[:, hc:hc + 1], in_=ps[:, 0:1], func=AF.Copy),
                     reads=[ps], writes=[pbcol])
            for g in range(4):
                kk = kin[g % 2]
                P.dma("sync", kk.ap, kvT_d[fam, g, :, :], reads=[kvT_b], writes=[kk])
                for hc in range(2):
                    ps = Rb.bank()
                    for l in range(32):
                        P.op("tensor", lambda e, ps=ps, l=l, hc=hc, kk=kk: e.matmul(
                            ps[:, 0:NC], lhsT=w1s[:, l, hc * 128:(hc + 1) * 128], rhs=kk[:, l:l + 16 * (NC - 1) + 1:16],
                            start=(l == 0), stop=(l == 31)), reads=[w1s, kk], writes=[ps])
                    P.op("scalar", lambda e, ps=ps, hc=hc: e.activation(out=Hg[:, hc, 0:NC], in_=ps[:, 0:NC],
                                                                      func=AF.Gelu_apprx_tanh, bias=pbcol[:, hc:hc + 1]),
                         reads=[ps, pbcol], writes=[Hg])
                if fam == 0:
                    ps = Rb.bank()
                    for hc in range(2):
                        P.op("tensor", lambda e, ps=ps, hc=hc: e.matmul(ps[:, 0:NC], lhsT=w2s[:, hc, :], rhs=Hg[:, hc, 0:NC],
                                                                      start=(hc == 0), stop=(hc == 1)), reads=[w2s, Hg],
                             writes=[ps])
                    P.op("scalar", lambda e, ps=ps, g=g: e.activation(out=kcT[:, g, 0:NC], in_=ps[:, 0:NC], func=AF.Copy),
                         reads=[ps], writes=[kcT])
                else:
                    for ic in range(NIC):
                        ni = min(128, NC - ic * 128)
                        ps = Rb.bank()
                        for hc in range(2):
                            P.op("tensor", lambda e, ps=ps, hc=hc, ic=ic, ni=ni: e.matmul(
                                ps[0:ni, 0:128], lhsT=Hg[:, hc, ic * 128:ic * 128 + ni], rhs=w2s[:, hc, :], start=(hc == 0),
                                stop=(hc == 1)), reads=[w2s, Hg], writes=[ps])
                        P.op("scalar", lambda e, ps=ps, ic=ic, g=g, ni=ni: e.activation(out=vc_tm[0:ni, ic, g, :],
                                                                                     in_=ps[0:ni, 0:128], func=AF.Copy),
                             reads=[ps], writes=[vc_tm])
        P.barrier()
        P.arena_off = mark
    phase_b()

    def phase_c():
        T = min(256, L)
        nt = L // T
        NTC = T // 128
        hT = P.alloc("hT", (16, T), BF16)
        big = P.alloc("big", (16, T), F32)
        wsl = [P.alloc("w%d" % i, (8192,), BF16) for i in range(2)]
        sq = P.alloc("sq", (2, T), F32)
        rstd = P.alloc("rstd", (T,), F32)
        xc = [P.alloc("xc%d" % i, (T,), F32) for i in range(2)]
        qT = P.alloc("qT", (16, T), BF16)
        oT = P.alloc("oT", (16, T), BF16)
        wg = P.alloc("wg", (16, 48), BF16)
        sgT = P.alloc("sgT", (T,), F32)
        Sel = P.alloc("Sel", (48, 128), F32)
        Em = P.alloc("Em", (L,), BF16)
        cover = P.alloc("cover", (NIC, 64), F32)
        onesT = P.alloc("onesT", (T,), BF16)
        NWC = 512 // 128 + NTC
        wmask = P.alloc("wmask", (NWC, T), BF16)
        cmask = P.alloc("cmask", (NIC, T), BF16)
        kw_s = P.alloc("kw_s", (NWC * 128,), BF16)
        vw_s = P.alloc("vw_s", (NWC, 128), BF16)
        ks_s = P.alloc("ks_s", (L,), BF16)
        vs_s = P.alloc("vs_s", (L // 128, 128), BF16)
        pTt = [P.alloc("pT%d" % i, (T,), BF16) for i in range(6)]
        pmt = [P.alloc("pm%d" % i, (T,), BF16) for i in range(6)]
        mskt = [P.alloc("msk%d" % i, (T,), BF16) for i in range(2)]
        pcs = P.alloc("pcs", (NIC, T), F32)
        pcn = P.alloc("pcn", (T,), F32)
        rden = P.alloc("rden", (T,), F32)
        t1 = P.alloc("t1", (T,), F32)
        t2 = P.alloc("t2", (T,), F32)
        oacc = P.alloc("oacc", (T,), F32)
        imp = P.alloc("imp", (64,), F32)
        s1 = P.alloc("s1", (64,), F32)
        s2 = P.alloc("s2", (64,), F32)
        s3 = P.alloc("s3", (64,), F32)
        m8 = P.alloc("m8", (16,), F32)
        selb = P.alloc("selb", (64,), BF16)
        selT = P.alloc("selT", (4, T), BF16)
        P.dma("sync", wg.ap, win[:, :, 5120:5168], reads=[wib], writes=[wg])
        P.op("gpsimd", lambda e: e.memset(onesT.ap, 1.0), writes=[onesT])
        P.op("gpsimd", lambda e: e.memset(Sel.ap, 1.0), writes=[Sel])
        P.op("gpsimd", lambda e: e.affine_select(out=Sel.ap[0:48], in_=Sel.ap[0:48], pattern=[[1, 48], [0, 128]],
                                                 compare_op=ALU.is_equal, fill=fillreg(e, 0.0), base=0, channel_multiplier=-1),
             reads=[Sel], writes=[Sel])
        P.op("gpsimd", lambda e: e.memset(Em.ap, 1.0), writes=[Em])
        P.op("gpsimd", lambda e: e.affine_select(out=Em.ap[0:NSL], in_=Em.ap[0:NSL], pattern=[[1, L]], compare_op=ALU.is_ge,
                                                 fill=fillreg(e, 0.0), base=0, channel_multiplier=-64), reads=[Em], writes=[Em])
        P.op("gpsimd", lambda e: e.affine_select(out=Em.ap[0:NSL], in_=Em.ap[0:NSL], pattern=[[-1, L]], compare_op=ALU.is_ge,
                                                 fill=fillreg(e, 0.0), base=63, channel_multiplier=64), reads=[Em], writes=[Em])
        P.op("gpsimd", lambda e: e.memset(cover.ap, 1.0), writes=[cover])
        for ic in range(NIC):
            P.op("gpsimd", lambda e, ic=ic: e.affine_select(out=cover[:, ic, :], in_=cover[:, ic, :], pattern=[[-64, 64]],
                                                            compare_op=ALU.is_ge, fill=fillreg(e, 0.0), base=16 * 128 * ic + 31,
                                                            channel_multiplier=16), reads=[cover], writes=[cover])
            P.op("gpsimd", lambda e, ic=ic: e.affine_select(out=cover[:, ic, :], in_=cover[:, ic, :], pattern=[[64, 64]],
                                                            compare_op=ALU.is_ge, fill=fillreg(e, 0.0), base=63 - 16 * 128 * ic,
                                                            channel_multiplier=-16), reads=[cover], writes=[cover])
        for o in range(NWC):
            P.op("gpsimd", lambda e, o=o: e.affine_select(out=wmask[:, o, :], in_=onesT.ap, pattern=[[1, T]],
                                                          compare_op=ALU.is_ge, fill=fillreg(e, 0.0), base=512 - 128 * o,
                                                          channel_multiplier=-1), reads=[onesT], writes=[wmask])
            P.op("gpsimd", lambda e, o=o: e.affine_select(out=wmask[:, o, :], in_=wmask[:, o, :], pattern=[[-1, T]],
                                                          compare_op=ALU.is_gt, fill=fillreg(e, 0.0), base=128 * o,
                                                          channel_multiplier=1), reads=[wmask], writes=[wmask])
        R = Rot(P, wsl, banks=(5, 6, 7))
        tiny = 1e-30
        if T <= 256:
            sslots = []
            for b_ in (5, 6, 7):
                for hh_ in range(2):
                    sslots.append(Tile(P.psum[b_].ap[:, hh_ * 256:(hh_ + 1) * 256], P.psum[b_].buf, hh_))
        else:
            sslots = [P.psum[b_] for b_ in (5, 6, 7)]
        sidx = [0]

        def sbank():
            s_ = sslots[sidx[0] % len(sslots)]
            sidx[0] += 1
            return s_

        def finish_branch(hq, br, den, o, first, last):
            P.op("vector", lambda e: e.tensor_scalar(out=rden.ap, in0=den[:, 0:T], scalar1=tiny, scalar2=None, op0=ALU.add),
                 reads=[den], writes=[rden])
            P.op("vector", lambda e: e.reciprocal(out=rden.ap, in_=rden.ap), reads=[rden], writes=[rden])
            gb = R.bank()
            col = hq * 3 + br
            P.op("tensor", lambda e: e.matmul(gb[:, 0:T], lhsT=Sel[0:48, col, :], rhs=sgT[0:48, :], start=True, stop=True),
                 reads=[Sel, sgT], writes=[gb])
            P.op("vector", lambda e: e.tensor_tensor(out=t1.ap, in0=rden.ap, in1=gb[:, 0:T], op=ALU.mult), reads=[rden, gb],
                 writes=[t1])
            if first:
                P.op("vector", lambda e: e.tensor_tensor(out=oacc.ap, in0=t1.ap, in1=o[:, 0:T], op=ALU.mult), reads=[t1, o],
                     writes=[oacc])
            else:
                P.op("vector", lambda e: e.tensor_tensor(out=t2.ap, in0=t1.ap, in1=o[:, 0:T], op=ALU.mult), reads=[t1, o],
                     writes=[t2])
                if last:
                    P.op("vector", lambda e: e.tensor_tensor(out=oT[:, hq, :], in0=oacc.ap, in1=t2.ap, op=ALU.add),
                         reads=[oacc, t2], writes=[oT.s(hq)])
                else:
                    P.op("vector", lambda e: e.tensor_tensor(out=oacc.ap, in0=oacc.ap, in1=t2.ap, op=ALU.add),
                         reads=[oacc, t2], writes=[oacc])

        cnt = [0]

        pend = []
        LAG = 2

        def drain(n):
            while len(pend) > n:
                pend.pop(0)()

        def defer(fn):
            pend.append(fn)

        def attend(hq, keyT_ap, keyT_dep, val_ap_fn, val_dep, mask_ap, mask_dep, den, o, first, last):
            r = cnt[0] % 4
            cnt[0] += 1
            ps = R.bank()
            pt, pm = pTt[r], pmt[r]
            P.op("tensor", lambda e: e.matmul(ps[:, 0:T], lhsT=keyT_ap, rhs=qT[:, hq, :], start=True, stop=True),
                 reads=[keyT_dep, qT.s(hq)], writes=[ps])
            P.op("scalar", lambda e: e.activation(out=pt.ap, in_=ps[:, 0:T], func=AF.Exp, scale=scale), reads=[ps], writes=[pt])
            P.op("vector", lambda e: e.tensor_tensor(out=pm.ap, in0=pt.ap, in1=mask_ap, op=ALU.mult), reads=[pt, mask_dep],
                 writes=[pm])

            def part2():
                P.op("tensor", lambda e: e.matmul(den[:, 0:T], lhsT=C.ones_b.ap, rhs=pm.ap, start=first, stop=last),
                     reads=[pm, C.ones_b], writes=[den])
                P.op("tensor", lambda e: e.matmul(o[:, 0:T], lhsT=val_ap_fn, rhs=pm.ap, start=first, stop=last),
                     reads=[pm, val_dep], writes=[o])

            defer(part2)
            drain(LAG)
            return pm

        for ti in range(nt):
            t0 = ti * T
            pre_norm(P, C, x_in, xin_buf, t0, T, O["pre%d_0" % li], big, hT, sq, rstd, P.psum[0])

            def put_q(j, ps):
                P.op("scalar", lambda e: e.activation(out=qT[:, j, :], in_=ps[:, 0:T], func=AF.Copy), reads=[ps],
                     writes=[qT.s(j)])

            gemm_fm(P, C, R, hT, 16, win, wib, 0, 16, T, put_q)
            ps = R.bank()
            for kc in range(16):
                P.op("tensor", lambda e, kc=kc, ps=ps: e.matmul(ps[0:48, 0:T], lhsT=wg[:, kc, :], rhs=hT[:, kc, :],
                                                              start=(kc == 0), stop=(kc == 15)), reads=[wg, hT.s(kc)],
                     writes=[ps])
            P.op("scalar", lambda e, ps=ps: e.activation(out=sgT[0:48, :], in_=ps[0:48, 0:T], func=AF.Sigmoid), reads=[ps],
                 writes=[sgT])
            nic_t = 0
            for ic in range(NIC):
                if 16 * 128 * ic + 31 <= t0 + T - 1:
                    nic_t = ic + 1
                    P.op("gpsimd", lambda e, ic=ic, t0=t0: e.affine_select(out=cmask[:, ic, :], in_=onesT.ap, pattern=[[1, T]],
                                                                    compare_op=ALU.is_ge, fill=fillreg(e, 0.0),
                                                                    base=t0 - 31 - 16 * 128 * ic, channel_multiplier=-16),
                         reads=[onesT], writes=[cmask.s(ic)])
            nkc = (t0 + T) // 128
            w_lo = max(0, (t0 - 512) // 128)
            for g in range(4):
                nw = nkc - w_lo
                P.dma("sync", kw_s[:, 0:nw * 128], kvT_d[3, g, :, w_lo * 128:nkc * 128], reads=[kvT_b], writes=[kw_s])
                P.dma("sync", vw_s[:, 0:nw, :], vtm_d[1, w_lo * 128:nkc * 128, g * 128:(g + 1) * 128].rearrange(
                    "(c p) d -> p c d", p=128), reads=[vtm_b], writes=[vw_s])
                P.dma("sync", ks_s[:, 0:nkc * 128], kvT_d[2, g, :, 0:nkc * 128], reads=[kvT_b], writes=[ks_s])
                P.dma("sync", vs_s[:, 0:nkc, :], vtm_d[0, 0:nkc * 128, g * 128:(g + 1) * 128].rearrange(
                    "(c p) d -> p c d", p=128), reads=[vtm_b], writes=[vs_s])
                for j in range(4):
                    hq = 4 * g + j
                    den, o = P.psum[1 + hq % 2], P.psum[3 + hq % 2]
                    if nic_t == 0:
                        P.op("gpsimd", lambda e, hq=hq: e.memset(big[:, hq, :], 0.0), writes=[big.s(hq)])
                        continue
                    pms = []
                    for ic in range(nic_t):
                        pm = attend(hq, kcT[:, g, ic * 128:(ic + 1) * 128], kcT, vc_tm[:, ic, g, :], vc_tm, cmask[:, ic, :],
                                    cmask.s(ic), den, o, ic == 0, ic == nic_t - 1)
                        pms.append(pm)
                    drain(0)
                    finish_branch(hq, 0, den, o, True, False)
                    if ti == 0 and hq == 0:
                        dbg_dump(P, C, "cmask", cmask, (NIC, T), BF16)
                        dbg_dump(P, C, "pm0", pms[0], (T,), BF16)
                        dbg_dump(P, C, "rden0", rden, (T,), F32)
                        dbg_dump(P, C, "t10", t1, (T,), F32)
                        dbg_dump(P, C, "oacc0", oacc, (T,), F32)
                    for ic in range(nic_t):
                        if j == 0:
                            P.op("vector", lambda e, ic=ic, pm=pms[ic]: e.tensor_tensor(out=pcs[:, ic, :], in0=pm.ap, in1=rden.ap,
                                                                                     op=ALU.mult), reads=[pm, rden],
                                 writes=[pcs.s(ic)])
                        else:
                            P.op("vector", lambda e, ic=ic, pm=pms[ic]: e.tensor_tensor(out=pcn.ap, in0=pm.ap, in1=rden.ap,
                                                                                     op=ALU.mult), reads=[pm, rden], writes=[pcn])
                            P.op("vector", lambda e, ic=ic: e.tensor_tensor(out=pcs[:, ic, :], in0=pcs[:, ic, :], in1=pcn.ap,
                                                                            op=ALU.add), reads=[pcs.s(ic), pcn],
                                 writes=[pcs.s(ic)])
                    P.op("gpsimd", lambda e, hq=hq: e.tensor_copy(out=big[:, hq, :], in_=oacc.ap), reads=[oacc],
                         writes=[big.s(hq)])
                for tc in range(NTC):
                    ps = R.bank()
                    if nic_t == 0:
                        P.op("gpsimd", lambda e: e.memset(imp.ap, 0.0), writes=[imp])
                    else:
                        for ic in range(nic_t):
                            P.op("tensor", lambda e, ps=ps, ic=ic, tc=tc: e.matmul(
                                ps[:, 0:64], lhsT=pcs[:, ic, tc * 128:(tc + 1) * 128], rhs=cover[:, ic, :], start=(ic == 0),
                                stop=(ic == nic_t - 1)), reads=[pcs.s(ic), cover], writes=[ps])
                        P.op("scalar", lambda e, ps=ps: e.activation(out=imp.ap, in_=ps[:, 0:64], func=AF.Copy), reads=[ps],
                             writes=[imp])
                    tb = t0 + tc * 128
                    P.op("gpsimd", lambda e, tb=tb: e.affine_select(out=s1.ap, in_=imp.ap, pattern=[[-64, 64]],
                                                                    compare_op=ALU.is_ge, fill=fillreg(e, 100.0), base=tb - 192,
                                                                    channel_multiplier=1), reads=[imp], writes=[s1])
                    P.op("gpsimd", lambda e, tb=tb: e.affine_select(out=s2.ap, in_=s1.ap, pattern=[[-64, 64]],
                                                                    compare_op=ALU.is_ge, fill=fillreg(e, -1.0), base=tb,
                                                                    channel_multiplier=1), reads=[s1], writes=[s2])
                    P.op("gpsimd", lambda e: e.memset(s2[:, 0:1], 100.0), reads=[s2], writes=[s2])
                    P.op("vector", lambda e: e.max(out=m8[:, 0:8], in_=s2[:, 0:NSL]), reads=[s2], writes=[m8])
                    P.op("vector", lambda e: e.match_replace(out=s3[:, 0:NSL], in_to_replace=m8[:, 0:8], in_values=s2[:, 0:NSL],
                                                             imm_value=-2.0), reads=[s2, m8], writes=[s3])
                    P.op("vector", lambda e: e.max(out=m8[:, 8:16], in_=s3[:, 0:NSL]), reads=[s3, m8], writes=[m8])
                    P.op("vector", lambda e: e.tensor_scalar(out=selb.ap, in0=s2.ap, scalar1=m8[:, 15:16], scalar2=None,
                                                             op0=ALU.is_ge), reads=[s2, m8], writes=[selb])
                    ps2 = R.bank()
                    psb = ps2.ap.bitcast(BF16)
                    P.op("tensor", lambda e, psb=psb: e.transpose(out=psb[0:64, 0:128], in_=selb.ap, identity=C.ident.ap),
                         reads=[selb, C.ident], writes=[ps2])
                    P.op("scalar", lambda e, psb=psb, tc=tc, g=g: e.activation(out=selT[0:64, g, tc * 128:(tc + 1) * 128],
                                                                              in_=psb[0:64, 0:128], func=AF.Copy), reads=[ps2],
                         writes=[selT])
                def sel_mask(kc_, g=g, t0=t0):
                    ps = R.bank()
                    m = mskt[kc_ % 2]
                    P.op("tensor", lambda e: e.matmul(ps[:, 0:T], lhsT=Em[0:NSL, kc_ * 128:(kc_ + 1) * 128], rhs=selT[0:NSL, g, :],
                                                      start=True, stop=True), reads=[Em, selT], writes=[ps])
                    P.op("scalar", lambda e: e.activation(out=m.ap, in_=ps[:, 0:T], func=AF.Copy), reads=[ps], writes=[m])
                    if kc_ * 128 + 127 > t0:
                        P.op("gpsimd", lambda e: e.affine_select(out=m.ap, in_=m.ap, pattern=[[1, T]], compare_op=ALU.is_ge,
                                                                 fill=fillreg(e, 0.0), base=t0 - kc_ * 128, channel_multiplier=-1),
                             reads=[m], writes=[m])
                    return m

                dens = [P.psum[1], P.psum[2]]
                os_ = [P.psum[3], P.psum[4]]
                for jp in range(2):
                    for kc_ in range(nkc):
                        m = sel_mask(kc_)
                        for jj in range(2):
                            hq = 4 * g + 2 * jp + jj
                            attend(hq, ks_s[:, kc_ * 128:(kc_ + 1) * 128], ks_s, vs_s[:, kc_, :], vs_s, m.ap, m, dens[jj],
                                   os_[jj], kc_ == 0, kc_ == nkc - 1)
                    for jj in range(2):
                        hq = 4 * g + 2 * jp + jj

                        def fin_sel(hq=hq, jj=jj):
                            P.op("gpsimd", lambda e: e.tensor_copy(out=oacc.ap, in_=big[:, hq, :]), reads=[big.s(hq)],
                                 writes=[oacc])
                            finish_branch(hq, 1, dens[jj], os_[jj], False, False)

                        defer(fin_sel)
                        den, o = dens[jj], os_[jj]
                        for wi in range(nw):
                            kc_ = w_lo + wi
                            oidx = kc_ - (t0 - 512) // 128
                            attend(hq, kw_s[:, wi * 128:(wi + 1) * 128], kw_s, vw_s[:, wi, :], vw_s, wmask[:, oidx, :], wmask, den, o,
                                   wi == 0, wi == nw - 1)
                        defer(lambda hq=hq, den=den, o=o: finish_branch(hq, 2, den, o, False, True))
                drain(0)
            if ti == 0:
                dbg_dump(P, C, "oT", oT, (16, T), BF16)
                dbg_dump(P, C, "cmp", big, (16, T), F32)
                dbg_dump(P, C, "qT", qT, (16, T), BF16)
                dbg_dump(P, C, "sgT", sgT, (T,), F32)
                dbg_dump(P, C, "kcT", kcT, (4, NIC * 128), BF16)
                dbg_dump(P, C, "vc", vc_tm, (NIC, 4, 128), BF16)
                dbg_dump(P, C, "selT", selT, (4, T), BF16)
            out_proj(P, C, oT, 16, wout, C.wbuf["nsa_out"], wsl, R, big, sq, rstd, T)
            post_residual(P, C, big, rstd, x_in, xin_buf, x_out, xout_buf, t0, T, O["post%d_0" % li], xc)
    phase_c()


STAGES = {"nsa": nsa_stage, "ssd": ssd_stage, "ffn": ffn_stage, "xa": xa_stage, "pool": pool_stage, "sgu": sgu_stage}


def build(L, plan, wshapes=None, dbg=False):
    nc = bass.Bass("TRN2", target_bir_lowering=False)
    _FILL.clear()
    C = Ctx()
    offs, ncols, _ = pack_cols(None)
    C.off = offs
    xT = nc.dram_tensor("xT", [D, L], F32, kind="ExternalInput")
    cols_d = nc.dram_tensor("cols", [128, ncols], F32, kind="ExternalInput")
    yT = nc.dram_tensor("yT", [D, L], F32, kind="ExternalOutput")
    xs = nc.dram_tensor("xs", [D, L], F32, kind="Internal")
    xs1 = nc.dram_tensor("xs1", [D, L], F32, kind="Internal")
    roffs, nrows, _ = pack_rows(None)
    C.roff = roffs
    rows_d = nc.dram_tensor("rows", [nrows], F32, kind="ExternalInput")
    C.rows = rows_d.ap()
    C.sgu_wsT = nc.dram_tensor("sgu_wsT", [128, 8, 128], F32, kind="ExternalInput").ap()
    C.nsa_posT = nc.dram_tensor("nsa_posT", [128, 2, 32], F32, kind="ExternalInput").ap()
    C.nc = nc
    C.dbg = dbg
    memT = nc.dram_tensor("memT", [D, 256], F32, kind="ExternalInput")
    C.memT, C.memT_buf = memT.ap(), Buf("memT")
    needed = set()
    for stg in plan:
        needed |= set(STAGE_W[stg[0]])
    LAST_NEEDED[:] = sorted(needed)
    C.wb, C.wbuf, C.wcast = {}, {}, {}
    for name in sorted(needed):
        key, shp = WEIGHTS[name]
        src = nc.dram_tensor(key, list(shp), F32, kind="ExternalInput")
        dst = nc.dram_tensor(name + "_bf", list(shp), BF16, kind="Internal")
        C.wb[name] = dst.ap()
        C.wbuf[name] = Buf(name)
        C.wcast[name] = (src, dst, C.wbuf[name])
    P = Prog(nc)
    with ExitStack() as st:
        P.arena_words = 47000
        P.arena = st.enter_context(nc.sbuf_tensor("arena", [128, P.arena_words], F32))
        P.psum = []
        for i in range(8):
            t = st.enter_context(nc.psum_tensor("ps%d" % i, [128, 512], F32))
            P.psum.append(Tile(t[:, :], Buf("ps%d" % i)))
        setup_persistent(P, C, cols_d.ap(), ncols)
        cast_weights(P, C)
        P.barrier()
        P.arena_off = P.arena_base
        aps = {"xT": xT.ap(), "xs": xs.ap(), "xs1": xs1.ap(), "yT": yT.ap()}
        for stg in plan:
            kind, li, src, dst = stg
            STAGES[kind](P, C, li, aps[src], Buf(src), aps[dst], Buf(dst), L)
            P.barrier()
            P.arena_off = P.arena_base
        P.emit(st)
    return nc


FULL_PLAN = []
_kinds = ["ssd", "nsa", "sgu", "pool"]
_cur = "xT"
for _i in range(DEPTH):
    for _k in (_kinds[_i], "xa", "ffn"):
        _last = (_i == DEPTH - 1 and _k == "ffn")
        _dst = "yT" if _last else ("xs" if _cur != "xs" else "xs1")
        FULL_PLAN.append((_k, _i, _cur, _dst))
        _cur = _dst

_NC_CACHE = {}


def kernel(**inputs):
    inp = {k: np.asarray(v) for k, v in inputs.items()}
    B, L, _ = inp["x"].shape
    if L not in _NC_CACHE:
        _NC_CACHE[L] = (build(L, FULL_PLAN), list(LAST_NEEDED))
    nc, needed = _NC_CACHE[L]
    in_maps = [make_in_map(inp, b, L, needed) for b in range(B)]
    res = run_bass_kernel_spmd(nc, in_maps, core_ids=list(range(B)))
    out = np.stack([np.asarray(res.results[b]["yT"]).T for b in range(B)], axis=0)
    return np.ascontiguousarray(out.astype(np.float32))
```
